# Optimizing a Trainium2 kernel written in Bass

```python
import jax, jax.numpy as jnp
from jax import lax
import numpy as np

D_MODEL = 2048
BATCH = 4
SEQ = 8192
DEPTH = 1

D_CONV = D_MODEL
CONV_WIDTH = 31
N_CONV_GROUPS = 16
RNN_BLOCKS = 16
RNN_BLOCK_DIM = 160
D_RNN = RNN_BLOCKS * RNN_BLOCK_DIM
RNN_CONV_WIDTH = 4
RG_C = 8.0
NORM_EPS = 1e-6
LN_EPS = 1e-5
IN_SIZES = (D_CONV, D_CONV, D_CONV, D_RNN, D_RNN, D_MODEL, D_MODEL)
D_IN = sum(IN_SIZES)

kernel_name = "hybrid_conformer_rglru_gated_block"


def _rmsnorm(x, g):
    xf = x.astype(jnp.float32)
    y = xf * lax.rsqrt(jnp.mean(xf * xf, axis=-1, keepdims=True) + NORM_EPS)
    return (y * g.astype(jnp.float32)).astype(x.dtype)


def _layernorm(x, g, b):
    xf = x.astype(jnp.float32)
    mu = jnp.mean(xf, axis=-1, keepdims=True)
    xc = xf - mu
    var = jnp.mean(xc * xc, axis=-1, keepdims=True)
    y = xc * lax.rsqrt(var + LN_EPS) * g.astype(jnp.float32) + b.astype(jnp.float32)
    return y.astype(x.dtype)


def _causal_depthwise_conv(x, w, b):
    width, ch = w.shape
    y = lax.conv_general_dilated(
        x, w[:, None, :].astype(x.dtype), window_strides=(1,), padding=[(width - 1, 0)],
        dimension_numbers=("NWC", "WIO", "NWC"), feature_group_count=ch)
    return y + b.astype(x.dtype)


def _block_diag(x, w, b):
    bsz, seq, _ = x.shape
    nh, dh, _ = w.shape
    y = jnp.einsum("bshi,hij->bshj", x.reshape(bsz, seq, nh, dh), w).reshape(bsz, seq, nh * dh)
    return y + b


def _rg_lru(x, w_a, b_a, w_x, b_x, lam):
    r = jax.nn.sigmoid(_block_diag(x, w_a, b_a).astype(jnp.float32))
    i = jax.nn.sigmoid(_block_diag(x, w_x, b_x).astype(jnp.float32))
    log_a = -RG_C * r * jax.nn.softplus(-lam.astype(jnp.float32))
    a = jnp.exp(log_a)
    mult = jnp.sqrt(jnp.maximum(-jnp.expm1(2.0 * log_a), 0.0))
    u = mult * (i * x.astype(jnp.float32))

    def combine(c1, c2):
        a1, b1 = c1
        a2, b2 = c2
        return a1 * a2, a2 * b1 + b2

    _, h = lax.associative_scan(combine, (a, u), axis=1)
    return h.astype(x.dtype)


def setup_inputs(seed: int = 0) -> dict:
    key = jax.random.key(seed)
    ks = jax.random.split(key, 20)
    f32 = jnp.float32
    L = DEPTH
    nrm = lambda k, shape, s: (jax.random.normal(k, shape, f32) * s)
    x = jax.random.normal(ks[0], (BATCH, SEQ, D_MODEL), f32)
    norm_g = 1.0 + nrm(ks[1], (L, D_MODEL), 0.02)
    w_in = nrm(ks[2], (L, D_MODEL, D_IN), D_MODEL ** -0.5)
    conv_dw_w = nrm(ks[3], (L, CONV_WIDTH, D_CONV), CONV_WIDTH ** -0.5)
    conv_dw_b = nrm(ks[4], (L, D_CONV), 0.02)
    conv_ln_g = 1.0 + nrm(ks[5], (L, D_CONV), 0.02)
    conv_ln_b = nrm(ks[6], (L, D_CONV), 0.02)
    w_conv_out = nrm(ks[7], (L, D_CONV, D_MODEL), D_CONV ** -0.5)
    rnn_conv_w = nrm(ks[8], (L, RNN_CONV_WIDTH, D_RNN), RNN_CONV_WIDTH ** -0.5)
    rnn_conv_b = nrm(ks[9], (L, D_RNN), 0.02)
    w_rg_a = nrm(ks[10], (L, RNN_BLOCKS, RNN_BLOCK_DIM, RNN_BLOCK_DIM), RNN_BLOCK_DIM ** -0.5)
    b_rg_a = nrm(ks[11], (L, D_RNN), 0.02)
    w_rg_x = nrm(ks[12], (L, RNN_BLOCKS, RNN_BLOCK_DIM, RNN_BLOCK_DIM), RNN_BLOCK_DIM ** -0.5)
    b_rg_x = nrm(ks[13], (L, D_RNN), 0.02)
    a_c = jax.random.uniform(ks[14], (L, D_RNN), f32, 0.9, 0.999)
    s = a_c ** (1.0 / RG_C)
    rg_lambda = jnp.log(s) - jnp.log1p(-s)
    w_rnn_out = nrm(ks[15], (L, D_RNN, D_MODEL), D_RNN ** -0.5)
    w_out = nrm(ks[16], (L, D_MODEL, D_MODEL), D_MODEL ** -0.5)
    final_norm_g = 1.0 + nrm(ks[17], (D_MODEL,), 0.02)
    return {"x": x, "norm_g": norm_g, "w_in": w_in,
            "conv_dw_w": conv_dw_w, "conv_dw_b": conv_dw_b, "conv_ln_g": conv_ln_g, "conv_ln_b": conv_ln_b,
            "w_conv_out": w_conv_out, "rnn_conv_w": rnn_conv_w, "rnn_conv_b": rnn_conv_b,
            "w_rg_a": w_rg_a, "b_rg_a": b_rg_a, "w_rg_x": w_rg_x, "b_rg_x": b_rg_x,
            "rg_lambda": rg_lambda, "w_rnn_out": w_rnn_out, "w_out": w_out, "final_norm_g": final_norm_g}


def reference(x, norm_g, w_in, conv_dw_w, conv_dw_b, conv_ln_g, conv_ln_b, w_conv_out,
              rnn_conv_w, rnn_conv_b, w_rg_a, b_rg_a, w_rg_x, b_rg_x, rg_lambda,
              w_rnn_out, w_out, final_norm_g):
    split_idx = [int(v) for v in np.cumsum(IN_SIZES)[:-1]]
    for l in range(DEPTH):
        h = _rmsnorm(x, norm_g[l])
        z = jnp.einsum("bsd,de->bse", h, w_in[l])
        c_val, c_glu, c_gate, r_x, r_gate, g_conv, g_rnn = jnp.split(z, split_idx, axis=-1)
        u = c_val * jax.nn.sigmoid(c_glu)
        u = _causal_depthwise_conv(u, conv_dw_w[l], conv_dw_b[l])
        u = jax.nn.silu(_layernorm(u, conv_ln_g[l], conv_ln_b[l]))
        y_conv = jnp.einsum("bsc,cd->bsd", u * jax.nn.silu(c_gate), w_conv_out[l])
        v = _causal_depthwise_conv(r_x, rnn_conv_w[l], rnn_conv_b[l])
        v = _rg_lru(v, w_rg_a[l], b_rg_a[l], w_rg_x[l], b_rg_x[l], rg_lambda[l])
        y_rnn = jnp.einsum("bsc,cd->bsd", v * jax.nn.silu(r_gate), w_rnn_out[l])
        merged = jax.nn.sigmoid(g_conv) * y_conv + jax.nn.sigmoid(g_rnn) * y_rnn
        x = x + jnp.einsum("bsd,de->bse", merged, w_out[l])
    return _rmsnorm(x, final_norm_g)
```

```python
from contextlib import ExitStack
import numpy as np
import concourse.bass as bass
import concourse.mybir as mybir
from concourse.bass_utils import run_bass_kernel_spmd

F32 = mybir.dt.float32
BF16 = mybir.dt.bfloat16
AF = mybir.ActivationFunctionType
ALU = mybir.AluOpType

D = 2048
DC = 2048
DR = 2560
DIN = 15360
NCC = 16
NRC = 20
KC = 16
CW = 31
RW = 4
T = 256
FILL_N = 2
TOK = 4096
NT = TOK // T
EPS = 1e-6
LN_EPS = 1e-5

O_CVAL, O_CGLU, O_CGATE, O_RX, O_RGATE, O_GCONV, O_GRNN = 0, 2048, 4096, 6144, 8704, 11264, 13312

P_G, P_CW, P_CB, P_LG, P_LB, P_RW, P_RB, P_BA, P_BX, P_LAM = 0, 16, 512, 528, 544, 560, 640, 660, 680, 700
NPAR = 720
Q_HC, Q_CC, Q_HBA, Q_HBX, Q_HLG, Q_HLB = 0, 20, 40, 60, 80, 96
NDER = 112

GATE_ICS = {0: (0, 1), 1: (0, 1, 2), 2: (1, 2, 3), 3: (2, 3, 4), 4: (3, 4)}
GATE_BLK = {}
_b = 0
for _gate in range(2):
    for _oc in range(5):
        for _ic in GATE_ICS[_oc]:
            GATE_BLK[(_gate, _oc, _ic)] = _b
            _b += 1
NBLK = _b

ENGS = ("pe", "act", "dve", "pool", "sp")


class Op:
    __slots__ = ("eng", "fn", "reads", "writes", "xreads", "dma_key", "deps", "milestone", "cnt", "idx", "sem", "cost", "table", "alldeps", "pos")


class Prog:
    def __init__(self, nc):
        self.nc = nc
        self.ops = []
        self.last_writer = {}
        self.readers = {}
        self.final_dma = []
        self.cutoff = None

    def op(self, eng, fn, reads=(), writes=(), xreads=(), dma_key=None, cost=0.3, table=None):
        o = Op()
        o.cost, o.table = cost, table
        o.eng, o.fn, o.dma_key = eng, fn, dma_key
        o.reads, o.writes, o.xreads = tuple(reads), tuple(writes), tuple(xreads)
        o.milestone, o.cnt, o.sem = False, None, None
        o.idx = len(self.ops)
        deps = set()
        for r in o.reads:
            lw = self.last_writer.get(r)
            if lw is not None:
                deps.add(lw)
        for w in o.writes + o.xreads:
            lw = self.last_writer.get(w)
            if lw is not None:
                deps.add(lw)
            rd = self.readers.get(w)
            if rd:
                deps.update(rd)
        for r in o.reads:
            self.readers.setdefault(r, []).append(o.idx)
        for w in o.writes + o.xreads:
            self.last_writer[w] = o.idx
            self.readers[w] = []
        deps.discard(o.idx)
        o.deps = deps
        self.ops.append(o)
        return o

    def schedule(self, window=4000, cutoff=None):
        ops = self.ops
        n = len(ops)
        ndeps = [len(o.alldeps) for o in ops]
        users = [[] for _ in range(n)]
        for o in ops:
            for d in o.alldeps:
                users[d].append(o.idx)
        ready = [0.0] * n
        finish = [0.0] * n
        cand = {e: [] for e in ENGS}
        for o in ops:
            if ndeps[o.idx] == 0:
                cand[o.eng].append(o.idx)
        t_eng = {e: 0.0 for e in ENGS}
        self.t0 = [0.0] * n
        order = {e: [] for e in ENGS}
        act_table = None
        dma_free = 0.0
        done = [False] * n
        low = 0
        nsched = 0
        if cutoff is None:
            cutoff = n
        po = {e: [o.idx for o in ops if o.eng == e and o.idx >= cutoff] for e in ENGS}
        po_ptr = {e: 0 for e in ENGS}
        while nsched < n:
            while low < n and done[low]:
                low += 1
            lim = low + window
            best = None
            best_key = None
            for e in ENGS:
                cl = cand[e]
                if not cl:
                    continue
                te = t_eng[e]
                nxt_po = po[e][po_ptr[e]] if po_ptr[e] < len(po[e]) else -1
                for i in cl:
                    if i > lim:
                        continue
                    if i >= cutoff and i != nxt_po:
                        continue
                    st = ready[i] if ready[i] > te else te
                    if e == "act":
                        tb = ops[i].table
                        if tb is not None and tb != act_table:
                            st += 1.4
                    key = (st, i)
                    if best_key is None or key < best_key:
                        best_key = key
                        best = (e, i)
            e, i = best
            o = ops[i]
            st = best_key[0]
            if o.dma_key is not None:
                issue = 0.08 if e == "sp" else 0.7
                t_eng[e] = st + issue
                xfer = o.cost / 250000.0
                s0 = max(st + issue, dma_free)
                dma_free = s0 + xfer
                finish[i] = s0 + xfer + 2.0
            else:
                if e == "act" and o.table is not None:
                    act_table = o.table
                t_eng[e] = st + o.cost
                finish[i] = st + o.cost + (0.15 if e == "pe" else 0.05)
            done[i] = True
            nsched += 1
            self.t0[i] = st
            if i >= cutoff:
                po_ptr[e] += 1
            cand[e].remove(i)
            o.pos = len(order[e])
            order[e].append(i)
            for u in users[i]:
                ndeps[u] -= 1
                if finish[i] > ready[u]:
                    ready[u] = finish[i]
                if ndeps[u] == 0:
                    cand[ops[u].eng].append(u)
        self.sim_time = max(finish) if n else 0.0
        return order

    def build(self, stack, reorder=True):
        nc = self.nc
        ops = self.ops
        for o in ops:
            o.alldeps = set(o.deps)
        if reorder:
            order = self.schedule(cutoff=self.cutoff)
        else:
            order = {e: [o.idx for o in ops if o.eng == e] for e in ENGS}
            for e in ENGS:
                for p_, i in enumerate(order[e]):
                    ops[i].pos = p_
        for o in ops:
            best = {}
            for d in o.alldeps:
                p = ops[d]
                if p.dma_key is None and o.dma_key is None and p.eng == o.eng:
                    if p.eng == "pe":
                        continue
                    touched = set(o.reads) | set(o.writes) | set(o.xreads)
                    war = (set(p.reads) | set(p.xreads)) & set(o.writes)
                    if not (set(p.writes) & touched) and not war:
                        continue
                k = p.dma_key if p.dma_key is not None else p.eng
                if k not in best or ops[best[k]].pos < p.pos:
                    best[k] = d
            o.deps = sorted(best.values())
            for d in o.deps:
                ops[d].milestone = True
        for o in ops:
            if o.dma_key is not None:
                o.milestone = True
        eng_sem, eng_cnt = {}, {}
        for e in ENGS:
            eng_sem[e] = stack.enter_context(nc.semaphore("sem_" + e))
            eng_cnt[e] = 0
        dma_sem, dma_cnt = {}, {}
        for e in ENGS:
            for i in order[e]:
                o = ops[i]
                if o.dma_key is None and o.milestone:
                    eng_cnt[o.eng] += 1
                    o.sem, o.cnt = eng_sem[o.eng], eng_cnt[o.eng]
        for o in ops:
            if o.dma_key is not None:
                if o.dma_key not in dma_sem:
                    dma_sem[o.dma_key] = stack.enter_context(nc.semaphore("dsem_%d" % len(dma_sem)))
                    dma_cnt[o.dma_key] = 0
                dma_cnt[o.dma_key] += 16
                o.sem, o.cnt = dma_sem[o.dma_key], dma_cnt[o.dma_key]
        for o in ops:
            if o.dma_key is not None and o.dma_key[0] == "castfull":
                o.cnt = dma_cnt[o.dma_key]
        self.n_sems = len(dma_sem) + len(ENGS)
        per_eng = {e: [ops[i] for i in order[e]] for e in ENGS}
        block = stack.enter_context(nc.Block())
        final = [(o.sem, o.cnt) for o in self.final_dma]

        def emit(e, engobj):
            known = {}
            for o in per_eng[e]:
                need = {}
                for d in o.deps:
                    p = ops[d]
                    k = id(p.sem)
                    if k not in need or need[k][1] < p.cnt:
                        need[k] = (p.sem, p.cnt)
                for k, (sem, cnt) in need.items():
                    if known.get(k, 0) >= cnt:
                        continue
                    engobj.wait_ge(sem, cnt)
                    known[k] = cnt
                ins = o.fn(engobj)
                if o.milestone:
                    ins.then_inc(o.sem, 16 if o.dma_key is not None else 1)
            if e == "sp":
                for sem, cnt in final:
                    engobj.wait_ge(sem, cnt)

        @block.tensor
        def _(eng):
            emit("pe", eng)

        @block.scalar
        def _(eng):
            emit("act", eng)

        @block.vector
        def _(eng):
            emit("dve", eng)

        @block.gpsimd
        def _(eng):
            emit("pool", eng)

        @block.sync
        def _(eng):
            emit("sp", eng)


class Ring:
    def __init__(self, items):
        self.items = items
        self.i = 0

    def next(self):
        k = self.i % len(self.items)
        self.i += 1
        return k, self.items[k]


def build_program(tok=TOK):
    nc = bass.Bass("TRN2", target_bir_lowering=False)
    ntl = tok // T
    dt = nc.dram_tensor
    xm = dt("xm", [tok, D], F32, kind="ExternalInput").ap()
    xp = dt("xp", [tok, D], F32, kind="ExternalInput").ap()
    flag_d = dt("flag", [128, 1], F32, kind="ExternalInput").ap()
    par_d = dt("params", [128, NPAR], F32, kind="ExternalInput").ap()
    fg_d = dt("fg", [128, D], F32, kind="ExternalInput").ap()
    w_in = dt("w_in", [D, DIN], F32, kind="ExternalInput").ap()
    w_co = dt("w_co", [DC, D], F32, kind="ExternalInput").ap()
    w_ro = dt("w_ro", [DR, D], F32, kind="ExternalInput").ap()
    w_o = dt("w_o", [D, D], F32, kind="ExternalInput").ap()
    gw = dt("gw", [4 * 128, NBLK * 128], F32, kind="ExternalInput").ap()
    y = dt("y", [tok, D], F32, kind="ExternalOutput").ap()
    def _grp_tiles(base):
        tl = []
        for g in range(4):
            tl += [(base + g * 640, 256), (base + g * 640 + 256, 256), (base + g * 640 + 512, 128)]
        return tl
    FAM = {}
    for _nm, _base in (("rx", O_RX), ("rgate", O_RGATE)):
        FAM[_nm] = (dt(_nm + "_b", [12 * 128, KC * 256], BF16, kind="Internal").ap(), w_in, KC, _grp_tiles(_base))
    for _nm, _base in (("cglu", O_CGLU), ("cval", O_CVAL), ("cgate", O_CGATE), ("gconv", O_GCONV), ("grnn", O_GRNN)):
        FAM[_nm] = (dt(_nm + "_b", [8 * 128, KC * 256], BF16, kind="Internal").ap(), w_in, KC, [(_base + i * 256, 256) for i in range(8)])
    FAM["wro"] = (dt("wro_b", [8 * 128, NRC * 256], BF16, kind="Internal").ap(), w_ro, NRC, [(i * 256, 256) for i in range(8)])
    FAM["wco"] = (dt("wco_b", [8 * 128, NCC * 256], BF16, kind="Internal").ap(), w_co, NCC, [(i * 256, 256) for i in range(8)])
    FAM["wo"] = (dt("wo_b", [8 * 128, KC * 256], BF16, kind="Internal").ap(), w_o, KC, [(i * 256, 256) for i in range(8)])
    gw_b = dt("gw_b", [4 * 128, NBLK * 128], BF16, kind="Internal").ap()
    dg_b = dt("dg_b", [NCC * 128, CW * 128], BF16, kind="Internal").ap()

    with ExitStack() as st:
        def sb(name, shape, dtype):
            return st.enter_context(nc.sbuf_tensor(name, shape, dtype))

        P = Prog(nc)

        par = sb("par", [128, NPAR], F32)
        der = sb("der", [128, NDER], F32)
        tmp20 = sb("tmp20", [128, 20], F32)
        flag = sb("flag_sb", [128, 1], F32)
        fg = sb("fg_sb", [128, D], F32)
        identf = sb("identf", [128, 128], F32)
        ident = sb("ident", [128, 128], BF16)
        onesf = sb("onesf", [128, 128], F32)
        NWS = 3
        wslots = [sb("wslot%d" % i, [128, 20 * 256], BF16) for i in range(NWS)]
        wring = Ring(wslots)
        dgslots = [sb("dgslot%d" % i, [128, CW, 128], BF16) for i in range(2)]
        dgring = Ring(dgslots)
        hT = [sb("hT%d" % i, [128, KC, T], BF16) for i in range(2)]
        hs = Ring([sb("hs%d" % i, [128, D], BF16) for i in range(2)])
        xring = Ring([sb("xblk%d" % i, [128, D], F32) for i in range(3)])
        junk = sb("junk", [128, D], BF16)
        ssq = sb("ssq", [128, 2], F32)
        rs = sb("rs", [128, 2], F32)
        ssq2 = sb("ssq2", [128, 2], F32)
        rs2 = sb("rs2", [128, 2], F32)
        cv = sb("cv", [128, NCC, T], F32)
        ug = sb("ug", [128, NCC, T], BF16)
        hg = sb("hg", [128, NRC, T], BF16)
        mg = cv[:].rearrange("p c t -> p (c t)")[:, 0:KC * T // 2].bitcast(BF16).rearrange("p (k t) -> p k t", t=T)
        ubuf = Ring([sb("ubuf%d" % i, [128, 30 + T], BF16) for i in range(3)])
        uh = sb("uh", [128, NCC, 30], BF16)
        rxr = Ring([sb("rx%d" % i, [128, 3 + T], F32) for i in range(2)])
        rxh = sb("rxh", [128, NRC, 3], F32)
        state = sb("state", [128, NRC], F32)
        vbuf = [sb("v%d" % i, [128, T], F32) for i in range(10)]
        vbb = [sb("vb%d" % i, [128, T], BF16) for i in range(10)]
        NG = 6
        bufA = Ring([sb("bA%d" % i, [128, T], F32) for i in range(NG)])
        bufB = Ring([sb("bB%d" % i, [128, T], F32) for i in range(NG)])
        bufC = Ring([sb("bC%d" % i, [128, T], F32) for i in range(NG)])
        bufD = Ring([sb("bD%d" % i, [128, T], F32) for i in range(2)])
        bufE = Ring([sb("bE%d" % i, [128, T], F32) for i in range(5)])
        tbuf = Ring([sb("tb%d" % i, [128, T], F32) for i in range(2)])
        sqt = Ring([sb("sq%d" % i, [128, T], F32) for i in range(2)])
        acc1 = sb("acc1", [128, T], F32)
        acc2 = sb("acc2", [128, T], F32)
        zbuf = Ring([sb("z%d" % i, [128, T], F32) for i in range(2)])
        t1buf = Ring([sb("t1%d" % i, [128, T], F32) for i in range(1)])
        t2buf = Ring([sb("t2%d" % i, [128, T], F32) for i in range(2)])
        lnM = sb("lnM", [128, T], F32)
        lnR = sb("lnR", [128, T], F32)
        lnN = sb("lnN", [128, T], F32)
        mt1 = Ring([sb("mt1%d" % i, [128, T], F32) for i in range(2)])
        mt2 = Ring([sb("mt2%d" % i, [128, T], F32) for i in range(2)])
        banks = [st.enter_context(nc.psum_tensor("bank%d" % i, [128, 512], F32)) for i in range(8)]
        bring = Ring(banks[0:7])
        pvb = banks[7]

        def newbank():
            k, b = bring.next()
            return ("ps", k), b

        def nfree(ap):
            n = 1
            for d in ap.shape[1:]:
                n *= int(d)
            return n

        def dma(eng, out, in_, reads, writes, key):
            nb = nfree(out) * int(out.shape[0]) * (2 if out.dtype == BF16 else 4)
            if in_.dtype != out.dtype:
                nb *= 3
            return P.op(eng, lambda e: e.dma_start(out=out, in_=in_), reads=reads, writes=writes, dma_key=key, cost=float(nb))

        TABLE = {AF.Tanh: "exp", AF.Exp: "exp", AF.Sqrt: "sqrt", AF.Ln: "ln"}

        def act(out, in_, func, reads=(), writes=(), xreads=(), **kw):
            n = nfree(out)
            cost = 0.2 + n / 1200.0 + (0.09 if not isinstance(kw.get("scale", 1.0), float) else 0.0) + (0.09 if not isinstance(kw.get("bias", 0.0), float) else 0.0)
            return P.op("act", lambda e: e.activation(out=out, in_=in_, func=func, **kw), reads=reads, writes=writes, xreads=xreads,
                        cost=cost, table=TABLE.get(func))

        def dve(fn, reads=(), writes=(), xreads=(), n=T, k=2.0):
            if xreads and k == 2.0:
                k = 1.0
            return P.op("dve", fn, reads=reads, writes=writes, xreads=xreads, cost=0.08 + k * n / 960.0)

        def pool(fn, reads=(), writes=(), n=T):
            return P.op("pool", fn, reads=reads, writes=writes, cost=0.35 + n / 350.0)

        def col(t, c):
            return t[:, c:c + 1]

        dma("sp", par[:], par_d, [], ["par"], "ld_par")
        dma("sp", flag[:], flag_d, [], ["flag"], "ld_flag")
        dma("sp", fg[:], fg_d, [], ["fg"], "ld_fg")
        pool(lambda e: e.memset(identf[:], 0.0), writes=["identf"])
        pool(lambda e: e.affine_select(out=identf[:], in_=identf[:], pattern=[[-1, 128]], compare_op=ALU.not_equal,
                                       fill=1.0, base=0, channel_multiplier=1), reads=["identf"], writes=["identf"])
        pool(lambda e: e.memset(onesf[:], 1.0), writes=["onesf"])
        pool(lambda e: e.memset(state[:], 0.0), writes=[("state", c) for c in range(NRC)])
        pool(lambda e: e.memset(rxh[:], 0.0), writes=[("rxh", c) for c in range(NRC)])
        pool(lambda e: e.memset(uh[:], 0.0), writes=[("uh", c) for c in range(NCC)])
        dve(lambda e: e.tensor_copy(out=ident[:], in_=identf[:]), reads=["identf"], writes=["ident"])
        act(tmp20[:], par[:, P_LAM:P_LAM + 20], AF.Exp, reads=["par"], writes=["tmp20"], scale=-1.0)
        act(tmp20[:], tmp20[:], AF.Ln, reads=["tmp20"], writes=["tmp20"], bias=1.0)
        dve(lambda e: e.tensor_scalar(out=der[:, Q_HC:Q_HC + 20], in0=tmp20[:], scalar1=-4.0, scalar2=None, op0=ALU.mult),
            reads=["tmp20"], writes=["der_hc"])
        dve(lambda e: e.tensor_scalar(out=der[:, Q_CC:Q_CC + 20], in0=tmp20[:], scalar1=-8.0, scalar2=None, op0=ALU.mult),
            reads=["tmp20"], writes=["der_cc"])
        for (q, p_, n) in ((Q_HBA, P_BA, 20), (Q_HBX, P_BX, 20), (Q_HLG, P_LG, 16), (Q_HLB, P_LB, 16)):
            dve(lambda e, q=q, p_=p_, n=n: e.tensor_scalar(out=der[:, q:q + n], in0=par[:, p_:p_ + n], scalar1=0.5,
                                                           scalar2=None, op0=ALU.mult), reads=["par"], writes=[("der", q)])
        DER = ["der_hc", "der_cc", ("der", Q_HBA), ("der", Q_HBX), ("der", Q_HLG), ("der", Q_HLB)]

        pace = {"res": None}

        def cast_grp(name, i):
            return (i // 3) if name == "rx" else (i if name == "gw" else 0)

        def cast_piece(name, i, dst, src):
            grp = cast_grp(name, i)
            dma("pool", dst, src, ([pace["res"]] if pace["res"] is not None else []), [("wb", name, i)], ("castfull", name, grp))

        def fam_piece(name, j):
            scr, src, nk, tiles = FAM[name]
            c0, w = tiles[j]
            dst = scr[j * 128:(j + 1) * 128, 0:nk * w].rearrange("p (k n) -> p k n", n=w)
            cast_piece(name, j, dst, src[:, c0:c0 + w].rearrange("(k p) n -> p k n", p=128))

        def early_casts():
            for g in range(4):
                for j in range(3):
                    fam_piece("rx", 3 * g + j)
                cast_piece("gw", g, gw_b[g * 128:(g + 1) * 128, :].rearrange("p (a n) -> p a n", n=256),
                           gw[g * 128:(g + 1) * 128, :].rearrange("p (a n) -> p a n", n=256))
        late_pieces = []
        for i in range(8):
            late_pieces.append(lambda i=i: fam_piece("cglu", i))
            late_pieces.append(lambda i=i: fam_piece("cval", i))
        for i in range(12):
            late_pieces.append(lambda i=i: fam_piece("rgate", i))
        for i in range(8):
            late_pieces.append(lambda i=i: fam_piece("cgate", i))
        for i in range(8):
            late_pieces.append(lambda i=i: fam_piece("gconv", i))
            late_pieces.append(lambda i=i: fam_piece("grnn", i))
            late_pieces.append(lambda i=i: fam_piece("wro", i))
            late_pieces.append(lambda i=i: fam_piece("wco", i))
        for i in range(8):
            late_pieces.append(lambda i=i: fam_piece("wo", i))

        def emit_late(n):
            for _ in range(n):
                if late_pieces:
                    late_pieces.pop(0)()

        def build_diag(c):
            s_, slot = dgring.next()
            for k in range(CW):
                dve(lambda e, slot=slot, k=k, c=c: e.tensor_scalar(out=slot[:, k, :], in0=identf[:],
                                                                   scalar1=col(par, P_CW + c * CW + k), scalar2=0.5,
                                                                   op0=ALU.mult, op1=ALU.mult),
                    reads=["identf", "par"], writes=[("dgslot", s_)], n=128, k=1)
            dma("sp", dg_b[c * 128:(c + 1) * 128, :], slot[:].rearrange("p k n -> p (k n)"), [("dgslot", s_)], [("dgb", c)], ("dgout", s_))

        slot_gen = {}

        def chk(res):
            assert slot_gen[res[0:2]] == res[2], "stale weight tile %r" % (res,)
            return res[0:2]

        def load_tile(name, j):
            scr, src, nk, tiles = FAM[name]
            c0, w = tiles[j]
            s, slot = wring.next()
            slot_gen[("wslot", s)] = slot_gen.get(("wslot", s), 0) + 1
            view = slot[:, 0:nk * w].rearrange("p (k n) -> p k n", n=w)
            members = [("wb", name, i) for i in range(len(tiles)) if cast_grp(name, i) == cast_grp(name, j)]
            dma("sp", slot[:, 0:nk * w], scr[j * 128:(j + 1) * 128, 0:nk * w], members, [("wslot", s)], ("wslot", s))
            return ("wslot", s, slot_gen[("wslot", s)]), view

        def load_win(col0, wbname, ncols=256):
            tiles = FAM[wbname][3]
            j = [i for i, (c0, w) in enumerate(tiles) if c0 == col0]
            assert len(j) == 1 and tiles[j[0]][1] == ncols, (wbname, col0, ncols)
            return load_tile(wbname, j[0])

        def load_gate(g):
            s, slot = wring.next()
            slot_gen[("wslot", s)] = slot_gen.get(("wslot", s), 0) + 1
            view = slot[:, 0:NBLK * 128].rearrange("p (k n) -> p k n", n=128)
            dma("sp", slot[:, 0:NBLK * 128], gw_b[g * 128:(g + 1) * 128, :], [("wb", "gw", g)], [("wslot", s)], ("wslot", s))
            return ("wslot", s, slot_gen[("wslot", s)]), view

        def load_dg(c):
            s, slot = dgring.next()
            dma("sp", slot[:].rearrange("p k n -> p (k n)"), dg_b[c * 128:(c + 1) * 128, :], [("dgb", c)], [("dgslot", s)], ("dgslot", s))
            return ("dgslot", s), slot

        def mm_group(bank_res, out_ap, pairs, extra_reads):
            n = len(pairs)
            pairs = [(l, r, [chk(x) if (isinstance(x, tuple) and x[0] == "wslot") else x for x in rd]) for (l, r, rd) in pairs]
            for i, (l, r, rd) in enumerate(pairs):
                ncol = nfree(r)
                P.op("pe", lambda e, l=l, r=r, i=i: e.matmul(out_ap, lhsT=l, rhs=r, start=(i == 0), stop=(i == n - 1)),
                     reads=list(rd) + list(extra_reads), writes=[bank_res], cost=max(64, ncol) / 1900.0 * (4.0 if r.dtype == F32 else 1.0))

        def prep(src, tile_idx, buf):
            r0 = tile_idx * T
            blocks = []
            for tb in range(2):
                xi, xb = xring.next()
                dma("pool", xb[:], src[r0 + tb * 128:r0 + (tb + 1) * 128, :], [], [("x", xi)], ("x", xi))
                hi, hsb = hs.next()
                act(junk[:], xb[:], AF.Square, reads=[("x", xi)], writes=["junk", ("ssq", tb)], accum_out=col(ssq, tb))
                blocks.append((xi, xb, hi, hsb))
            dve(lambda e: e.tensor_scalar(out=rs[:], in0=ssq[:], scalar1=1.0 / D, scalar2=EPS, op0=ALU.mult, op1=ALU.add),
                reads=[("ssq", 0), ("ssq", 1)], writes=["rs"], n=2, k=1)
            act(rs[:], rs[:], AF.Sqrt, reads=["rs"], writes=["rs"])
            dve(lambda e: e.reciprocal(out=rs[:], in_=rs[:]), reads=["rs"], writes=["rs"], n=2, k=8)
            pe_parts = []
            for tb, (xi, xb, hi, hsb) in enumerate(blocks):
                act(hsb[:], xb[:], AF.Copy, reads=[("x", xi), "rs"], writes=[("hs", hi)], scale=col(rs, tb))
                pe_parts.append((tb, hi, hsb))
            return pe_parts

        def prep_pe(pe_parts, buf, all_act=False):
            for (tb, hi, hsb) in pe_parts:
                for q in range(4):
                    bres, bank = newbank()
                    bb = bank[:].bitcast(BF16)
                    for i in range(4):
                        fc = q * 4 + i
                        P.op("pe", lambda e, bb=bb, i=i, fc=fc, hsb=hsb: e.transpose(out=bb[:, i * 128:(i + 1) * 128],
                                                                                     in_=hsb[:, fc * 128:(fc + 1) * 128],
                                                                                     identity=ident[:]),
                             reads=[("hs", hi), "ident"], writes=[bres], cost=0.08)
                    for i in range(4):
                        fc = q * 4 + i
                        o_ap = hT[buf][:, fc, tb * 128:(tb + 1) * 128]
                        i_ap = bb[:, i * 128:(i + 1) * 128]
                        if q % 2 == 0 and not all_act:
                            dve(lambda e, o_ap=o_ap, i_ap=i_ap, fc=fc: e.tensor_scalar(out=o_ap, in0=i_ap, scalar1=col(par, P_G + fc),
                                                                                       scalar2=None, op0=ALU.mult),
                                reads=["par"], writes=[("hT", buf, fc)], xreads=[bres], n=128, k=1)
                        else:
                            act(o_ap, i_ap, AF.Copy, reads=["par"], writes=[("hT", buf, fc)], xreads=[bres], scale=col(par, P_G + fc))

        def hT_reads(buf):
            return [("hT", buf, fc) for fc in range(KC)]

        def rnn_stage(buf, main, mid_hook=None, late_n=0, fillers=None, nfill=2):
            hr = hT_reads(buf)
            wt = {}

            pend_cast = []

            def flush_cast():
                while pend_cast:
                    vi = pend_cast.pop(0)
                    if main:
                        act(vbb[vi][:], vbuf[vi][:], AF.Copy, reads=[("v", vi)], writes=[("vb", vi)])
                    else:
                        pool(lambda e, vi=vi: e.tensor_copy(out=vbb[vi][:], in_=vbuf[vi][:]), reads=[("v", vi)], writes=[("vb", vi)])

            def rx_chunk(c):
                lc = c % 5
                if lc % 2 == 0:
                    wt["rx"] = load_win(O_RX + c * 128, "rx", 128 if lc == 4 else 256)
                wres, wv = wt["rx"]
                bres, bank = newbank()
                o = (lc % 2) * 128
                mm_group(bres, bank[:, 0:T], [(wv[:, k, o:o + 128], hT[buf][:, k, :], [wres, ("hT", buf, k)]) for k in range(KC)], [])
                ri, rx = rxr.next()
                pace["n"] = pace.get("n", 0) + 1
                act(rx[:, 3:3 + T], bank[:, 0:T], AF.Identity, xreads=[bres], writes=[("rx", ri), ("pace", pace["n"])])
                dve(lambda e, rx=rx, c=c: e.tensor_copy(out=rx[:, 0:3], in_=rxh[:, c, :]), reads=[("rxh", c)], writes=[("rxhalo", ri)], n=3, k=1)
                flush_cast()
                vi = c % 10
                v = vbuf[vi]
                dve(lambda e, rx=rx, c=c: e.tensor_scalar(out=pvb[:, 0:T], in0=rx[:, 3:3 + T], scalar1=col(par, P_RW + c * RW + 3),
                                                          scalar2=col(par, P_RB + c), op0=ALU.mult, op1=ALU.add),
                    reads=[("rx", ri), "par"], writes=["pv"], k=1)
                for k in range(3):
                    last = (k == 2)
                    dve(lambda e, rx=rx, v=v, c=c, k=k, last=last: e.scalar_tensor_tensor(out=(v[:] if last else pvb[:, 0:T]), in0=rx[:, k:k + T],
                                                                                          scalar=col(par, P_RW + c * RW + k),
                                                                                          in1=pvb[:, 0:T], op0=ALU.mult, op1=ALU.add),
                        reads=[("rx", ri), ("rxhalo", ri), "par", "pv"], writes=([("v", vi)] if last else ["pv"]), k=1)
                dve(lambda e, rx=rx, c=c: e.tensor_copy(out=rxh[:, c, :], in_=rx[:, T:T + 3]), reads=[("rx", ri)], writes=[("rxh", c)], n=3, k=1)
                pend_cast.append(vi)

            def gates_block(g):
                gres, gv = load_gate(g)
                items = []
                for oc in range(5):
                    c = g * 5 + oc
                    rec = {"c": c}
                    for gate in range(2):
                        bres, bank = newbank()
                        pairs = []
                        for ic in GATE_ICS[oc]:
                            vi = (g * 5 + ic) % 10
                            pairs.append((gv[:, GATE_BLK[(gate, oc, ic)], :], vbb[vi][:], [gres, ("vb", vi)]))
                        mm_group(bres, bank[:, 0:T], pairs, [])
                        rec[gate] = (bres, bank)
                    ai, A = bufA.next()
                    bi, B = bufB.next()
                    ci, C = bufC.next()
                    rec.update(A=A, ai=ai, B=B, bi=bi, C=C, ci=ci)
                    bres, bank = rec[0]
                    act(A[:], bank[:, 0:T], AF.Tanh, xreads=[bres], reads=DER, writes=[("A", ai)], scale=0.5, bias=col(der, Q_HBA + c))
                    bres, bank = rec[1]
                    act(C[:], bank[:, 0:T], AF.Tanh, xreads=[bres], reads=DER, writes=[("C", ci)], scale=0.5, bias=col(der, Q_HBX + c))
                    act(B[:], A[:], AF.Exp, reads=[("A", ai)] + DER, writes=[("B", bi)], scale=col(der, Q_HC + c), bias=col(der, Q_HC + c))
                    pool(lambda e, A=A, B=B: e.tensor_tensor(out=A[:], in0=B[:], in1=B[:], op=ALU.mult), reads=[("B", bi)], writes=[("A", ai)])
                    vi_ = c % 10
                    if main:
                        dve(lambda e, C=C, vv_=vbuf[vi_]: e.scalar_tensor_tensor(out=C[:], in0=C[:], scalar=1.0, in1=vv_[:], op0=ALU.add, op1=ALU.mult),
                            reads=[("C", ci), ("v", vi_)], writes=[("C", ci)])
                    else:
                        pool(lambda e, C=C: e.tensor_scalar(out=C[:], in0=C[:], scalar1=1.0, scalar2=1.0, op0=ALU.add, op1=ALU.mult),
                             reads=[("C", ci)], writes=[("C", ci)])
                        pool(lambda e, C=C, vv_=vbuf[vi_]: e.tensor_tensor(out=C[:], in0=C[:], in1=vv_[:], op=ALU.mult),
                             reads=[("C", ci), ("v", vi_)], writes=[("C", ci)])
                    items.append(rec)
                if main:
                    for oc in range(5):
                        rec = items[oc]
                        c = rec["c"]
                        lc = c % 5
                        if lc % 2 == 0:
                            wt["rg"] = load_win(O_RGATE + c * 128, "rgate", 128 if lc == 4 else 256)
                        wres, wv = wt["rg"]
                        bres, bank = newbank()
                        o = (lc % 2) * 128
                        mm_group(bres, bank[:, 0:T], [(wv[:, k, o:o + 128], hT[buf][:, k, :], [wres, ("hT", buf, k)]) for k in range(KC)], [])
                        ei, E = bufE.next()
                        act(E[:], bank[:, 0:T], AF.Tanh, xreads=[bres], writes=[("E", ei)], scale=0.5)
                        dve(lambda e, E=E, bank=bank: e.scalar_tensor_tensor(out=E[:], in0=E[:], scalar=1.0, in1=bank[:, 0:T],
                                                                             op0=ALU.add, op1=ALU.mult),
                            reads=[("E", ei)], writes=[("E", ei)], xreads=[bres])
                        rec.update(E=E, ei=ei)
                for rec in items:
                    A, ai = rec["A"], rec["ai"]
                    act(A[:], A[:], AF.Sqrt, reads=[("A", ai)], writes=[("A", ai)], scale=-1.0, bias=1.0)
                for rec in items:
                    c = rec["c"]
                    A, ai, B, bi, C, ci = rec["A"], rec["ai"], rec["B"], rec["bi"], rec["C"], rec["ci"]
                    vi = c % 10
                    v = vbuf[vi]
                    dve(lambda e, C=C, A=A: e.scalar_tensor_tensor(out=C[:], in0=A[:], scalar=0.0, in1=C[:], op0=ALU.max, op1=ALU.mult),
                        reads=[("C", ci), ("A", ai)], writes=[("C", ci)])
                    di, Dd = bufD.next()
                    dve(lambda e, Dd=Dd, B=B, C=C, c=c: e.tensor_tensor_scan(out=Dd[:], data0=B[:], data1=C[:], initial=col(state, c),
                                                                             op0=ALU.mult, op1=ALU.add),
                        reads=[("B", bi), ("C", ci), ("state", c)], writes=[("D", di)])
                    dve(lambda e, Dd=Dd, c=c: e.tensor_copy(out=col(state, c), in_=Dd[:, T - 1:T]), reads=[("D", di)], writes=[("state", c)], n=1, k=1)
                    if main:
                        E, ei = rec["E"], rec["ei"]
                        dve(lambda e, Dd=Dd, E=E, c=c: e.tensor_tensor(out=hg[:, c, :], in0=Dd[:], in1=E[:], op=ALU.mult),
                            reads=[("D", di), ("E", ei)], writes=[("hg", c)])

            for g in range(4):
                for c in range(g * 5, g * 5 + 5):
                    rx_chunk(c)
                flush_cast()
                if late_n:
                    pace["res"] = ("pace", pace["n"])
                    emit_late(late_n)
                if g == 2 and mid_hook is not None:
                    mid_hook()
                if fillers:
                    for _ in range(nfill):
                        if fillers:
                            fillers.pop(0)()
                if g >= 1:
                    gates_block(g - 1)
            gates_block(3)

        def conv_halo(buf):
            hr = hT_reads(buf)
            wt = {}
            for c in range(NCC):
                if c % 2 == 0:
                    wt["g"] = load_win(O_CGLU + c * 128, "cglu")
                    wt["v"] = load_win(O_CVAL + c * 128, "cval")
                o = (c % 2) * 128
                gres, gv = wt["g"]
                vres, vv = wt["v"]
                b1, bk1 = newbank()
                mm_group(b1, bk1[:, 0:32], [(gv[:, k, o:o + 128], hT[buf][:, k, T - 32:T], [gres, ("hT", buf, k)]) for k in range(KC)], [])
                b2, bk2 = newbank()
                mm_group(b2, bk2[:, 0:32], [(vv[:, k, o:o + 128], hT[buf][:, k, T - 32:T], [vres, ("hT", buf, k)]) for k in range(KC)], [])
                ti, tt = tbuf.next()
                act(tt[:, 0:32], bk1[:, 0:32], AF.Tanh, xreads=[b1], writes=[("t", ti)], scale=0.5)
                dve(lambda e, tt=tt, bk2=bk2, c=c: e.scalar_tensor_tensor(out=uh[:, c, :], in0=tt[:, 2:32], scalar=1.0, in1=bk2[:, 2:32],
                                                                          op0=ALU.add, op1=ALU.mult),
                    reads=[("t", ti)], writes=[("uh", c)], xreads=[b2])

        def mask_state():
            dve(lambda e: e.tensor_scalar(out=state[:], in0=state[:], scalar1=flag[:, 0:1], scalar2=None, op0=ALU.mult),
                reads=["flag"] + [("state", c) for c in range(NRC)], writes=[("state", c) for c in range(NRC)])
            dve(lambda e: e.tensor_scalar(out=rxh[:].rearrange("p c k -> p (c k)"), in0=rxh[:].rearrange("p c k -> p (c k)"),
                                          scalar1=flag[:, 0:1], scalar2=None, op0=ALU.mult),
                reads=["flag"] + [("rxh", c) for c in range(NRC)], writes=[("rxh", c) for c in range(NRC)])
            dve(lambda e: e.tensor_scalar(out=uh[:].rearrange("p c k -> p (c k)"), in0=uh[:].rearrange("p c k -> p (c k)"),
                                          scalar1=flag[:, 0:1], scalar2=None, op0=ALU.mult),
                reads=["flag"] + [("uh", c) for c in range(NCC)], writes=[("uh", c) for c in range(NCC)])

        def conv_stage(buf):
            hr = hT_reads(buf)
            wt = {}
            pend = []

            def gv_chunk(c):
                if c % 2 == 0:
                    wt["g"] = load_win(O_CGLU + c * 128, "cglu")
                    wt["v"] = load_win(O_CVAL + c * 128, "cval")
                o = (c % 2) * 128
                gres, gv_ = wt["g"]
                vres, vv = wt["v"]
                b1, bk1 = newbank()
                mm_group(b1, bk1[:, 0:T], [(gv_[:, k, o:o + 128], hT[buf][:, k, :], [gres, ("hT", buf, k)]) for k in range(KC)], [])
                b2, bk2 = newbank()
                mm_group(b2, bk2[:, 0:T], [(vv[:, k, o:o + 128], hT[buf][:, k, :], [vres, ("hT", buf, k)]) for k in range(KC)], [])
                ti, tt = tbuf.next()
                act(tt[:], bk1[:, 0:T], AF.Tanh, xreads=[b1], writes=[("t", ti)], scale=0.5)
                ui, ub = ubuf.next()
                dve(lambda e, tt=tt, bk2=bk2, ub=ub: e.scalar_tensor_tensor(out=ub[:, 30:30 + T], in0=tt[:], scalar=1.0, in1=bk2[:, 0:T],
                                                                            op0=ALU.add, op1=ALU.mult),
                    reads=[("t", ti)], writes=[("u", ui)], xreads=[b2])
                pool(lambda e, ub=ub, c=c: e.tensor_copy(out=ub[:, 0:30], in_=uh[:, c, :]), reads=[("uh", c)], writes=[("uhalo", ui)])
                pool(lambda e, ub=ub, c=c: e.tensor_copy(out=uh[:, c, :], in_=ub[:, T:T + 30]), reads=[("u", ui)], writes=[("uh", c)])
                return ui, ub

            def conv_chunk(c, ui, ub):
                dres, dgv = load_dg(c)
                b3, bk3 = newbank()
                mm_group(b3, bk3[:, 0:T], [(dgv[:, k, :], ub[:, k:k + T], [dres]) for k in range(CW)], [("u", ui), ("uhalo", ui)])
                act(cv[:, c, :], bk3[:, 0:T], AF.Identity, xreads=[b3], reads=["par"], writes=[("cv", c), ("mg", 2 * c), ("mg", 2 * c + 1)],
                    bias=col(par, P_CB + c))
                si, sq = sqt.next()
                act(sq[:], cv[:, c, :], AF.Square, reads=[("cv", c)], writes=[("sq", si)])
                if c == 0:
                    pool(lambda e: e.tensor_copy(out=acc1[:], in_=cv[:, 0, :]), reads=[("cv", 0)], writes=["acc1"])
                    pool(lambda e, sq=sq: e.tensor_copy(out=acc2[:], in_=sq[:]), reads=[("sq", si)], writes=["acc2"])
                else:
                    pool(lambda e, c=c: e.tensor_tensor(out=acc1[:], in0=acc1[:], in1=cv[:, c, :], op=ALU.add), reads=[("cv", c), "acc1"], writes=["acc1"])
                    pool(lambda e, sq=sq: e.tensor_tensor(out=acc2[:], in0=acc2[:], in1=sq[:], op=ALU.add), reads=[("sq", si), "acc2"], writes=["acc2"])

            def step_pair(c0):
                for c in (c0, c0 + 1):
                    ui, ub = gv_chunk(c)
                    pend.append((c, ui, ub))
                    if len(pend) > 1:
                        conv_chunk(*pend.pop(0))

            def rest():
                conv_chunk(*pend.pop(0))
                conv_rest(buf)

            return step_pair, rest

        def conv_rest(buf):
            wt = {}
            b1, bk1 = newbank()
            P.op("pe", lambda e: e.matmul(bk1[:, 0:T], lhsT=onesf[:], rhs=acc1[:], start=True, stop=True), reads=["onesf", "acc1"], writes=[b1], cost=0.6)
            b2, bk2 = newbank()
            P.op("pe", lambda e: e.matmul(bk2[:, 0:T], lhsT=onesf[:], rhs=acc2[:], start=True, stop=True), reads=["onesf", "acc2"], writes=[b2], cost=0.6)
            dve(lambda e: e.tensor_scalar(out=lnM[:], in0=bk1[:, 0:T], scalar1=1.0 / DC, scalar2=None, op0=ALU.mult), xreads=[b1], writes=["lnM"])
            dve(lambda e: e.tensor_tensor(out=lnN[:], in0=lnM[:], in1=lnM[:], op=ALU.mult), reads=["lnM"], writes=["lnN"])
            dve(lambda e: e.scalar_tensor_tensor(out=lnR[:], in0=bk2[:, 0:T], scalar=1.0 / DC, in1=lnN[:], op0=ALU.mult, op1=ALU.subtract),
                reads=["lnN"], writes=["lnR"], xreads=[b2])
            dve(lambda e: e.tensor_scalar(out=lnR[:], in0=lnR[:], scalar1=0.0, scalar2=LN_EPS, op0=ALU.max, op1=ALU.add), reads=["lnR"], writes=["lnR"])
            act(lnR[:], lnR[:], AF.Sqrt, reads=["lnR"], writes=["lnR"])
            dve(lambda e: e.reciprocal(out=lnR[:], in_=lnR[:]), reads=["lnR"], writes=["lnR"], k=8)
            dve(lambda e: e.scalar_tensor_tensor(out=lnN[:], in0=lnM[:], scalar=-1.0, in1=lnR[:], op0=ALU.mult, op1=ALU.mult),
                reads=["lnM", "lnR"], writes=["lnN"])
            for c in range(NCC):
                if c % 2 == 0:
                    wt["cg"] = load_win(O_CGATE + c * 128, "cgate")
                o = (c % 2) * 128
                wres, wv = wt["cg"]
                b3, bk3 = newbank()
                mm_group(b3, bk3[:, 0:T], [(wv[:, k, o:o + 128], hT[buf][:, k, :], [wres, ("hT", buf, k)]) for k in range(KC)], [])
                t2i, t2 = t2buf.next()
                act(t2[:], bk3[:, 0:T], AF.Tanh, xreads=[b3], writes=[("t2", t2i)], scale=0.5)
                dve(lambda e, t2=t2, bk3=bk3: e.scalar_tensor_tensor(out=t2[:], in0=t2[:], scalar=1.0, in1=bk3[:, 0:T], op0=ALU.add, op1=ALU.mult),
                    reads=[("t2", t2i)], writes=[("t2", t2i)], xreads=[b3])
                zi, z = zbuf.next()
                pool(lambda e, z=z, c=c: e.tensor_tensor(out=z[:], in0=cv[:, c, :], in1=lnR[:], op=ALU.mult), reads=[("cv", c), "lnR"], writes=[("z", zi)])
                pool(lambda e, z=z: e.tensor_tensor(out=z[:], in0=z[:], in1=lnN[:], op=ALU.add), reads=[("z", zi), "lnN"], writes=[("z", zi)])
                dve(lambda e, z=z, c=c: e.tensor_scalar(out=z[:], in0=z[:], scalar1=col(par, P_LG + c), scalar2=col(par, P_LB + c),
                                                        op0=ALU.mult, op1=ALU.add), reads=[("z", zi), "par"], writes=[("z", zi)])
                t1i, t1 = t1buf.next()
                act(t1[:], z[:], AF.Tanh, reads=[("z", zi)], writes=[("t1", t1i)], scale=0.5)
                dve(lambda e, z=z, t1=t1: e.scalar_tensor_tensor(out=z[:], in0=t1[:], scalar=1.0, in1=z[:], op0=ALU.add, op1=ALU.mult),
                    reads=[("t1", t1i), ("z", zi)], writes=[("z", zi)])
                dve(lambda e, z=z, t2=t2, c=c: e.tensor_tensor(out=ug[:, c, :], in0=z[:], in1=t2[:], op=ALU.mult),
                    reads=[("z", zi), ("t2", t2i)], writes=[("ug", c)])

        def merge_stage(buf):
            hr = hT_reads(buf)
            for mp in range(KC // 2):
                ms = (2 * mp, 2 * mp + 1)
                recs = {m: {} for m in ms}
                for key, col0, wbn in (("gc", O_GCONV, "gconv"), ("gr", O_GRNN, "grnn")):
                    wres, wv = load_win(col0 + ms[0] * 128, wbn)
                    for j, m in enumerate(ms):
                        b1, bk1 = newbank()
                        mm_group(b1, bk1[:, 0:T], [(wv[:, k, j * 128:(j + 1) * 128], hT[buf][:, k, :], [wres, ("hT", buf, k)]) for k in range(KC)], [])
                        ring = mt1 if key == "gc" else mt2
                        i1, m1 = ring.next()
                        act(m1[:], bk1[:, 0:T], AF.Tanh, xreads=[b1], writes=[(key, i1)], scale=0.5)
                        recs[m][key] = (i1, m1)
                for key, gkey, src, nk, wbn, rname in (("yr", "gr", hg, NRC, "wro", "hg"), ("yc", "gc", ug, NCC, "wco", "ug")):
                    wres, wv = load_tile(wbn, mp)
                    for j, m in enumerate(ms):
                        b3, bk3 = newbank()
                        mm_group(b3, bk3[:, 0:T], [(wv[:, k, j * 128:(j + 1) * 128], src[:, k, :], [wres, (rname, k)]) for k in range(nk)], [])
                        i1, m1 = recs[m][gkey]
                        dve(lambda e, m1=m1, bk3=bk3: e.scalar_tensor_tensor(out=m1[:], in0=m1[:], scalar=1.0, in1=bk3[:, 0:T], op0=ALU.add, op1=ALU.mult),
                            reads=[(gkey, i1)], writes=[(gkey, i1)], xreads=[b3])
                for m in ms:
                    i1, m1 = recs[m]["gc"]
                    i2, m2 = recs[m]["gr"]
                    dve(lambda e, m1=m1, m2=m2, m=m: e.tensor_tensor(out=mg[:, m, :], in0=m1[:], in1=m2[:], op=ALU.add),
                        reads=[("gc", i1), ("gr", i2)], writes=[("mg", m), ("cv", m // 2)])

        def out_stage(tile_idx):
            r0 = tile_idx * T
            xs = []
            for tb in range(2):
                xi, xb = xring.next()
                dma("pool", xb[:], xm[r0 + tb * 128:r0 + (tb + 1) * 128, :], [], [("x", xi)], ("x", xi))
                xs.append((xi, xb))
            for cg in range(8):
                wres, wv = load_tile("wo", cg)
                for tb in range(2):
                    xi, xb = xs[tb]
                    b1, bk1 = newbank()
                    mm_group(b1, bk1[:, 0:256], [(mg[:, k, tb * 128:(tb + 1) * 128], wv[:, k, :], [wres, ("mg", k)]) for k in range(KC)], [])
                    dve(lambda e, xb=xb, bk1=bk1, cg=cg: e.scalar_tensor_tensor(out=xb[:, cg * 256:(cg + 1) * 256], in0=bk1[:, 0:256], scalar=0.125,
                                                                               in1=xb[:, cg * 256:(cg + 1) * 256], op0=ALU.mult, op1=ALU.add),
                        reads=[("x", xi)], writes=[("x", xi)], xreads=[b1])
            for tb in range(2):
                xi, xb = xs[tb]
                act(junk[:], xb[:], AF.Square, reads=[("x", xi)], writes=["junk", ("ssq2", tb)], accum_out=col(ssq2, tb))
            dve(lambda e: e.tensor_scalar(out=rs2[:], in0=ssq2[:], scalar1=1.0 / D, scalar2=EPS, op0=ALU.mult, op1=ALU.add),
                reads=[("ssq2", 0), ("ssq2", 1)], writes=["rs2"], n=2, k=1)
            act(rs2[:], rs2[:], AF.Sqrt, reads=["rs2"], writes=["rs2"])
            dve(lambda e: e.reciprocal(out=rs2[:], in_=rs2[:]), reads=["rs2"], writes=["rs2"], n=2, k=8)
            outs = []
            for tb in range(2):
                xi, xb = xs[tb]
                dve(lambda e, xb=xb, tb=tb: e.scalar_tensor_tensor(out=xb[:], in0=xb[:], scalar=col(rs2, tb), in1=fg[:], op0=ALU.mult, op1=ALU.mult),
                    reads=[("x", xi), "rs2", "fg"], writes=[("x", xi)], n=D, k=2)
                outs.append(dma("pool", y[r0 + tb * 128:r0 + (tb + 1) * 128, :], xb[:], [("x", xi)], [], ("xo", xi)))
            return outs

        final = []
        diag_plan = {}
        if ntl == 1:
            diag_plan[0] = list(range(NCC))
        else:
            per = -(-NCC // (ntl - 1))
            per = max(per, 2)
            cc_ = 0
            for t_ in range(0, ntl - 1):
                diag_plan[t_] = list(range(cc_, min(NCC, cc_ + per)))
                cc_ += per
        for c in range(NCC):
            build_diag(c)
        n_groups = 4 * ntl
        late_per_group = -(-len(late_pieces) // max(1, n_groups - 6))
        pp = prep(xp, 0, 0)
        early_casts()
        prep_pe(pp, 0)
        cur = 0
        for t in range(ntl):
            nxt = 1 - cur
            if t + 1 < ntl:
                pp = prep(xp, t + 1, nxt)
            else:
                pp = prep(xm, 0, nxt)
            rnn_stage(cur, main=False, mid_hook=lambda pp=pp, nxt=nxt: prep_pe(pp, nxt, all_act=True), late_n=late_per_group)
            if t == ntl - 1:
                conv_halo(cur)
            cur = nxt
        emit_late(len(late_pieces))
        P.cutoff = len(P.ops)
        mask_state()
        for t in range(ntl):
            nxt = 1 - cur
            step_pair, conv_finish = conv_stage(cur)
            fillers = [lambda c0=c0: step_pair(c0) for c0 in range(0, NCC, 2)]
            rnn_stage(cur, main=True, fillers=fillers, nfill=FILL_N)
            if t + 1 < ntl:
                pp = prep(xm, t + 1, nxt)
            while fillers:
                fillers.pop(0)()
            conv_finish()
            merge_stage(cur)
            if t + 1 < ntl:
                prep_pe(pp, nxt)
            final += out_stage(t)
            cur = nxt
        P.final_dma = final
        P.build(st)
    return nc


def _pack_params(norm_g, conv_dw_w, conv_dw_b, conv_ln_g, conv_ln_b, rnn_conv_w, rnn_conv_b, b_rg_a, b_rg_x, rg_lambda):
    par = np.zeros((128, NPAR), np.float32)
    par[:, P_G:P_G + 16] = norm_g.reshape(16, 128).T
    par[:, P_CW:P_CW + 496] = conv_dw_w.reshape(CW, NCC, 128).transpose(2, 1, 0).reshape(128, NCC * CW)
    par[:, P_CB:P_CB + 16] = conv_dw_b.reshape(16, 128).T
    par[:, P_LG:P_LG + 16] = conv_ln_g.reshape(16, 128).T
    par[:, P_LB:P_LB + 16] = conv_ln_b.reshape(16, 128).T
    par[:, P_RW:P_RW + 80] = rnn_conv_w.reshape(RW, NRC, 128).transpose(2, 1, 0).reshape(128, NRC * RW)
    par[:, P_RB:P_RB + 20] = rnn_conv_b.reshape(20, 128).T
    par[:, P_BA:P_BA + 20] = b_rg_a.reshape(20, 128).T
    par[:, P_BX:P_BX + 20] = b_rg_x.reshape(20, 128).T
    par[:, P_LAM:P_LAM + 20] = rg_lambda.reshape(20, 128).T
    return par


def _pack_gates(w_a, w_x):
    out = np.zeros((4, 128, NBLK, 128), np.float32)
    for gate, w in enumerate((w_a, w_x)):
        for g in range(4):
            dense = np.zeros((640, 640), np.float32)
            for hh in range(4):
                dense[hh * 160:(hh + 1) * 160, hh * 160:(hh + 1) * 160] = w[g * 4 + hh]
            for oc in range(5):
                for ic in GATE_ICS[oc]:
                    out[g, :, GATE_BLK[(gate, oc, ic)], :] = dense[ic * 128:(ic + 1) * 128, oc * 128:(oc + 1) * 128]
    return out.reshape(4 * 128, NBLK * 128)


_NC_CACHE = {}


def kernel(x, norm_g, w_in, conv_dw_w, conv_dw_b, conv_ln_g, conv_ln_b, w_conv_out,
           rnn_conv_w, rnn_conv_b, w_rg_a, b_rg_a, w_rg_x, b_rg_x, rg_lambda,
           w_rnn_out, w_out, final_norm_g):
    f = lambda a: np.ascontiguousarray(np.asarray(a, dtype=np.float32))
    x = f(x)
    par = _pack_params(f(norm_g)[0], f(conv_dw_w)[0], f(conv_dw_b)[0], f(conv_ln_g)[0], f(conv_ln_b)[0],
                       f(rnn_conv_w)[0], f(rnn_conv_b)[0], f(b_rg_a)[0], f(b_rg_x)[0], f(rg_lambda)[0])
    gwp = _pack_gates(f(w_rg_a)[0], f(w_rg_x)[0])
    fgb = np.ascontiguousarray(np.broadcast_to(f(final_norm_g)[None, :], (128, D)))
    shared = {"params": par, "fg": fgb, "w_in": f(w_in)[0], "w_co": f(w_conv_out)[0], "w_ro": f(w_rnn_out)[0],
              "w_o": f(w_out)[0], "gw": gwp}
    in_maps = []
    for c in range(8):
        b, half = c // 2, c % 2
        m = dict(shared)
        m["xm"] = np.ascontiguousarray(x[b, half * TOK:(half + 1) * TOK, :])
        m["xp"] = np.ascontiguousarray(x[b, 0:TOK, :])
        m["flag"] = np.full((128, 1), float(half), np.float32)
        in_maps.append(m)
    if "nc" not in _NC_CACHE:
        _NC_CACHE["nc"] = build_program()
    nc = _NC_CACHE["nc"]
    res = run_bass_kernel_spmd(nc, in_maps, core_ids=list(range(8)))
    out = np.empty((4, 2 * TOK, D), np.float32)
    for c in range(8):
        b, half = c // 2, c % 2
        out[b, half * TOK:(half + 1) * TOK, :] = res.results[c]["y"]
    return out
```

```python
from contextlib import ExitStack
import numpy as np
import concourse.bass as bass
import concourse.mybir as mybir
from concourse.bass_utils import run_bass_kernel_spmd

F32 = mybir.dt.float32
BF16 = mybir.dt.bfloat16
AF = mybir.ActivationFunctionType
ALU = mybir.AluOpType

D = 2048
DC = 2048
DR = 2560
DIN = 15360
NCC = 16
NRC = 20
KC = 16
CW = 31
RW = 4
T = 256
NDVE = 8
FILL_N = 2
TOK = 4096
NT = TOK // T
EPS = 1e-6
LN_EPS = 1e-5

O_CVAL, O_CGLU, O_CGATE, O_RX, O_RGATE, O_GCONV, O_GRNN = 0, 2048, 4096, 6144, 8704, 11264, 13312

P_G, P_CW, P_CB, P_LG, P_LB, P_RW, P_RB, P_BA, P_BX, P_LAM = 0, 16, 512, 528, 544, 560, 640, 660, 680, 700
NPAR = 720
Q_HC, Q_CC, Q_HBA, Q_HBX, Q_HLG, Q_HLB = 0, 20, 40, 60, 80, 96
NDER = 112

GATE_ICS = {0: (0, 1), 1: (0, 1, 2), 2: (1, 2, 3), 3: (2, 3, 4), 4: (3, 4)}
GATE_BLK = {}
_b = 0
for _gate in range(2):
    for _oc in range(5):
        for _ic in GATE_ICS[_oc]:
            GATE_BLK[(_gate, _oc, _ic)] = _b
            _b += 1
NBLK = _b

ENGS = ("pe", "act", "dve", "pool", "sp")


class Op:
    __slots__ = ("eng", "fn", "reads", "writes", "xreads", "dma_key", "deps", "milestone", "cnt", "idx", "sem", "cost", "table", "alldeps", "pos")


class Prog:
    def __init__(self, nc):
        self.nc = nc
        self.ops = []
        self.last_writer = {}
        self.readers = {}
        self.final_dma = []
        self.cutoff = None

    def op(self, eng, fn, reads=(), writes=(), xreads=(), dma_key=None, cost=0.3, table=None):
        o = Op()
        o.cost, o.table = cost, table
        o.eng, o.fn, o.dma_key = eng, fn, dma_key
        o.reads, o.writes, o.xreads = tuple(reads), tuple(writes), tuple(xreads)
        o.milestone, o.cnt, o.sem = False, None, None
        o.idx = len(self.ops)
        deps = set()
        for r in o.reads:
            lw = self.last_writer.get(r)
            if lw is not None:
                deps.add(lw)
        for w in o.writes + o.xreads:
            lw = self.last_writer.get(w)
            if lw is not None:
                deps.add(lw)
            rd = self.readers.get(w)
            if rd:
                deps.update(rd)
        for r in o.reads:
            self.readers.setdefault(r, []).append(o.idx)
        for w in o.writes + o.xreads:
            self.last_writer[w] = o.idx
            self.readers[w] = []
        deps.discard(o.idx)
        o.deps = deps
        self.ops.append(o)
        return o

    def schedule(self, window=4000, cutoff=None):
        ops = self.ops
        n = len(ops)
        ndeps = [len(o.alldeps) for o in ops]
        users = [[] for _ in range(n)]
        for o in ops:
            for d in o.alldeps:
                users[d].append(o.idx)
        ready = [0.0] * n
        finish = [0.0] * n
        cand = {e: [] for e in ENGS}
        for o in ops:
            if ndeps[o.idx] == 0:
                cand[o.eng].append(o.idx)
        t_eng = {e: 0.0 for e in ENGS}
        self.t0 = [0.0] * n
        order = {e: [] for e in ENGS}
        act_table = None
        dma_free = 0.0
        done = [False] * n
        low = 0
        nsched = 0
        if cutoff is None:
            cutoff = n
        po = {e: [o.idx for o in ops if o.eng == e and o.idx >= cutoff] for e in ENGS}
        po_ptr = {e: 0 for e in ENGS}
        while nsched < n:
            while low < n and done[low]:
                low += 1
            lim = low + window
            best = None
            best_key = None
            for e in ENGS:
                cl = cand[e]
                if not cl:
                    continue
                te = t_eng[e]
                nxt_po = po[e][po_ptr[e]] if po_ptr[e] < len(po[e]) else -1
                for i in cl:
                    if i > lim:
                        continue
                    if i >= cutoff and i != nxt_po:
                        continue
                    st = ready[i] if ready[i] > te else te
                    if e == "act":
                        tb = ops[i].table
                        if tb is not None and tb != act_table:
                            st += 1.4
                    key = (st, i)
                    if best_key is None or key < best_key:
                        best_key = key
                        best = (e, i)
            e, i = best
            o = ops[i]
            st = best_key[0]
            if o.dma_key is not None:
                issue = 0.08 if e == "sp" else 0.7
                t_eng[e] = st + issue
                xfer = o.cost / 250000.0
                s0 = max(st + issue, dma_free)
                dma_free = s0 + xfer
                finish[i] = s0 + xfer + 2.0
            else:
                if e == "act" and o.table is not None:
                    act_table = o.table
                t_eng[e] = st + o.cost
                finish[i] = st + o.cost + (0.15 if e == "pe" else 0.05)
            done[i] = True
            nsched += 1
            self.t0[i] = st
            if i >= cutoff:
                po_ptr[e] += 1
            cand[e].remove(i)
            o.pos = len(order[e])
            order[e].append(i)
            for u in users[i]:
                ndeps[u] -= 1
                if finish[i] > ready[u]:
                    ready[u] = finish[i]
                if ndeps[u] == 0:
                    cand[ops[u].eng].append(u)
        self.sim_time = max(finish) if n else 0.0
        return order

    def build(self, stack, reorder=True):
        nc = self.nc
        ops = self.ops
        for o in ops:
            o.alldeps = set(o.deps)
        if reorder:
            order = self.schedule(cutoff=self.cutoff)
        else:
            order = {e: [o.idx for o in ops if o.eng == e] for e in ENGS}
            for e in ENGS:
                for p_, i in enumerate(order[e]):
                    ops[i].pos = p_
        for o in ops:
            best = {}
            for d in o.alldeps:
                p = ops[d]
                if p.dma_key is None and o.dma_key is None and p.eng == o.eng:
                    if p.eng == "pe":
                        continue
                    touched = set(o.reads) | set(o.writes) | set(o.xreads)
                    war = (set(p.reads) | set(p.xreads)) & set(o.writes)
                    if not (set(p.writes) & touched) and not war:
                        continue
                k = p.dma_key if p.dma_key is not None else p.eng
                if k not in best or ops[best[k]].pos < p.pos:
                    best[k] = d
            o.deps = sorted(best.values())
            for d in o.deps:
                ops[d].milestone = True
        for o in ops:
            if o.dma_key is not None:
                o.milestone = True
        eng_sem, eng_cnt = {}, {}
        for e in ENGS:
            eng_sem[e] = stack.enter_context(nc.semaphore("sem_" + e))
            eng_cnt[e] = 0
        dma_sem, dma_cnt = {}, {}
        for e in ENGS:
            for i in order[e]:
                o = ops[i]
                if o.dma_key is None and o.milestone:
                    eng_cnt[o.eng] += 1
                    o.sem, o.cnt = eng_sem[o.eng], eng_cnt[o.eng]
        for o in ops:
            if o.dma_key is not None:
                if o.dma_key not in dma_sem:
                    dma_sem[o.dma_key] = stack.enter_context(nc.semaphore("dsem_%d" % len(dma_sem)))
                    dma_cnt[o.dma_key] = 0
                dma_cnt[o.dma_key] += 16
                o.sem, o.cnt = dma_sem[o.dma_key], dma_cnt[o.dma_key]
        for o in ops:
            if o.dma_key is not None and o.dma_key[0] == "castfull":
                o.cnt = dma_cnt[o.dma_key]
        self.n_sems = len(dma_sem) + len(ENGS)
        per_eng = {e: [ops[i] for i in order[e]] for e in ENGS}
        block = stack.enter_context(nc.Block())
        final = [(o.sem, o.cnt) for o in self.final_dma]

        def emit(e, engobj):
            known = {}
            for o in per_eng[e]:
                need = {}
                for d in o.deps:
                    p = ops[d]
                    k = id(p.sem)
                    if k not in need or need[k][1] < p.cnt:
                        need[k] = (p.sem, p.cnt)
                for k, (sem, cnt) in need.items():
                    if known.get(k, 0) >= cnt:
                        continue
                    engobj.wait_ge(sem, cnt)
                    known[k] = cnt
                ins = o.fn(engobj)
                if o.milestone:
                    ins.then_inc(o.sem, 16 if o.dma_key is not None else 1)
            if e == "sp":
                for sem, cnt in final:
                    engobj.wait_ge(sem, cnt)

        @block.tensor
        def _(eng):
            emit("pe", eng)

        @block.scalar
        def _(eng):
            emit("act", eng)

        @block.vector
        def _(eng):
            emit("dve", eng)

        @block.gpsimd
        def _(eng):
            emit("pool", eng)

        @block.sync
        def _(eng):
            emit("sp", eng)


class Ring:
    def __init__(self, items):
        self.items = items
        self.i = 0

    def next(self):
        k = self.i % len(self.items)
        self.i += 1
        return k, self.items[k]


def build_program(tok=TOK):
    nc = bass.Bass("TRN2", target_bir_lowering=False)
    ntl = tok // T
    dt = nc.dram_tensor
    xm = dt("xm", [tok, D], F32, kind="ExternalInput").ap()
    xp = dt("xp", [tok, D], F32, kind="ExternalInput").ap()
    flag_d = dt("flag", [128, 1], F32, kind="ExternalInput").ap()
    par_d = dt("params", [128, NPAR], F32, kind="ExternalInput").ap()
    fg_d = dt("fg", [128, D], F32, kind="ExternalInput").ap()
    w_in = dt("w_in", [D, DIN], F32, kind="ExternalInput").ap()
    w_co = dt("w_co", [DC, D], F32, kind="ExternalInput").ap()
    w_ro = dt("w_ro", [DR, D], F32, kind="ExternalInput").ap()
    w_o = dt("w_o", [D, D], F32, kind="ExternalInput").ap()
    gw = dt("gw", [4 * 128, NBLK * 128], F32, kind="ExternalInput").ap()
    y = dt("y", [tok, D], F32, kind="ExternalOutput").ap()
    def _grp_tiles(base):
        tl = []
        for g in range(4):
            tl += [(base + g * 640, 256), (base + g * 640 + 256, 256), (base + g * 640 + 512, 128)]
        return tl
    FAM = {}
    for _nm, _base in (("rx", O_RX), ("rgate", O_RGATE)):
        FAM[_nm] = (dt(_nm + "_b", [12 * 128, KC * 256], BF16, kind="Internal").ap(), w_in, KC, _grp_tiles(_base))
    for _nm, _base in (("cglu", O_CGLU), ("cval", O_CVAL), ("cgate", O_CGATE), ("gconv", O_GCONV), ("grnn", O_GRNN)):
        FAM[_nm] = (dt(_nm + "_b", [8 * 128, KC * 256], BF16, kind="Internal").ap(), w_in, KC, [(_base + i * 256, 256) for i in range(8)])
    FAM["wro"] = (dt("wro_b", [8 * 128, NRC * 256], BF16, kind="Internal").ap(), w_ro, NRC, [(i * 256, 256) for i in range(8)])
    FAM["wco"] = (dt("wco_b", [8 * 128, NCC * 256], BF16, kind="Internal").ap(), w_co, NCC, [(i * 256, 256) for i in range(8)])
    FAM["wo"] = (dt("wo_b", [8 * 128, KC * 256], BF16, kind="Internal").ap(), w_o, KC, [(i * 256, 256) for i in range(8)])
    gw_b = dt("gw_b", [4 * 128, NBLK * 128], BF16, kind="Internal").ap()
    dg_b = dt("dg_b", [NCC * 128, CW * 128], BF16, kind="Internal").ap()

    with ExitStack() as st:
        def sb(name, shape, dtype):
            return st.enter_context(nc.sbuf_tensor(name, shape, dtype))

        P = Prog(nc)

        par = sb("par", [128, NPAR], F32)
        der = sb("der", [128, NDER], F32)
        tmp20 = sb("tmp20", [128, 20], F32)
        hwt = sb("hwt", [128, NCC * NDVE], F32)
        flag = sb("flag_sb", [128, 1], F32)
        fg = sb("fg_sb", [128, D], F32)
        identf = sb("identf", [128, 128], F32)
        ident = sb("ident", [128, 128], BF16)
        onesf = sb("onesf", [128, 128], F32)
        NWS = 3
        wslots = [sb("wslot%d" % i, [128, 20 * 256], BF16) for i in range(NWS)]
        wring = Ring(wslots)
        dgslots = [sb("dgslot%d" % i, [128, CW, 128], BF16) for i in range(2)]
        dgring = Ring(dgslots)
        hT = [sb("hT%d" % i, [128, KC, T], BF16) for i in range(2)]
        hs = Ring([sb("hs%d" % i, [128, D], BF16) for i in range(2)])
        xring = Ring([sb("xblk%d" % i, [128, D], F32) for i in range(3)])
        junk = sb("junk", [128, D], BF16)
        ssq = sb("ssq", [128, 2], F32)
        rs = sb("rs", [128, 2], F32)
        ssq2 = sb("ssq2", [128, 2], F32)
        rs2 = sb("rs2", [128, 2], F32)
        cv = sb("cv", [128, NCC, T], F32)
        ug = sb("ug", [128, NCC, T], BF16)
        hg = sb("hg", [128, NRC, T], BF16)
        mg = cv[:].rearrange("p c t -> p (c t)")[:, 0:KC * T // 2].bitcast(BF16).rearrange("p (k t) -> p k t", t=T)
        ubuf = Ring([sb("ubuf%d" % i, [128, 30 + T], BF16) for i in range(3)])
        uh = sb("uh", [128, NCC, 30], BF16)
        rxr = Ring([sb("rx%d" % i, [128, 3 + T], F32) for i in range(2)])
        rxh = sb("rxh", [128, NRC, 3], F32)
        state = sb("state", [128, NRC], F32)
        vbuf = [sb("v%d" % i, [128, T], F32) for i in range(10)]
        vbb = [sb("vb%d" % i, [128, T], BF16) for i in range(10)]
        NG = 6
        bufA = Ring([sb("bA%d" % i, [128, T], F32) for i in range(NG)])
        bufB = Ring([sb("bB%d" % i, [128, T], F32) for i in range(NG)])
        bufC = Ring([sb("bC%d" % i, [128, T], F32) for i in range(NG)])
        bufD = Ring([sb("bD%d" % i, [128, T], F32) for i in range(2)])
        bufE = Ring([sb("bE%d" % i, [128, T], F32) for i in range(5)])
        tbuf = Ring([sb("tb%d" % i, [128, T], F32) for i in range(2)])
        sqt = Ring([sb("sq%d" % i, [128, T], F32) for i in range(2)])
        acc1 = sb("acc1", [128, T], F32)
        acc2 = sb("acc2", [128, T], F32)
        zbuf = Ring([sb("z%d" % i, [128, T], F32) for i in range(2)])
        t1buf = Ring([sb("t1%d" % i, [128, T], F32) for i in range(1)])
        t2buf = Ring([sb("t2%d" % i, [128, T], F32) for i in range(2)])
        lnM = sb("lnM", [128, T], F32)
        lnR = sb("lnR", [128, T], F32)
        lnN = sb("lnN", [128, T], F32)
        mt1 = Ring([sb("mt1%d" % i, [128, T], F32) for i in range(2)])
        mt2 = Ring([sb("mt2%d" % i, [128, T], F32) for i in range(2)])
        banks = [st.enter_context(nc.psum_tensor("bank%d" % i, [128, 512], F32)) for i in range(8)]
        bring = Ring(banks[0:7])
        pvb = banks[7]

        def newbank():
            k, b = bring.next()
            return ("ps", k), b

        def nfree(ap):
            n = 1
            for d in ap.shape[1:]:
                n *= int(d)
            return n

        def dma(eng, out, in_, reads, writes, key):
            nb = nfree(out) * int(out.shape[0]) * (2 if out.dtype == BF16 else 4)
            if in_.dtype != out.dtype:
                nb *= 3
            return P.op(eng, lambda e: e.dma_start(out=out, in_=in_), reads=reads, writes=writes, dma_key=key, cost=float(nb))

        TABLE = {AF.Tanh: "exp", AF.Exp: "exp", AF.Sqrt: "sqrt", AF.Ln: "ln"}

        def act(out, in_, func, reads=(), writes=(), xreads=(), **kw):
            n = nfree(out)
            cost = 0.2 + n / 1200.0 + (0.09 if not isinstance(kw.get("scale", 1.0), float) else 0.0) + (0.09 if not isinstance(kw.get("bias", 0.0), float) else 0.0)
            return P.op("act", lambda e: e.activation(out=out, in_=in_, func=func, **kw), reads=reads, writes=writes, xreads=xreads,
                        cost=cost, table=TABLE.get(func))

        def dve(fn, reads=(), writes=(), xreads=(), n=T, k=2.0):
            if xreads and k == 2.0:
                k = 1.0
            return P.op("dve", fn, reads=reads, writes=writes, xreads=xreads, cost=0.08 + k * n / 960.0)

        def pool(fn, reads=(), writes=(), n=T):
            return P.op("pool", fn, reads=reads, writes=writes, cost=0.35 + n / 350.0)

        def col(t, c):
            return t[:, c:c + 1]

        dma("sp", par[:], par_d, [], ["par"], "ld_par")
        dma("sp", flag[:], flag_d, [], ["flag"], "ld_flag")
        dma("sp", fg[:], fg_d, [], ["fg"], "ld_fg")
        pool(lambda e: e.memset(identf[:], 0.0), writes=["identf"])
        pool(lambda e: e.affine_select(out=identf[:], in_=identf[:], pattern=[[-1, 128]], compare_op=ALU.not_equal,
                                       fill=1.0, base=0, channel_multiplier=1), reads=["identf"], writes=["identf"])
        pool(lambda e: e.memset(onesf[:], 1.0), writes=["onesf"])
        pool(lambda e: e.memset(state[:], 0.0), writes=[("state", c) for c in range(NRC)])
        pool(lambda e: e.memset(rxh[:], 0.0), writes=[("rxh", c) for c in range(NRC)])
        pool(lambda e: e.memset(uh[:], 0.0), writes=[("uh", c) for c in range(NCC)])
        dve(lambda e: e.tensor_copy(out=ident[:], in_=identf[:]), reads=["identf"], writes=["ident"])
        act(tmp20[:], par[:, P_LAM:P_LAM + 20], AF.Exp, reads=["par"], writes=["tmp20"], scale=-1.0)
        act(tmp20[:], tmp20[:], AF.Ln, reads=["tmp20"], writes=["tmp20"], bias=1.0)
        dve(lambda e: e.tensor_scalar(out=der[:, Q_HC:Q_HC + 20], in0=tmp20[:], scalar1=-4.0, scalar2=None, op0=ALU.mult),
            reads=["tmp20"], writes=["der_hc"])
        dve(lambda e: e.tensor_scalar(out=der[:, Q_CC:Q_CC + 20], in0=tmp20[:], scalar1=-8.0, scalar2=None, op0=ALU.mult),
            reads=["tmp20"], writes=["der_cc"])
        for (q, p_, n) in ((Q_HBA, P_BA, 20), (Q_HBX, P_BX, 20), (Q_HLG, P_LG, 16), (Q_HLB, P_LB, 16)):
            dve(lambda e, q=q, p_=p_, n=n: e.tensor_scalar(out=der[:, q:q + n], in0=par[:, p_:p_ + n], scalar1=0.5,
                                                           scalar2=None, op0=ALU.mult), reads=["par"], writes=[("der", q)])
        DER = ["der_hc", "der_cc", ("der", Q_HBA), ("der", Q_HBX), ("der", Q_HLG), ("der", Q_HLB)]
        for c in range(NCC):
            dve(lambda e, c=c: e.tensor_scalar(out=hwt[:, c * NDVE:(c + 1) * NDVE], in0=par[:, P_CW + c * CW + CW - NDVE:P_CW + (c + 1) * CW],
                                               scalar1=0.5, scalar2=None, op0=ALU.mult), reads=["par"], writes=["hwt"], n=NDVE, k=1)

        pace = {"res": None}

        def cast_grp(name, i):
            return (i // 3) if name == "rx" else (i if name == "gw" else 0)

        def cast_piece(name, i, dst, src):
            grp = cast_grp(name, i)
            dma("pool", dst, src, ([pace["res"]] if pace["res"] is not None else []), [("wb", name, i)], ("castfull", name, grp))

        def fam_piece(name, j):
            scr, src, nk, tiles = FAM[name]
            c0, w = tiles[j]
            dst = scr[j * 128:(j + 1) * 128, 0:nk * w].rearrange("p (k n) -> p k n", n=w)
            cast_piece(name, j, dst, src[:, c0:c0 + w].rearrange("(k p) n -> p k n", p=128))

        def early_casts():
            for g in range(4):
                for j in range(3):
                    fam_piece("rx", 3 * g + j)
                cast_piece("gw", g, gw_b[g * 128:(g + 1) * 128, :].rearrange("p (a n) -> p a n", n=256),
                           gw[g * 128:(g + 1) * 128, :].rearrange("p (a n) -> p a n", n=256))
        late_pieces = []
        for i in range(8):
            late_pieces.append(lambda i=i: fam_piece("cglu", i))
            late_pieces.append(lambda i=i: fam_piece("cval", i))
        for i in range(12):
            late_pieces.append(lambda i=i: fam_piece("rgate", i))
        for i in range(8):
            late_pieces.append(lambda i=i: fam_piece("cgate", i))
        for i in range(8):
            late_pieces.append(lambda i=i: fam_piece("gconv", i))
            late_pieces.append(lambda i=i: fam_piece("grnn", i))
            late_pieces.append(lambda i=i: fam_piece("wro", i))
            late_pieces.append(lambda i=i: fam_piece("wco", i))
        for i in range(8):
            late_pieces.append(lambda i=i: fam_piece("wo", i))

        def emit_late(n):
            for _ in range(n):
                if late_pieces:
                    late_pieces.pop(0)()

        def build_diag(c):
            s_, slot = dgring.next()
            for k in range(CW):
                dve(lambda e, slot=slot, k=k, c=c: e.tensor_scalar(out=slot[:, k, :], in0=identf[:],
                                                                   scalar1=col(par, P_CW + c * CW + k), scalar2=0.5,
                                                                   op0=ALU.mult, op1=ALU.mult),
                    reads=["identf", "par"], writes=[("dgslot", s_)], n=128, k=1)
            dma("sp", dg_b[c * 128:(c + 1) * 128, :], slot[:].rearrange("p k n -> p (k n)"), [("dgslot", s_)], [("dgb", c)], ("dgout", s_))

        slot_gen = {}

        def chk(res):
            assert slot_gen[res[0:2]] == res[2], "stale weight tile %r" % (res,)
            return res[0:2]

        def load_tile(name, j):
            scr, src, nk, tiles = FAM[name]
            c0, w = tiles[j]
            s, slot = wring.next()
            slot_gen[("wslot", s)] = slot_gen.get(("wslot", s), 0) + 1
            view = slot[:, 0:nk * w].rearrange("p (k n) -> p k n", n=w)
            members = [("wb", name, i) for i in range(len(tiles)) if cast_grp(name, i) == cast_grp(name, j)]
            dma("sp", slot[:, 0:nk * w], scr[j * 128:(j + 1) * 128, 0:nk * w], members, [("wslot", s)], ("wslot", s))
            return ("wslot", s, slot_gen[("wslot", s)]), view

        def load_win(col0, wbname, ncols=256):
            tiles = FAM[wbname][3]
            j = [i for i, (c0, w) in enumerate(tiles) if c0 == col0]
            assert len(j) == 1 and tiles[j[0]][1] == ncols, (wbname, col0, ncols)
            return load_tile(wbname, j[0])

        def load_gate(g):
            s, slot = wring.next()
            slot_gen[("wslot", s)] = slot_gen.get(("wslot", s), 0) + 1
            view = slot[:, 0:NBLK * 128].rearrange("p (k n) -> p k n", n=128)
            dma("sp", slot[:, 0:NBLK * 128], gw_b[g * 128:(g + 1) * 128, :], [("wb", "gw", g)], [("wslot", s)], ("wslot", s))
            return ("wslot", s, slot_gen[("wslot", s)]), view

        def load_dg(c):
            s, slot = dgring.next()
            dma("sp", slot[:].rearrange("p k n -> p (k n)"), dg_b[c * 128:(c + 1) * 128, :], [("dgb", c)], [("dgslot", s)], ("dgslot", s))
            return ("dgslot", s), slot

        def mm_group(bank_res, out_ap, pairs, extra_reads):
            n = len(pairs)
            pairs = [(l, r, [chk(x) if (isinstance(x, tuple) and x[0] == "wslot") else x for x in rd]) for (l, r, rd) in pairs]
            for i, (l, r, rd) in enumerate(pairs):
                ncol = nfree(r)
                P.op("pe", lambda e, l=l, r=r, i=i: e.matmul(out_ap, lhsT=l, rhs=r, start=(i == 0), stop=(i == n - 1)),
                     reads=list(rd) + list(extra_reads), writes=[bank_res], cost=max(64, ncol) / 1900.0 * (4.0 if r.dtype == F32 else 1.0))

        def prep(src, tile_idx, buf):
            r0 = tile_idx * T
            blocks = []
            for tb in range(2):
                xi, xb = xring.next()
                dma("pool", xb[:], src[r0 + tb * 128:r0 + (tb + 1) * 128, :], [], [("x", xi)], ("x", xi))
                hi, hsb = hs.next()
                act(junk[:], xb[:], AF.Square, reads=[("x", xi)], writes=["junk", ("ssq", tb)], accum_out=col(ssq, tb))
                blocks.append((xi, xb, hi, hsb))
            dve(lambda e: e.tensor_scalar(out=rs[:], in0=ssq[:], scalar1=1.0 / D, scalar2=EPS, op0=ALU.mult, op1=ALU.add),
                reads=[("ssq", 0), ("ssq", 1)], writes=["rs"], n=2, k=1)
            act(rs[:], rs[:], AF.Sqrt, reads=["rs"], writes=["rs"])
            dve(lambda e: e.reciprocal(out=rs[:], in_=rs[:]), reads=["rs"], writes=["rs"], n=2, k=8)
            pe_parts = []
            for tb, (xi, xb, hi, hsb) in enumerate(blocks):
                act(hsb[:], xb[:], AF.Copy, reads=[("x", xi), "rs"], writes=[("hs", hi)], scale=col(rs, tb))
                pe_parts.append((tb, hi, hsb))
            return pe_parts

        def prep_pe(pe_parts, buf, all_act=False):
            for (tb, hi, hsb) in pe_parts:
                for q in range(4):
                    bres, bank = newbank()
                    bb = bank[:].bitcast(BF16)
                    for i in range(4):
                        fc = q * 4 + i
                        P.op("pe", lambda e, bb=bb, i=i, fc=fc, hsb=hsb: e.transpose(out=bb[:, i * 128:(i + 1) * 128],
                                                                                     in_=hsb[:, fc * 128:(fc + 1) * 128],
                                                                                     identity=ident[:]),
                             reads=[("hs", hi), "ident"], writes=[bres], cost=0.08)
                    for i in range(4):
                        fc = q * 4 + i
                        o_ap = hT[buf][:, fc, tb * 128:(tb + 1) * 128]
                        i_ap = bb[:, i * 128:(i + 1) * 128]
                        if q % 2 == 0 and not all_act:
                            dve(lambda e, o_ap=o_ap, i_ap=i_ap, fc=fc: e.tensor_scalar(out=o_ap, in0=i_ap, scalar1=col(par, P_G + fc),
                                                                                       scalar2=None, op0=ALU.mult),
                                reads=["par"], writes=[("hT", buf, fc)], xreads=[bres], n=128, k=1)
                        else:
                            act(o_ap, i_ap, AF.Copy, reads=["par"], writes=[("hT", buf, fc)], xreads=[bres], scale=col(par, P_G + fc))

        def hT_reads(buf):
            return [("hT", buf, fc) for fc in range(KC)]

        def rnn_stage(buf, main, mid_hook=None, late_n=0, fillers=None, nfill=2):
            hr = hT_reads(buf)
            wt = {}

            pend_cast = []

            def flush_cast():
                while pend_cast:
                    vi = pend_cast.pop(0)
                    if main:
                        act(vbb[vi][:], vbuf[vi][:], AF.Copy, reads=[("v", vi)], writes=[("vb", vi)])
                    else:
                        pool(lambda e, vi=vi: e.tensor_copy(out=vbb[vi][:], in_=vbuf[vi][:]), reads=[("v", vi)], writes=[("vb", vi)])

            def rx_chunk(c):
                lc = c % 5
                if lc % 2 == 0:
                    wt["rx"] = load_win(O_RX + c * 128, "rx", 128 if lc == 4 else 256)
                wres, wv = wt["rx"]
                bres, bank = newbank()
                o = (lc % 2) * 128
                mm_group(bres, bank[:, 0:T], [(wv[:, k, o:o + 128], hT[buf][:, k, :], [wres, ("hT", buf, k)]) for k in range(KC)], [])
                ri, rx = rxr.next()
                pace["n"] = pace.get("n", 0) + 1
                act(rx[:, 3:3 + T], bank[:, 0:T], AF.Identity, xreads=[bres], writes=[("rx", ri), ("pace", pace["n"])])
                dve(lambda e, rx=rx, c=c: e.tensor_copy(out=rx[:, 0:3], in_=rxh[:, c, :]), reads=[("rxh", c)], writes=[("rxhalo", ri)], n=3, k=1)
                flush_cast()
                vi = c % 10
                v = vbuf[vi]
                dve(lambda e, rx=rx, c=c: e.tensor_scalar(out=pvb[:, 0:T], in0=rx[:, 3:3 + T], scalar1=col(par, P_RW + c * RW + 3),
                                                          scalar2=col(par, P_RB + c), op0=ALU.mult, op1=ALU.add),
                    reads=[("rx", ri), "par"], writes=["pv"], k=1)
                for k in range(3):
                    last = (k == 2)
                    dve(lambda e, rx=rx, v=v, c=c, k=k, last=last: e.scalar_tensor_tensor(out=(v[:] if last else pvb[:, 0:T]), in0=rx[:, k:k + T],
                                                                                          scalar=col(par, P_RW + c * RW + k),
                                                                                          in1=pvb[:, 0:T], op0=ALU.mult, op1=ALU.add),
                        reads=[("rx", ri), ("rxhalo", ri), "par", "pv"], writes=([("v", vi)] if last else ["pv"]), k=1)
                dve(lambda e, rx=rx, c=c: e.tensor_copy(out=rxh[:, c, :], in_=rx[:, T:T + 3]), reads=[("rx", ri)], writes=[("rxh", c)], n=3, k=1)
                pend_cast.append(vi)

            def gates_block(g):
                gres, gv = load_gate(g)
                items = []
                for oc in range(5):
                    c = g * 5 + oc
                    rec = {"c": c}
                    for gate in range(2):
                        bres, bank = newbank()
                        pairs = []
                        for ic in GATE_ICS[oc]:
                            vi = (g * 5 + ic) % 10
                            pairs.append((gv[:, GATE_BLK[(gate, oc, ic)], :], vbb[vi][:], [gres, ("vb", vi)]))
                        mm_group(bres, bank[:, 0:T], pairs, [])
                        rec[gate] = (bres, bank)
                    ai, A = bufA.next()
                    bi, B = bufB.next()
                    ci, C = bufC.next()
                    rec.update(A=A, ai=ai, B=B, bi=bi, C=C, ci=ci)
                    bres, bank = rec[0]
                    act(A[:], bank[:, 0:T], AF.Tanh, xreads=[bres], reads=DER, writes=[("A", ai)], scale=0.5, bias=col(der, Q_HBA + c))
                    bres, bank = rec[1]
                    act(C[:], bank[:, 0:T], AF.Tanh, xreads=[bres], reads=DER, writes=[("C", ci)], scale=0.5, bias=col(der, Q_HBX + c))
                    act(B[:], A[:], AF.Exp, reads=[("A", ai)] + DER, writes=[("B", bi)], scale=col(der, Q_HC + c), bias=col(der, Q_HC + c))
                    pool(lambda e, A=A, B=B: e.tensor_tensor(out=A[:], in0=B[:], in1=B[:], op=ALU.mult), reads=[("B", bi)], writes=[("A", ai)])
                    vi_ = c % 10
                    if main:
                        dve(lambda e, C=C, vv_=vbuf[vi_]: e.scalar_tensor_tensor(out=C[:], in0=C[:], scalar=1.0, in1=vv_[:], op0=ALU.add, op1=ALU.mult),
                            reads=[("C", ci), ("v", vi_)], writes=[("C", ci)])
                    else:
                        pool(lambda e, C=C: e.tensor_scalar(out=C[:], in0=C[:], scalar1=1.0, scalar2=1.0, op0=ALU.add, op1=ALU.mult),
                             reads=[("C", ci)], writes=[("C", ci)])
                        pool(lambda e, C=C, vv_=vbuf[vi_]: e.tensor_tensor(out=C[:], in0=C[:], in1=vv_[:], op=ALU.mult),
                             reads=[("C", ci), ("v", vi_)], writes=[("C", ci)])
                    items.append(rec)
                if main:
                    for oc in range(5):
                        rec = items[oc]
                        c = rec["c"]
                        lc = c % 5
                        if lc % 2 == 0:
                            wt["rg"] = load_win(O_RGATE + c * 128, "rgate", 128 if lc == 4 else 256)
                        wres, wv = wt["rg"]
                        bres, bank = newbank()
                        o = (lc % 2) * 128
                        mm_group(bres, bank[:, 0:T], [(wv[:, k, o:o + 128], hT[buf][:, k, :], [wres, ("hT", buf, k)]) for k in range(KC)], [])
                        ei, E = bufE.next()
                        act(E[:], bank[:, 0:T], AF.Tanh, xreads=[bres], writes=[("E", ei)], scale=0.5)
                        dve(lambda e, E=E, bank=bank: e.scalar_tensor_tensor(out=E[:], in0=E[:], scalar=1.0, in1=bank[:, 0:T],
                                                                             op0=ALU.add, op1=ALU.mult),
                            reads=[("E", ei)], writes=[("E", ei)], xreads=[bres])
                        rec.update(E=E, ei=ei)
                for rec in items:
                    A, ai = rec["A"], rec["ai"]
                    act(A[:], A[:], AF.Sqrt, reads=[("A", ai)], writes=[("A", ai)], scale=-1.0, bias=1.0)
                for rec in items:
                    c = rec["c"]
                    A, ai, B, bi, C, ci = rec["A"], rec["ai"], rec["B"], rec["bi"], rec["C"], rec["ci"]
                    vi = c % 10
                    v = vbuf[vi]
                    dve(lambda e, C=C, A=A: e.scalar_tensor_tensor(out=C[:], in0=A[:], scalar=0.0, in1=C[:], op0=ALU.max, op1=ALU.mult),
                        reads=[("C", ci), ("A", ai)], writes=[("C", ci)])
                    di, Dd = bufD.next()
                    dve(lambda e, Dd=Dd, B=B, C=C, c=c: e.tensor_tensor_scan(out=Dd[:], data0=B[:], data1=C[:], initial=col(state, c),
                                                                             op0=ALU.mult, op1=ALU.add),
                        reads=[("B", bi), ("C", ci), ("state", c)], writes=[("D", di)])
                    dve(lambda e, Dd=Dd, c=c: e.tensor_copy(out=col(state, c), in_=Dd[:, T - 1:T]), reads=[("D", di)], writes=[("state", c)], n=1, k=1)
                    if main:
                        E, ei = rec["E"], rec["ei"]
                        dve(lambda e, Dd=Dd, E=E, c=c: e.tensor_tensor(out=hg[:, c, :], in0=Dd[:], in1=E[:], op=ALU.mult),
                            reads=[("D", di), ("E", ei)], writes=[("hg", c)])

            for g in range(4):
                for c in range(g * 5, g * 5 + 5):
                    rx_chunk(c)
                flush_cast()
                if late_n:
                    pace["res"] = ("pace", pace["n"])
                    emit_late(late_n)
                if g == 2 and mid_hook is not None:
                    mid_hook()
                if fillers:
                    for _ in range(nfill):
                        if fillers:
                            fillers.pop(0)()
                if g >= 1:
                    gates_block(g - 1)
            gates_block(3)

        def conv_halo(buf):
            hr = hT_reads(buf)
            wt = {}
            for c in range(NCC):
                if c % 2 == 0:
                    wt["g"] = load_win(O_CGLU + c * 128, "cglu")
                    wt["v"] = load_win(O_CVAL + c * 128, "cval")
                o = (c % 2) * 128
                gres, gv = wt["g"]
                vres, vv = wt["v"]
                b1, bk1 = newbank()
                mm_group(b1, bk1[:, 0:32], [(gv[:, k, o:o + 128], hT[buf][:, k, T - 32:T], [gres, ("hT", buf, k)]) for k in range(KC)], [])
                b2, bk2 = newbank()
                mm_group(b2, bk2[:, 0:32], [(vv[:, k, o:o + 128], hT[buf][:, k, T - 32:T], [vres, ("hT", buf, k)]) for k in range(KC)], [])
                ti, tt = tbuf.next()
                act(tt[:, 0:32], bk1[:, 0:32], AF.Tanh, xreads=[b1], writes=[("t", ti)], scale=0.5)
                dve(lambda e, tt=tt, bk2=bk2, c=c: e.scalar_tensor_tensor(out=uh[:, c, :], in0=tt[:, 2:32], scalar=1.0, in1=bk2[:, 2:32],
                                                                          op0=ALU.add, op1=ALU.mult),
                    reads=[("t", ti)], writes=[("uh", c)], xreads=[b2])

        def mask_state():
            dve(lambda e: e.tensor_scalar(out=state[:], in0=state[:], scalar1=flag[:, 0:1], scalar2=None, op0=ALU.mult),
                reads=["flag"] + [("state", c) for c in range(NRC)], writes=[("state", c) for c in range(NRC)])
            dve(lambda e: e.tensor_scalar(out=rxh[:].rearrange("p c k -> p (c k)"), in0=rxh[:].rearrange("p c k -> p (c k)"),
                                          scalar1=flag[:, 0:1], scalar2=None, op0=ALU.mult),
                reads=["flag"] + [("rxh", c) for c in range(NRC)], writes=[("rxh", c) for c in range(NRC)])
            dve(lambda e: e.tensor_scalar(out=uh[:].rearrange("p c k -> p (c k)"), in0=uh[:].rearrange("p c k -> p (c k)"),
                                          scalar1=flag[:, 0:1], scalar2=None, op0=ALU.mult),
                reads=["flag"] + [("uh", c) for c in range(NCC)], writes=[("uh", c) for c in range(NCC)])

        def conv_stage(buf):
            hr = hT_reads(buf)
            wt = {}
            pend = []

            def gv_chunk(c):
                if c % 2 == 0:
                    wt["g"] = load_win(O_CGLU + c * 128, "cglu")
                    wt["v"] = load_win(O_CVAL + c * 128, "cval")
                o = (c % 2) * 128
                gres, gv_ = wt["g"]
                vres, vv = wt["v"]
                b1, bk1 = newbank()
                mm_group(b1, bk1[:, 0:T], [(gv_[:, k, o:o + 128], hT[buf][:, k, :], [gres, ("hT", buf, k)]) for k in range(KC)], [])
                b2, bk2 = newbank()
                mm_group(b2, bk2[:, 0:T], [(vv[:, k, o:o + 128], hT[buf][:, k, :], [vres, ("hT", buf, k)]) for k in range(KC)], [])
                ti, tt = tbuf.next()
                act(tt[:], bk1[:, 0:T], AF.Tanh, xreads=[b1], writes=[("t", ti)], scale=0.5)
                ui, ub = ubuf.next()
                dve(lambda e, tt=tt, bk2=bk2, ub=ub: e.scalar_tensor_tensor(out=ub[:, 30:30 + T], in0=tt[:], scalar=1.0, in1=bk2[:, 0:T],
                                                                            op0=ALU.add, op1=ALU.mult),
                    reads=[("t", ti)], writes=[("u", ui)], xreads=[b2])
                pool(lambda e, ub=ub, c=c: e.tensor_copy(out=ub[:, 0:30], in_=uh[:, c, :]), reads=[("uh", c)], writes=[("uhalo", ui)])
                pool(lambda e, ub=ub, c=c: e.tensor_copy(out=uh[:, c, :], in_=ub[:, T:T + 30]), reads=[("u", ui)], writes=[("uh", c)])
                return ui, ub

            def conv_chunk(c, ui, ub):
                dres, dgv = load_dg(c)
                b3, bk3 = newbank()
                mm_group(b3, bk3[:, 0:T], [(dgv[:, k, :], ub[:, k:k + T], [dres]) for k in range(CW - NDVE)], [("u", ui), ("uhalo", ui)])
                for j in range(NDVE):
                    k = CW - NDVE + j
                    dve(lambda e, ub=ub, bk3=bk3, c=c, j=j, k=k: e.scalar_tensor_tensor(out=bk3[:, 0:T], in0=ub[:, k:k + T], scalar=col(hwt, c * NDVE + j),
                                                                                        in1=bk3[:, 0:T], op0=ALU.mult, op1=ALU.add),
                        reads=[("u", ui), ("uhalo", ui), "hwt"], writes=[b3], k=2)
                act(cv[:, c, :], bk3[:, 0:T], AF.Identity, xreads=[b3], reads=["par"], writes=[("cv", c), ("mg", 2 * c), ("mg", 2 * c + 1)],
                    bias=col(par, P_CB + c))
                si, sq = sqt.next()
                act(sq[:], cv[:, c, :], AF.Square, reads=[("cv", c)], writes=[("sq", si)])
                if c == 0:
                    pool(lambda e: e.tensor_copy(out=acc1[:], in_=cv[:, 0, :]), reads=[("cv", 0)], writes=["acc1"])
                    pool(lambda e, sq=sq: e.tensor_copy(out=acc2[:], in_=sq[:]), reads=[("sq", si)], writes=["acc2"])
                else:
                    pool(lambda e, c=c: e.tensor_tensor(out=acc1[:], in0=acc1[:], in1=cv[:, c, :], op=ALU.add), reads=[("cv", c), "acc1"], writes=["acc1"])
                    pool(lambda e, sq=sq: e.tensor_tensor(out=acc2[:], in0=acc2[:], in1=sq[:], op=ALU.add), reads=[("sq", si), "acc2"], writes=["acc2"])

            def step_pair(c0):
                for c in (c0, c0 + 1):
                    ui, ub = gv_chunk(c)
                    pend.append((c, ui, ub))
                    if len(pend) > 1:
                        conv_chunk(*pend.pop(0))

            def rest():
                conv_chunk(*pend.pop(0))
                conv_rest(buf)

            return step_pair, rest

        def conv_rest(buf):
            wt = {}
            b1, bk1 = newbank()
            P.op("pe", lambda e: e.matmul(bk1[:, 0:T], lhsT=onesf[:], rhs=acc1[:], start=True, stop=True), reads=["onesf", "acc1"], writes=[b1], cost=0.6)
            b2, bk2 = newbank()
            P.op("pe", lambda e: e.matmul(bk2[:, 0:T], lhsT=onesf[:], rhs=acc2[:], start=True, stop=True), reads=["onesf", "acc2"], writes=[b2], cost=0.6)
            dve(lambda e: e.tensor_scalar(out=lnM[:], in0=bk1[:, 0:T], scalar1=1.0 / DC, scalar2=None, op0=ALU.mult), xreads=[b1], writes=["lnM"])
            dve(lambda e: e.tensor_tensor(out=lnN[:], in0=lnM[:], in1=lnM[:], op=ALU.mult), reads=["lnM"], writes=["lnN"])
            dve(lambda e: e.scalar_tensor_tensor(out=lnR[:], in0=bk2[:, 0:T], scalar=1.0 / DC, in1=lnN[:], op0=ALU.mult, op1=ALU.subtract),
                reads=["lnN"], writes=["lnR"], xreads=[b2])
            dve(lambda e: e.tensor_scalar(out=lnR[:], in0=lnR[:], scalar1=0.0, scalar2=LN_EPS, op0=ALU.max, op1=ALU.add), reads=["lnR"], writes=["lnR"])
            act(lnR[:], lnR[:], AF.Sqrt, reads=["lnR"], writes=["lnR"])
            dve(lambda e: e.reciprocal(out=lnR[:], in_=lnR[:]), reads=["lnR"], writes=["lnR"], k=8)
            dve(lambda e: e.scalar_tensor_tensor(out=lnN[:], in0=lnM[:], scalar=-1.0, in1=lnR[:], op0=ALU.mult, op1=ALU.mult),
                reads=["lnM", "lnR"], writes=["lnN"])
            for c in range(NCC):
                if c % 2 == 0:
                    wt["cg"] = load_win(O_CGATE + c * 128, "cgate")
                o = (c % 2) * 128
                wres, wv = wt["cg"]
                b3, bk3 = newbank()
                mm_group(b3, bk3[:, 0:T], [(wv[:, k, o:o + 128], hT[buf][:, k, :], [wres, ("hT", buf, k)]) for k in range(KC)], [])
                t2i, t2 = t2buf.next()
                act(t2[:], bk3[:, 0:T], AF.Tanh, xreads=[b3], writes=[("t2", t2i)], scale=0.5)
                dve(lambda e, t2=t2, bk3=bk3: e.scalar_tensor_tensor(out=t2[:], in0=t2[:], scalar=1.0, in1=bk3[:, 0:T], op0=ALU.add, op1=ALU.mult),
                    reads=[("t2", t2i)], writes=[("t2", t2i)], xreads=[b3])
                zi, z = zbuf.next()
                pool(lambda e, z=z, c=c: e.tensor_tensor(out=z[:], in0=cv[:, c, :], in1=lnR[:], op=ALU.mult), reads=[("cv", c), "lnR"], writes=[("z", zi)])
                pool(lambda e, z=z: e.tensor_tensor(out=z[:], in0=z[:], in1=lnN[:], op=ALU.add), reads=[("z", zi), "lnN"], writes=[("z", zi)])
                dve(lambda e, z=z, c=c: e.tensor_scalar(out=z[:], in0=z[:], scalar1=col(par, P_LG + c), scalar2=col(par, P_LB + c),
                                                        op0=ALU.mult, op1=ALU.add), reads=[("z", zi), "par"], writes=[("z", zi)])
                t1i, t1 = t1buf.next()
                act(t1[:], z[:], AF.Tanh, reads=[("z", zi)], writes=[("t1", t1i)], scale=0.5)
                dve(lambda e, z=z, t1=t1: e.scalar_tensor_tensor(out=z[:], in0=t1[:], scalar=1.0, in1=z[:], op0=ALU.add, op1=ALU.mult),
                    reads=[("t1", t1i), ("z", zi)], writes=[("z", zi)])
                dve(lambda e, z=z, t2=t2, c=c: e.tensor_tensor(out=ug[:, c, :], in0=z[:], in1=t2[:], op=ALU.mult),
                    reads=[("z", zi), ("t2", t2i)], writes=[("ug", c)])

        def merge_stage(buf):
            hr = hT_reads(buf)
            for mp in range(KC // 2):
                ms = (2 * mp, 2 * mp + 1)
                recs = {m: {} for m in ms}
                for key, col0, wbn in (("gc", O_GCONV, "gconv"), ("gr", O_GRNN, "grnn")):
                    wres, wv = load_win(col0 + ms[0] * 128, wbn)
                    for j, m in enumerate(ms):
                        b1, bk1 = newbank()
                        mm_group(b1, bk1[:, 0:T], [(wv[:, k, j * 128:(j + 1) * 128], hT[buf][:, k, :], [wres, ("hT", buf, k)]) for k in range(KC)], [])
                        ring = mt1 if key == "gc" else mt2
                        i1, m1 = ring.next()
                        act(m1[:], bk1[:, 0:T], AF.Tanh, xreads=[b1], writes=[(key, i1)], scale=0.5)
                        recs[m][key] = (i1, m1)
                for key, gkey, src, nk, wbn, rname in (("yr", "gr", hg, NRC, "wro", "hg"), ("yc", "gc", ug, NCC, "wco", "ug")):
                    wres, wv = load_tile(wbn, mp)
                    for j, m in enumerate(ms):
                        b3, bk3 = newbank()
                        mm_group(b3, bk3[:, 0:T], [(wv[:, k, j * 128:(j + 1) * 128], src[:, k, :], [wres, (rname, k)]) for k in range(nk)], [])
                        i1, m1 = recs[m][gkey]
                        dve(lambda e, m1=m1, bk3=bk3: e.scalar_tensor_tensor(out=m1[:], in0=m1[:], scalar=1.0, in1=bk3[:, 0:T], op0=ALU.add, op1=ALU.mult),
                            reads=[(gkey, i1)], writes=[(gkey, i1)], xreads=[b3])
                for m in ms:
                    i1, m1 = recs[m]["gc"]
                    i2, m2 = recs[m]["gr"]
                    dve(lambda e, m1=m1, m2=m2, m=m: e.tensor_tensor(out=mg[:, m, :], in0=m1[:], in1=m2[:], op=ALU.add),
                        reads=[("gc", i1), ("gr", i2)], writes=[("mg", m), ("cv", m // 2)])

        def out_stage(tile_idx):
            r0 = tile_idx * T
            xs = []
            for tb in range(2):
                xi, xb = xring.next()
                dma("pool", xb[:], xm[r0 + tb * 128:r0 + (tb + 1) * 128, :], [], [("x", xi)], ("x", xi))
                xs.append((xi, xb))
            for cg in range(8):
                wres, wv = load_tile("wo", cg)
                for tb in range(2):
                    xi, xb = xs[tb]
                    b1, bk1 = newbank()
                    mm_group(b1, bk1[:, 0:256], [(mg[:, k, tb * 128:(tb + 1) * 128], wv[:, k, :], [wres, ("mg", k)]) for k in range(KC)], [])
                    dve(lambda e, xb=xb, bk1=bk1, cg=cg: e.scalar_tensor_tensor(out=xb[:, cg * 256:(cg + 1) * 256], in0=bk1[:, 0:256], scalar=0.125,
                                                                               in1=xb[:, cg * 256:(cg + 1) * 256], op0=ALU.mult, op1=ALU.add),
                        reads=[("x", xi)], writes=[("x", xi)], xreads=[b1])
            for tb in range(2):
                xi, xb = xs[tb]
                act(junk[:], xb[:], AF.Square, reads=[("x", xi)], writes=["junk", ("ssq2", tb)], accum_out=col(ssq2, tb))
            dve(lambda e: e.tensor_scalar(out=rs2[:], in0=ssq2[:], scalar1=1.0 / D, scalar2=EPS, op0=ALU.mult, op1=ALU.add),
                reads=[("ssq2", 0), ("ssq2", 1)], writes=["rs2"], n=2, k=1)
            act(rs2[:], rs2[:], AF.Sqrt, reads=["rs2"], writes=["rs2"])
            dve(lambda e: e.reciprocal(out=rs2[:], in_=rs2[:]), reads=["rs2"], writes=["rs2"], n=2, k=8)
            outs = []
            for tb in range(2):
                xi, xb = xs[tb]
                dve(lambda e, xb=xb, tb=tb: e.scalar_tensor_tensor(out=xb[:], in0=xb[:], scalar=col(rs2, tb), in1=fg[:], op0=ALU.mult, op1=ALU.mult),
                    reads=[("x", xi), "rs2", "fg"], writes=[("x", xi)], n=D, k=2)
                outs.append(dma("pool", y[r0 + tb * 128:r0 + (tb + 1) * 128, :], xb[:], [("x", xi)], [], ("xo", xi)))
            return outs

        final = []
        diag_plan = {}
        if ntl == 1:
            diag_plan[0] = list(range(NCC))
        else:
            per = -(-NCC // (ntl - 1))
            per = max(per, 2)
            cc_ = 0
            for t_ in range(0, ntl - 1):
                diag_plan[t_] = list(range(cc_, min(NCC, cc_ + per)))
                cc_ += per
        for c in range(NCC):
            build_diag(c)
        n_groups = 4 * ntl
        late_per_group = -(-len(late_pieces) // max(1, n_groups - 6))
        pp = prep(xp, 0, 0)
        early_casts()
        prep_pe(pp, 0)
        cur = 0
        for t in range(ntl):
            nxt = 1 - cur
            if t + 1 < ntl:
                pp = prep(xp, t + 1, nxt)
            else:
                pp = prep(xm, 0, nxt)
            rnn_stage(cur, main=False, mid_hook=lambda pp=pp, nxt=nxt: prep_pe(pp, nxt, all_act=True), late_n=late_per_group)
            if t == ntl - 1:
                conv_halo(cur)
            cur = nxt
        emit_late(len(late_pieces))
        P.cutoff = len(P.ops)
        mask_state()
        for t in range(ntl):
            nxt = 1 - cur
            step_pair, conv_finish = conv_stage(cur)
            fillers = [lambda c0=c0: step_pair(c0) for c0 in range(0, NCC, 2)]
            rnn_stage(cur, main=True, fillers=fillers, nfill=FILL_N)
            if t + 1 < ntl:
                pp = prep(xm, t + 1, nxt)
            while fillers:
                fillers.pop(0)()
            conv_finish()
            merge_stage(cur)
            if t + 1 < ntl:
                prep_pe(pp, nxt)
            final += out_stage(t)
            cur = nxt
        P.final_dma = final
        P.build(st)
    return nc


def _pack_params(norm_g, conv_dw_w, conv_dw_b, conv_ln_g, conv_ln_b, rnn_conv_w, rnn_conv_b, b_rg_a, b_rg_x, rg_lambda):
    par = np.zeros((128, NPAR), np.float32)
    par[:, P_G:P_G + 16] = norm_g.reshape(16, 128).T
    par[:, P_CW:P_CW + 496] = conv_dw_w.reshape(CW, NCC, 128).transpose(2, 1, 0).reshape(128, NCC * CW)
    par[:, P_CB:P_CB + 16] = conv_dw_b.reshape(16, 128).T
    par[:, P_LG:P_LG + 16] = conv_ln_g.reshape(16, 128).T
    par[:, P_LB:P_LB + 16] = conv_ln_b.reshape(16, 128).T
    par[:, P_RW:P_RW + 80] = rnn_conv_w.reshape(RW, NRC, 128).transpose(2, 1, 0).reshape(128, NRC * RW)
    par[:, P_RB:P_RB + 20] = rnn_conv_b.reshape(20, 128).T
    par[:, P_BA:P_BA + 20] = b_rg_a.reshape(20, 128).T
    par[:, P_BX:P_BX + 20] = b_rg_x.reshape(20, 128).T
    par[:, P_LAM:P_LAM + 20] = rg_lambda.reshape(20, 128).T
    return par


def _pack_gates(w_a, w_x):
    out = np.zeros((4, 128, NBLK, 128), np.float32)
    for gate, w in enumerate((w_a, w_x)):
        for g in range(4):
            dense = np.zeros((640, 640), np.float32)
            for hh in range(4):
                dense[hh * 160:(hh + 1) * 160, hh * 160:(hh + 1) * 160] = w[g * 4 + hh]
            for oc in range(5):
                for ic in GATE_ICS[oc]:
                    out[g, :, GATE_BLK[(gate, oc, ic)], :] = dense[ic * 128:(ic + 1) * 128, oc * 128:(oc + 1) * 128]
    return out.reshape(4 * 128, NBLK * 128)


_NC_CACHE = {}


def kernel(x, norm_g, w_in, conv_dw_w, conv_dw_b, conv_ln_g, conv_ln_b, w_conv_out,
           rnn_conv_w, rnn_conv_b, w_rg_a, b_rg_a, w_rg_x, b_rg_x, rg_lambda,
           w_rnn_out, w_out, final_norm_g):
    f = lambda a: np.ascontiguousarray(np.asarray(a, dtype=np.float32))
    x = f(x)
    par = _pack_params(f(norm_g)[0], f(conv_dw_w)[0], f(conv_dw_b)[0], f(conv_ln_g)[0], f(conv_ln_b)[0],
                       f(rnn_conv_w)[0], f(rnn_conv_b)[0], f(b_rg_a)[0], f(b_rg_x)[0], f(rg_lambda)[0])
    gwp = _pack_gates(f(w_rg_a)[0], f(w_rg_x)[0])
    fgb = np.ascontiguousarray(np.broadcast_to(f(final_norm_g)[None, :], (128, D)))
    shared = {"params": par, "fg": fgb, "w_in": f(w_in)[0], "w_co": f(w_conv_out)[0], "w_ro": f(w_rnn_out)[0],
              "w_o": f(w_out)[0], "gw": gwp}
    in_maps = []
    for c in range(8):
        b, half = c // 2, c % 2
        m = dict(shared)
        m["xm"] = np.ascontiguousarray(x[b, half * TOK:(half + 1) * TOK, :])
        m["xp"] = np.ascontiguousarray(x[b, 0:TOK, :])
        m["flag"] = np.full((128, 1), float(half), np.float32)
        in_maps.append(m)
    if "nc" not in _NC_CACHE:
        _NC_CACHE["nc"] = build_program()
    nc = _NC_CACHE["nc"]
    res = run_bass_kernel_spmd(nc, in_maps, core_ids=list(range(8)))
    out = np.empty((4, 2 * TOK, D), np.float32)
    for c in range(8):
        b, half = c // 2, c % 2
        out[b, half * TOK:(half + 1) * TOK, :] = res.results[c]["y"]
    return out
```

```python
from contextlib import ExitStack
import numpy as np
import concourse.bass as bass
import concourse.mybir as mybir
from concourse.bass_utils import run_bass_kernel_spmd

F32 = mybir.dt.float32
BF16 = mybir.dt.bfloat16
AF = mybir.ActivationFunctionType
ALU = mybir.AluOpType

D = 2048
DC = 2048
DR = 2560
DIN = 15360
NCC = 16
NRC = 20
KC = 16
CW = 31
RW = 4
T = 256
FILL_N = 3
TOK = 4096
NT = TOK // T
EPS = 1e-6
LN_EPS = 1e-5

O_CVAL, O_CGLU, O_CGATE, O_RX, O_RGATE, O_GCONV, O_GRNN = 0, 2048, 4096, 6144, 8704, 11264, 13312

P_G, P_CW, P_CB, P_LG, P_LB, P_RW, P_RB, P_BA, P_BX, P_LAM = 0, 16, 512, 528, 544, 560, 640, 660, 680, 700
NPAR = 720
Q_HC, Q_CC, Q_HBA, Q_HBX, Q_HLG, Q_HLB = 0, 20, 40, 60, 80, 96
NDER = 112

GATE_ICS = {0: (0, 1), 1: (0, 1, 2), 2: (1, 2, 3), 3: (2, 3, 4), 4: (3, 4)}
GATE_BLK = {}
_b = 0
for _gate in range(2):
    for _oc in range(5):
        for _ic in GATE_ICS[_oc]:
            GATE_BLK[(_gate, _oc, _ic)] = _b
            _b += 1
NBLK = _b

ENGS = ("pe", "act", "dve", "pool", "sp")


class Op:
    __slots__ = ("eng", "fn", "reads", "writes", "xreads", "dma_key", "deps", "milestone", "cnt", "idx", "sem", "cost", "table", "alldeps", "pos")


class Prog:
    def __init__(self, nc):
        self.nc = nc
        self.ops = []
        self.last_writer = {}
        self.readers = {}
        self.final_dma = []
        self.cutoff = None

    def op(self, eng, fn, reads=(), writes=(), xreads=(), dma_key=None, cost=0.3, table=None):
        o = Op()
        o.cost, o.table = cost, table
        o.eng, o.fn, o.dma_key = eng, fn, dma_key
        o.reads, o.writes, o.xreads = tuple(reads), tuple(writes), tuple(xreads)
        o.milestone, o.cnt, o.sem = False, None, None
        o.idx = len(self.ops)
        deps = set()
        for r in o.reads:
            lw = self.last_writer.get(r)
            if lw is not None:
                deps.add(lw)
        for w in o.writes + o.xreads:
            lw = self.last_writer.get(w)
            if lw is not None:
                deps.add(lw)
            rd = self.readers.get(w)
            if rd:
                deps.update(rd)
        for r in o.reads:
            self.readers.setdefault(r, []).append(o.idx)
        for w in o.writes + o.xreads:
            self.last_writer[w] = o.idx
            self.readers[w] = []
        deps.discard(o.idx)
        o.deps = deps
        self.ops.append(o)
        return o

    def schedule(self, window=4000, cutoff=None):
        ops = self.ops
        n = len(ops)
        ndeps = [len(o.alldeps) for o in ops]
        users = [[] for _ in range(n)]
        for o in ops:
            for d in o.alldeps:
                users[d].append(o.idx)
        ready = [0.0] * n
        finish = [0.0] * n
        cand = {e: [] for e in ENGS}
        for o in ops:
            if ndeps[o.idx] == 0:
                cand[o.eng].append(o.idx)
        t_eng = {e: 0.0 for e in ENGS}
        self.t0 = [0.0] * n
        order = {e: [] for e in ENGS}
        act_table = None
        dma_free = 0.0
        done = [False] * n
        low = 0
        nsched = 0
        if cutoff is None:
            cutoff = n
        po = {e: [o.idx for o in ops if o.eng == e and o.idx >= cutoff] for e in ENGS}
        po_ptr = {e: 0 for e in ENGS}
        while nsched < n:
            while low < n and done[low]:
                low += 1
            lim = low + window
            best = None
            best_key = None
            for e in ENGS:
                cl = cand[e]
                if not cl:
                    continue
                te = t_eng[e]
                nxt_po = po[e][po_ptr[e]] if po_ptr[e] < len(po[e]) else -1
                for i in cl:
                    if i > lim:
                        continue
                    if i >= cutoff and i != nxt_po:
                        continue
                    st = ready[i] if ready[i] > te else te
                    if e == "act":
                        tb = ops[i].table
                        if tb is not None and tb != act_table:
                            st += 1.4
                    key = (st, i)
                    if best_key is None or key < best_key:
                        best_key = key
                        best = (e, i)
            e, i = best
            o = ops[i]
            st = best_key[0]
            if o.dma_key is not None:
                issue = 0.08 if e == "sp" else 0.7
                t_eng[e] = st + issue
                xfer = o.cost / 250000.0
                s0 = max(st + issue, dma_free)
                dma_free = s0 + xfer
                finish[i] = s0 + xfer + 2.0
            else:
                if e == "act" and o.table is not None:
                    act_table = o.table
                t_eng[e] = st + o.cost
                finish[i] = st + o.cost + (0.15 if e == "pe" else 0.05)
            done[i] = True
            nsched += 1
            self.t0[i] = st
            if i >= cutoff:
                po_ptr[e] += 1
            cand[e].remove(i)
            o.pos = len(order[e])
            order[e].append(i)
            for u in users[i]:
                ndeps[u] -= 1
                if finish[i] > ready[u]:
                    ready[u] = finish[i]
                if ndeps[u] == 0:
                    cand[ops[u].eng].append(u)
        self.sim_time = max(finish) if n else 0.0
        return order

    def build(self, stack, reorder=True):
        nc = self.nc
        ops = self.ops
        for o in ops:
            o.alldeps = set(o.deps)
        if reorder:
            order = self.schedule(cutoff=self.cutoff)
        else:
            order = {e: [o.idx for o in ops if o.eng == e] for e in ENGS}
            for e in ENGS:
                for p_, i in enumerate(order[e]):
                    ops[i].pos = p_
        for o in ops:
            best = {}
            for d in o.alldeps:
                p = ops[d]
                if p.dma_key is None and o.dma_key is None and p.eng == o.eng:
                    if p.eng == "pe":
                        continue
                    touched = set(o.reads) | set(o.writes) | set(o.xreads)
                    war = (set(p.reads) | set(p.xreads)) & set(o.writes)
                    if not (set(p.writes) & touched) and not war:
                        continue
                k = p.dma_key if p.dma_key is not None else p.eng
                if k not in best or ops[best[k]].pos < p.pos:
                    best[k] = d
            o.deps = sorted(best.values())
            for d in o.deps:
                ops[d].milestone = True
        for o in ops:
            if o.dma_key is not None:
                o.milestone = True
        eng_sem, eng_cnt = {}, {}
        for e in ENGS:
            eng_sem[e] = stack.enter_context(nc.semaphore("sem_" + e))
            eng_cnt[e] = 0
        dma_sem, dma_cnt = {}, {}
        for e in ENGS:
            for i in order[e]:
                o = ops[i]
                if o.dma_key is None and o.milestone:
                    eng_cnt[o.eng] += 1
                    o.sem, o.cnt = eng_sem[o.eng], eng_cnt[o.eng]
        for o in ops:
            if o.dma_key is not None:
                if o.dma_key not in dma_sem:
                    dma_sem[o.dma_key] = stack.enter_context(nc.semaphore("dsem_%d" % len(dma_sem)))
                    dma_cnt[o.dma_key] = 0
                dma_cnt[o.dma_key] += 16
                o.sem, o.cnt = dma_sem[o.dma_key], dma_cnt[o.dma_key]
        for o in ops:
            if o.dma_key is not None and o.dma_key[0] == "castfull":
                o.cnt = dma_cnt[o.dma_key]
        self.n_sems = len(dma_sem) + len(ENGS)
        per_eng = {e: [ops[i] for i in order[e]] for e in ENGS}
        block = stack.enter_context(nc.Block())
        final = [(o.sem, o.cnt) for o in self.final_dma]

        def emit(e, engobj):
            known = {}
            for o in per_eng[e]:
                need = {}
                for d in o.deps:
                    p = ops[d]
                    k = id(p.sem)
                    if k not in need or need[k][1] < p.cnt:
                        need[k] = (p.sem, p.cnt)
                for k, (sem, cnt) in need.items():
                    if known.get(k, 0) >= cnt:
                        continue
                    engobj.wait_ge(sem, cnt)
                    known[k] = cnt
                ins = o.fn(engobj)
                if o.milestone:
                    ins.then_inc(o.sem, 16 if o.dma_key is not None else 1)
            if e == "sp":
                for sem, cnt in final:
                    engobj.wait_ge(sem, cnt)

        @block.tensor
        def _(eng):
            emit("pe", eng)

        @block.scalar
        def _(eng):
            emit("act", eng)

        @block.vector
        def _(eng):
            emit("dve", eng)

        @block.gpsimd
        def _(eng):
            emit("pool", eng)

        @block.sync
        def _(eng):
            emit("sp", eng)


class Ring:
    def __init__(self, items):
        self.items = items
        self.i = 0

    def next(self):
        k = self.i % len(self.items)
        self.i += 1
        return k, self.items[k]


def build_program(tok=TOK):
    nc = bass.Bass("TRN2", target_bir_lowering=False)
    ntl = tok // T
    dt = nc.dram_tensor
    xm = dt("xm", [tok, D], F32, kind="ExternalInput").ap()
    xp = dt("xp", [tok, D], F32, kind="ExternalInput").ap()
    flag_d = dt("flag", [128, 1], F32, kind="ExternalInput").ap()
    par_d = dt("params", [128, NPAR], F32, kind="ExternalInput").ap()
    fg_d = dt("fg", [128, D], F32, kind="ExternalInput").ap()
    w_in = dt("w_in", [D, DIN], F32, kind="ExternalInput").ap()
    w_co = dt("w_co", [DC, D], F32, kind="ExternalInput").ap()
    w_ro = dt("w_ro", [DR, D], F32, kind="ExternalInput").ap()
    w_o = dt("w_o", [D, D], F32, kind="ExternalInput").ap()
    gw = dt("gw", [4 * 128, NBLK * 128], F32, kind="ExternalInput").ap()
    y = dt("y", [tok, D], F32, kind="ExternalOutput").ap()
    def _grp_tiles(base):
        tl = []
        for g in range(4):
            tl += [(base + g * 640, 256), (base + g * 640 + 256, 256), (base + g * 640 + 512, 128)]
        return tl
    FAM = {}
    for _nm, _base in (("rx", O_RX), ("rgate", O_RGATE)):
        FAM[_nm] = (dt(_nm + "_b", [12 * 128, KC * 256], BF16, kind="Internal").ap(), w_in, KC, _grp_tiles(_base))
    for _nm, _base in (("cglu", O_CGLU), ("cval", O_CVAL), ("cgate", O_CGATE), ("gconv", O_GCONV), ("grnn", O_GRNN)):
        FAM[_nm] = (dt(_nm + "_b", [8 * 128, KC * 256], BF16, kind="Internal").ap(), w_in, KC, [(_base + i * 256, 256) for i in range(8)])
    FAM["wro"] = (dt("wro_b", [8 * 128, NRC * 256], BF16, kind="Internal").ap(), w_ro, NRC, [(i * 256, 256) for i in range(8)])
    FAM["wco"] = (dt("wco_b", [8 * 128, NCC * 256], BF16, kind="Internal").ap(), w_co, NCC, [(i * 256, 256) for i in range(8)])
    FAM["wo"] = (dt("wo_b", [8 * 128, KC * 256], BF16, kind="Internal").ap(), w_o, KC, [(i * 256, 256) for i in range(8)])
    gw_b = dt("gw_b", [4 * 128, NBLK * 128], BF16, kind="Internal").ap()
    dg_b = dt("dg_b", [NCC * 128, CW * 128], BF16, kind="Internal").ap()

    with ExitStack() as st:
        def sb(name, shape, dtype):
            return st.enter_context(nc.sbuf_tensor(name, shape, dtype))

        P = Prog(nc)

        par = sb("par", [128, NPAR], F32)
        der = sb("der", [128, NDER], F32)
        tmp20 = sb("tmp20", [128, 20], F32)
        flag = sb("flag_sb", [128, 1], F32)
        fg = sb("fg_sb", [128, D], F32)
        identf = sb("identf", [128, 128], F32)
        ident = sb("ident", [128, 128], BF16)
        onesf = sb("onesf", [128, 128], F32)
        NWS = 3
        wslots = [sb("wslot%d" % i, [128, 20 * 256], BF16) for i in range(NWS)]
        wring = Ring(wslots)
        dgslots = [sb("dgslot%d" % i, [128, CW, 128], BF16) for i in range(2)]
        dgring = Ring(dgslots)
        hT = [sb("hT%d" % i, [128, KC, T], BF16) for i in range(2)]
        hs = Ring([sb("hs%d" % i, [128, D], BF16) for i in range(2)])
        xring = Ring([sb("xblk%d" % i, [128, D], F32) for i in range(3)])
        junk = sb("junk", [128, D], BF16)
        ssq = sb("ssq", [128, 2], F32)
        rs = sb("rs", [128, 2], F32)
        ssq2 = sb("ssq2", [128, 2], F32)
        rs2 = sb("rs2", [128, 2], F32)
        cv = sb("cv", [128, NCC, T], F32)
        ug = sb("ug", [128, NCC, T], BF16)
        hg = sb("hg", [128, NRC, T], BF16)
        mg = cv[:].rearrange("p c t -> p (c t)")[:, 0:KC * T // 2].bitcast(BF16).rearrange("p (k t) -> p k t", t=T)
        ubuf = Ring([sb("ubuf%d" % i, [128, 30 + T], BF16) for i in range(3)])
        uh = sb("uh", [128, NCC, 30], BF16)
        rxr = Ring([sb("rx%d" % i, [128, 3 + T], F32) for i in range(2)])
        rxh = sb("rxh", [128, NRC, 3], F32)
        state = sb("state", [128, NRC], F32)
        vbuf = [sb("v%d" % i, [128, T], F32) for i in range(10)]
        vbb = [sb("vb%d" % i, [128, T], BF16) for i in range(10)]
        NG = 6
        bufA = Ring([sb("bA%d" % i, [128, T], F32) for i in range(NG)])
        bufB = Ring([sb("bB%d" % i, [128, T], F32) for i in range(NG)])
        bufC = Ring([sb("bC%d" % i, [128, T], F32) for i in range(NG)])
        bufD = Ring([sb("bD%d" % i, [128, T], F32) for i in range(2)])
        bufE = Ring([sb("bE%d" % i, [128, T], F32) for i in range(5)])
        tbuf = Ring([sb("tb%d" % i, [128, T], F32) for i in range(2)])
        sqt = Ring([sb("sq%d" % i, [128, T], F32) for i in range(2)])
        acc1 = sb("acc1", [128, T], F32)
        acc2 = sb("acc2", [128, T], F32)
        zbuf = Ring([sb("z%d" % i, [128, T], F32) for i in range(2)])
        t1buf = Ring([sb("t1%d" % i, [128, T], F32) for i in range(1)])
        t2buf = Ring([sb("t2%d" % i, [128, T], F32) for i in range(2)])
        lnM = sb("lnM", [128, T], F32)
        lnR = sb("lnR", [128, T], F32)
        lnN = sb("lnN", [128, T], F32)
        mt1 = Ring([sb("mt1%d" % i, [128, T], F32) for i in range(2)])
        mt2 = Ring([sb("mt2%d" % i, [128, T], F32) for i in range(2)])
        banks = [st.enter_context(nc.psum_tensor("bank%d" % i, [128, 512], F32)) for i in range(8)]
        bring = Ring(banks[0:7])
        pvb = banks[7]

        def newbank():
            k, b = bring.next()
            return ("ps", k), b

        def nfree(ap):
            n = 1
            for d in ap.shape[1:]:
                n *= int(d)
            return n

        def dma(eng, out, in_, reads, writes, key):
            nb = nfree(out) * int(out.shape[0]) * (2 if out.dtype == BF16 else 4)
            if in_.dtype != out.dtype:
                nb *= 3
            return P.op(eng, lambda e: e.dma_start(out=out, in_=in_), reads=reads, writes=writes, dma_key=key, cost=float(nb))

        TABLE = {AF.Tanh: "exp", AF.Exp: "exp", AF.Sqrt: "sqrt", AF.Ln: "ln"}

        def act(out, in_, func, reads=(), writes=(), xreads=(), **kw):
            n = nfree(out)
            cost = 0.2 + n / 1200.0 + (0.09 if not isinstance(kw.get("scale", 1.0), float) else 0.0) + (0.09 if not isinstance(kw.get("bias", 0.0), float) else 0.0)
            return P.op("act", lambda e: e.activation(out=out, in_=in_, func=func, **kw), reads=reads, writes=writes, xreads=xreads,
                        cost=cost, table=TABLE.get(func))

        def dve(fn, reads=(), writes=(), xreads=(), n=T, k=2.0):
            if xreads and k == 2.0:
                k = 1.0
            return P.op("dve", fn, reads=reads, writes=writes, xreads=xreads, cost=0.08 + k * n / 960.0)

        def pool(fn, reads=(), writes=(), n=T):
            return P.op("pool", fn, reads=reads, writes=writes, cost=0.35 + n / 350.0)

        def col(t, c):
            return t[:, c:c + 1]

        dma("sp", par[:], par_d, [], ["par"], "ld_par")
        dma("sp", flag[:], flag_d, [], ["flag"], "ld_flag")
        dma("sp", fg[:], fg_d, [], ["fg"], "ld_fg")
        pool(lambda e: e.memset(identf[:], 0.0), writes=["identf"])
        pool(lambda e: e.affine_select(out=identf[:], in_=identf[:], pattern=[[-1, 128]], compare_op=ALU.not_equal,
                                       fill=1.0, base=0, channel_multiplier=1), reads=["identf"], writes=["identf"])
        pool(lambda e: e.memset(onesf[:], 1.0), writes=["onesf"])
        pool(lambda e: e.memset(state[:], 0.0), writes=[("state", c) for c in range(NRC)])
        pool(lambda e: e.memset(rxh[:], 0.0), writes=[("rxh", c) for c in range(NRC)])
        pool(lambda e: e.memset(uh[:], 0.0), writes=[("uh", c) for c in range(NCC)])
        dve(lambda e: e.tensor_copy(out=ident[:], in_=identf[:]), reads=["identf"], writes=["ident"])
        act(tmp20[:], par[:, P_LAM:P_LAM + 20], AF.Exp, reads=["par"], writes=["tmp20"], scale=-1.0)
        act(tmp20[:], tmp20[:], AF.Ln, reads=["tmp20"], writes=["tmp20"], bias=1.0)
        dve(lambda e: e.tensor_scalar(out=der[:, Q_HC:Q_HC + 20], in0=tmp20[:], scalar1=-4.0, scalar2=None, op0=ALU.mult),
            reads=["tmp20"], writes=["der_hc"])
        dve(lambda e: e.tensor_scalar(out=der[:, Q_CC:Q_CC + 20], in0=tmp20[:], scalar1=-8.0, scalar2=None, op0=ALU.mult),
            reads=["tmp20"], writes=["der_cc"])
        for (q, p_, n) in ((Q_HBA, P_BA, 20), (Q_HBX, P_BX, 20), (Q_HLG, P_LG, 16), (Q_HLB, P_LB, 16)):
            dve(lambda e, q=q, p_=p_, n=n: e.tensor_scalar(out=der[:, q:q + n], in0=par[:, p_:p_ + n], scalar1=0.5,
                                                           scalar2=None, op0=ALU.mult), reads=["par"], writes=[("der", q)])
        DER = ["der_hc", "der_cc", ("der", Q_HBA), ("der", Q_HBX), ("der", Q_HLG), ("der", Q_HLB)]

        pace = {"res": None}

        def cast_grp(name, i):
            return (i // 3) if name == "rx" else (i if name == "gw" else 0)

        def cast_piece(name, i, dst, src):
            grp = cast_grp(name, i)
            dma("pool", dst, src, ([pace["res"]] if pace["res"] is not None else []), [("wb", name, i)], ("castfull", name, grp))

        def fam_piece(name, j):
            scr, src, nk, tiles = FAM[name]
            c0, w = tiles[j]
            dst = scr[j * 128:(j + 1) * 128, 0:nk * w].rearrange("p (k n) -> p k n", n=w)
            cast_piece(name, j, dst, src[:, c0:c0 + w].rearrange("(k p) n -> p k n", p=128))

        def early_casts():
            for g in range(4):
                for j in range(3):
                    fam_piece("rx", 3 * g + j)
                cast_piece("gw", g, gw_b[g * 128:(g + 1) * 128, :].rearrange("p (a n) -> p a n", n=256),
                           gw[g * 128:(g + 1) * 128, :].rearrange("p (a n) -> p a n", n=256))
        late_pieces = []
        for i in range(8):
            late_pieces.append(lambda i=i: fam_piece("cglu", i))
            late_pieces.append(lambda i=i: fam_piece("cval", i))
        for i in range(12):
            late_pieces.append(lambda i=i: fam_piece("rgate", i))
        for i in range(8):
            late_pieces.append(lambda i=i: fam_piece("cgate", i))
        for i in range(8):
            late_pieces.append(lambda i=i: fam_piece("gconv", i))
            late_pieces.append(lambda i=i: fam_piece("grnn", i))
            late_pieces.append(lambda i=i: fam_piece("wro", i))
            late_pieces.append(lambda i=i: fam_piece("wco", i))
        for i in range(8):
            late_pieces.append(lambda i=i: fam_piece("wo", i))

        def emit_late(n):
            for _ in range(n):
                if late_pieces:
                    late_pieces.pop(0)()

        def build_diag(c):
            s_, slot = dgring.next()
            for k in range(CW):
                dve(lambda e, slot=slot, k=k, c=c: e.tensor_scalar(out=slot[:, k, :], in0=identf[:],
                                                                   scalar1=col(par, P_CW + c * CW + k), scalar2=0.5,
                                                                   op0=ALU.mult, op1=ALU.mult),
                    reads=["identf", "par"], writes=[("dgslot", s_)], n=128, k=1)
            dma("sp", dg_b[c * 128:(c + 1) * 128, :], slot[:].rearrange("p k n -> p (k n)"), [("dgslot", s_)], [("dgb", c)], ("dgout", s_))

        slot_gen = {}

        def chk(res):
            assert slot_gen[res[0:2]] == res[2], "stale weight tile %r" % (res,)
            return res[0:2]

        def load_tile(name, j):
            scr, src, nk, tiles = FAM[name]
            c0, w = tiles[j]
            s, slot = wring.next()
            slot_gen[("wslot", s)] = slot_gen.get(("wslot", s), 0) + 1
            view = slot[:, 0:nk * w].rearrange("p (k n) -> p k n", n=w)
            members = [("wb", name, i) for i in range(len(tiles)) if cast_grp(name, i) == cast_grp(name, j)]
            dma("sp", slot[:, 0:nk * w], scr[j * 128:(j + 1) * 128, 0:nk * w], members, [("wslot", s)], ("wslot", s))
            return ("wslot", s, slot_gen[("wslot", s)]), view

        def load_win(col0, wbname, ncols=256):
            tiles = FAM[wbname][3]
            j = [i for i, (c0, w) in enumerate(tiles) if c0 == col0]
            assert len(j) == 1 and tiles[j[0]][1] == ncols, (wbname, col0, ncols)
            return load_tile(wbname, j[0])

        def load_gate(g):
            s, slot = wring.next()
            slot_gen[("wslot", s)] = slot_gen.get(("wslot", s), 0) + 1
            view = slot[:, 0:NBLK * 128].rearrange("p (k n) -> p k n", n=128)
            dma("sp", slot[:, 0:NBLK * 128], gw_b[g * 128:(g + 1) * 128, :], [("wb", "gw", g)], [("wslot", s)], ("wslot", s))
            return ("wslot", s, slot_gen[("wslot", s)]), view

        def load_dg(c):
            s, slot = dgring.next()
            dma("sp", slot[:].rearrange("p k n -> p (k n)"), dg_b[c * 128:(c + 1) * 128, :], [("dgb", c)], [("dgslot", s)], ("dgslot", s))
            return ("dgslot", s), slot

        def mm_group(bank_res, out_ap, pairs, extra_reads):
            n = len(pairs)
            pairs = [(l, r, [chk(x) if (isinstance(x, tuple) and x[0] == "wslot") else x for x in rd]) for (l, r, rd) in pairs]
            for i, (l, r, rd) in enumerate(pairs):
                ncol = nfree(r)
                P.op("pe", lambda e, l=l, r=r, i=i: e.matmul(out_ap, lhsT=l, rhs=r, start=(i == 0), stop=(i == n - 1)),
                     reads=list(rd) + list(extra_reads), writes=[bank_res], cost=max(64, ncol) / 1900.0 * (4.0 if r.dtype == F32 else 1.0))

        def prep(src, tile_idx, buf):
            r0 = tile_idx * T
            blocks = []
            for tb in range(2):
                xi, xb = xring.next()
                dma("pool", xb[:], src[r0 + tb * 128:r0 + (tb + 1) * 128, :], [], [("x", xi)], ("x", xi))
                hi, hsb = hs.next()
                act(junk[:], xb[:], AF.Square, reads=[("x", xi)], writes=["junk", ("ssq", tb)], accum_out=col(ssq, tb))
                blocks.append((xi, xb, hi, hsb))
            dve(lambda e: e.tensor_scalar(out=rs[:], in0=ssq[:], scalar1=1.0 / D, scalar2=EPS, op0=ALU.mult, op1=ALU.add),
                reads=[("ssq", 0), ("ssq", 1)], writes=["rs"], n=2, k=1)
            act(rs[:], rs[:], AF.Sqrt, reads=["rs"], writes=["rs"])
            dve(lambda e: e.reciprocal(out=rs[:], in_=rs[:]), reads=["rs"], writes=["rs"], n=2, k=8)
            pe_parts = []
            for tb, (xi, xb, hi, hsb) in enumerate(blocks):
                act(hsb[:], xb[:], AF.Copy, reads=[("x", xi), "rs"], writes=[("hs", hi)], scale=col(rs, tb))
                pe_parts.append((tb, hi, hsb))
            return pe_parts

        def prep_pe(pe_parts, buf, all_act=False):
            for (tb, hi, hsb) in pe_parts:
                for q in range(4):
                    bres, bank = newbank()
                    bb = bank[:].bitcast(BF16)
                    for i in range(4):
                        fc = q * 4 + i
                        P.op("pe", lambda e, bb=bb, i=i, fc=fc, hsb=hsb: e.transpose(out=bb[:, i * 128:(i + 1) * 128],
                                                                                     in_=hsb[:, fc * 128:(fc + 1) * 128],
                                                                                     identity=ident[:]),
                             reads=[("hs", hi), "ident"], writes=[bres], cost=0.08)
                    for i in range(4):
                        fc = q * 4 + i
                        o_ap = hT[buf][:, fc, tb * 128:(tb + 1) * 128]
                        i_ap = bb[:, i * 128:(i + 1) * 128]
                        if q % 2 == 0 and not all_act:
                            dve(lambda e, o_ap=o_ap, i_ap=i_ap, fc=fc: e.tensor_scalar(out=o_ap, in0=i_ap, scalar1=col(par, P_G + fc),
                                                                                       scalar2=None, op0=ALU.mult),
                                reads=["par"], writes=[("hT", buf, fc)], xreads=[bres], n=128, k=1)
                        else:
                            act(o_ap, i_ap, AF.Copy, reads=["par"], writes=[("hT", buf, fc)], xreads=[bres], scale=col(par, P_G + fc))

        def hT_reads(buf):
            return [("hT", buf, fc) for fc in range(KC)]

        def rnn_stage(buf, main, mid_hook=None, late_n=0, fillers=None, nfill=2):
            hr = hT_reads(buf)
            wt = {}

            pend_cast = []

            def flush_cast():
                while pend_cast:
                    vi = pend_cast.pop(0)
                    if main:
                        act(vbb[vi][:], vbuf[vi][:], AF.Copy, reads=[("v", vi)], writes=[("vb", vi)])
                    else:
                        pool(lambda e, vi=vi: e.tensor_copy(out=vbb[vi][:], in_=vbuf[vi][:]), reads=[("v", vi)], writes=[("vb", vi)])

            def rx_chunk(c):
                lc = c % 5
                if lc % 2 == 0:
                    wt["rx"] = load_win(O_RX + c * 128, "rx", 128 if lc == 4 else 256)
                wres, wv = wt["rx"]
                bres, bank = newbank()
                o = (lc % 2) * 128
                mm_group(bres, bank[:, 0:T], [(wv[:, k, o:o + 128], hT[buf][:, k, :], [wres, ("hT", buf, k)]) for k in range(KC)], [])
                ri, rx = rxr.next()
                pace["n"] = pace.get("n", 0) + 1
                act(rx[:, 3:3 + T], bank[:, 0:T], AF.Identity, xreads=[bres], writes=[("rx", ri), ("pace", pace["n"])])
                dve(lambda e, rx=rx, c=c: e.tensor_copy(out=rx[:, 0:3], in_=rxh[:, c, :]), reads=[("rxh", c)], writes=[("rxhalo", ri)], n=3, k=1)
                flush_cast()
                vi = c % 10
                v = vbuf[vi]
                dve(lambda e, rx=rx, c=c: e.tensor_scalar(out=pvb[:, 0:T], in0=rx[:, 3:3 + T], scalar1=col(par, P_RW + c * RW + 3),
                                                          scalar2=col(par, P_RB + c), op0=ALU.mult, op1=ALU.add),
                    reads=[("rx", ri), "par"], writes=["pv"], k=1)
                for k in range(3):
                    last = (k == 2)
                    dve(lambda e, rx=rx, v=v, c=c, k=k, last=last: e.scalar_tensor_tensor(out=(v[:] if last else pvb[:, 0:T]), in0=rx[:, k:k + T],
                                                                                          scalar=col(par, P_RW + c * RW + k),
                                                                                          in1=pvb[:, 0:T], op0=ALU.mult, op1=ALU.add),
                        reads=[("rx", ri), ("rxhalo", ri), "par", "pv"], writes=([("v", vi)] if last else ["pv"]), k=1)
                dve(lambda e, rx=rx, c=c: e.tensor_copy(out=rxh[:, c, :], in_=rx[:, T:T + 3]), reads=[("rx", ri)], writes=[("rxh", c)], n=3, k=1)
                pend_cast.append(vi)

            def gates_block(g):
                gres, gv = load_gate(g)
                items = []
                for oc in range(5):
                    c = g * 5 + oc
                    rec = {"c": c}
                    for gate in range(2):
                        bres, bank = newbank()
                        pairs = []
                        for ic in GATE_ICS[oc]:
                            vi = (g * 5 + ic) % 10
                            pairs.append((gv[:, GATE_BLK[(gate, oc, ic)], :], vbb[vi][:], [gres, ("vb", vi)]))
                        mm_group(bres, bank[:, 0:T], pairs, [])
                        rec[gate] = (bres, bank)
                    ai, A = bufA.next()
                    bi, B = bufB.next()
                    ci, C = bufC.next()
                    rec.update(A=A, ai=ai, B=B, bi=bi, C=C, ci=ci)
                    bres, bank = rec[0]
                    act(A[:], bank[:, 0:T], AF.Tanh, xreads=[bres], reads=DER, writes=[("A", ai)], scale=0.5, bias=col(der, Q_HBA + c))
                    bres, bank = rec[1]
                    act(C[:], bank[:, 0:T], AF.Tanh, xreads=[bres], reads=DER, writes=[("C", ci)], scale=0.5, bias=col(der, Q_HBX + c))
                    act(B[:], A[:], AF.Exp, reads=[("A", ai)] + DER, writes=[("B", bi)], scale=col(der, Q_HC + c), bias=col(der, Q_HC + c))
                    pool(lambda e, A=A, B=B: e.tensor_tensor(out=A[:], in0=B[:], in1=B[:], op=ALU.mult), reads=[("B", bi)], writes=[("A", ai)])
                    vi_ = c % 10
                    if main:
                        dve(lambda e, C=C, vv_=vbuf[vi_]: e.scalar_tensor_tensor(out=C[:], in0=C[:], scalar=1.0, in1=vv_[:], op0=ALU.add, op1=ALU.mult),
                            reads=[("C", ci), ("v", vi_)], writes=[("C", ci)])
                    else:
                        pool(lambda e, C=C: e.tensor_scalar(out=C[:], in0=C[:], scalar1=1.0, scalar2=1.0, op0=ALU.add, op1=ALU.mult),
                             reads=[("C", ci)], writes=[("C", ci)])
                        pool(lambda e, C=C, vv_=vbuf[vi_]: e.tensor_tensor(out=C[:], in0=C[:], in1=vv_[:], op=ALU.mult),
                             reads=[("C", ci), ("v", vi_)], writes=[("C", ci)])
                    items.append(rec)
                if main:
                    for oc in range(5):
                        rec = items[oc]
                        c = rec["c"]
                        lc = c % 5
                        if lc % 2 == 0:
                            wt["rg"] = load_win(O_RGATE + c * 128, "rgate", 128 if lc == 4 else 256)
                        wres, wv = wt["rg"]
                        bres, bank = newbank()
                        o = (lc % 2) * 128
                        mm_group(bres, bank[:, 0:T], [(wv[:, k, o:o + 128], hT[buf][:, k, :], [wres, ("hT", buf, k)]) for k in range(KC)], [])
                        ei, E = bufE.next()
                        act(E[:], bank[:, 0:T], AF.Tanh, xreads=[bres], writes=[("E", ei)], scale=0.5)
                        dve(lambda e, E=E, bank=bank: e.scalar_tensor_tensor(out=E[:], in0=E[:], scalar=1.0, in1=bank[:, 0:T],
                                                                             op0=ALU.add, op1=ALU.mult),
                            reads=[("E", ei)], writes=[("E", ei)], xreads=[bres])
                        rec.update(E=E, ei=ei)
                for rec in items:
                    A, ai = rec["A"], rec["ai"]
                    act(A[:], A[:], AF.Sqrt, reads=[("A", ai)], writes=[("A", ai)], scale=-1.0, bias=1.0)
                for rec in items:
                    c = rec["c"]
                    A, ai, B, bi, C, ci = rec["A"], rec["ai"], rec["B"], rec["bi"], rec["C"], rec["ci"]
                    vi = c % 10
                    v = vbuf[vi]
                    dve(lambda e, C=C, A=A: e.scalar_tensor_tensor(out=C[:], in0=A[:], scalar=0.0, in1=C[:], op0=ALU.max, op1=ALU.mult),
                        reads=[("C", ci), ("A", ai)], writes=[("C", ci)])
                    di, Dd = bufD.next()
                    dve(lambda e, Dd=Dd, B=B, C=C, c=c: e.tensor_tensor_scan(out=Dd[:], data0=B[:], data1=C[:], initial=col(state, c),
                                                                             op0=ALU.mult, op1=ALU.add),
                        reads=[("B", bi), ("C", ci), ("state", c)], writes=[("D", di)])
                    dve(lambda e, Dd=Dd, c=c: e.tensor_copy(out=col(state, c), in_=Dd[:, T - 1:T]), reads=[("D", di)], writes=[("state", c)], n=1, k=1)
                    if main:
                        E, ei = rec["E"], rec["ei"]
                        dve(lambda e, Dd=Dd, E=E, c=c: e.tensor_tensor(out=hg[:, c, :], in0=Dd[:], in1=E[:], op=ALU.mult),
                            reads=[("D", di), ("E", ei)], writes=[("hg", c)])

            for g in range(4):
                for c in range(g * 5, g * 5 + 5):
                    rx_chunk(c)
                flush_cast()
                if late_n:
                    pace["res"] = ("pace", pace["n"])
                    emit_late(late_n)
                if g == 2 and mid_hook is not None:
                    mid_hook()
                if fillers:
                    for _ in range(nfill):
                        if fillers:
                            fillers.pop(0)()
                if g >= 1:
                    gates_block(g - 1)
            gates_block(3)

        def conv_halo(buf):
            hr = hT_reads(buf)
            wt = {}
            for c in range(NCC):
                if c % 2 == 0:
                    wt["g"] = load_win(O_CGLU + c * 128, "cglu")
                    wt["v"] = load_win(O_CVAL + c * 128, "cval")
                o = (c % 2) * 128
                gres, gv = wt["g"]
                vres, vv = wt["v"]
                b1, bk1 = newbank()
                mm_group(b1, bk1[:, 0:32], [(gv[:, k, o:o + 128], hT[buf][:, k, T - 32:T], [gres, ("hT", buf, k)]) for k in range(KC)], [])
                b2, bk2 = newbank()
                mm_group(b2, bk2[:, 0:32], [(vv[:, k, o:o + 128], hT[buf][:, k, T - 32:T], [vres, ("hT", buf, k)]) for k in range(KC)], [])
                ti, tt = tbuf.next()
                act(tt[:, 0:32], bk1[:, 0:32], AF.Tanh, xreads=[b1], writes=[("t", ti)], scale=0.5)
                dve(lambda e, tt=tt, bk2=bk2, c=c: e.scalar_tensor_tensor(out=uh[:, c, :], in0=tt[:, 2:32], scalar=1.0, in1=bk2[:, 2:32],
                                                                          op0=ALU.add, op1=ALU.mult),
                    reads=[("t", ti)], writes=[("uh", c)], xreads=[b2])

        def mask_state():
            dve(lambda e: e.tensor_scalar(out=state[:], in0=state[:], scalar1=flag[:, 0:1], scalar2=None, op0=ALU.mult),
                reads=["flag"] + [("state", c) for c in range(NRC)], writes=[("state", c) for c in range(NRC)])
            dve(lambda e: e.tensor_scalar(out=rxh[:].rearrange("p c k -> p (c k)"), in0=rxh[:].rearrange("p c k -> p (c k)"),
                                          scalar1=flag[:, 0:1], scalar2=None, op0=ALU.mult),
                reads=["flag"] + [("rxh", c) for c in range(NRC)], writes=[("rxh", c) for c in range(NRC)])
            dve(lambda e: e.tensor_scalar(out=uh[:].rearrange("p c k -> p (c k)"), in0=uh[:].rearrange("p c k -> p (c k)"),
                                          scalar1=flag[:, 0:1], scalar2=None, op0=ALU.mult),
                reads=["flag"] + [("uh", c) for c in range(NCC)], writes=[("uh", c) for c in range(NCC)])

        def conv_stage(buf):
            hr = hT_reads(buf)
            wt = {}
            pend = []

            def gv_chunk(c):
                if c % 2 == 0:
                    wt["g"] = load_win(O_CGLU + c * 128, "cglu")
                    wt["v"] = load_win(O_CVAL + c * 128, "cval")
                o = (c % 2) * 128
                gres, gv_ = wt["g"]
                vres, vv = wt["v"]
                b1, bk1 = newbank()
                mm_group(b1, bk1[:, 0:T], [(gv_[:, k, o:o + 128], hT[buf][:, k, :], [gres, ("hT", buf, k)]) for k in range(KC)], [])
                b2, bk2 = newbank()
                mm_group(b2, bk2[:, 0:T], [(vv[:, k, o:o + 128], hT[buf][:, k, :], [vres, ("hT", buf, k)]) for k in range(KC)], [])
                ti, tt = tbuf.next()
                act(tt[:], bk1[:, 0:T], AF.Tanh, xreads=[b1], writes=[("t", ti)], scale=0.5)
                ui, ub = ubuf.next()
                dve(lambda e, tt=tt, bk2=bk2, ub=ub: e.scalar_tensor_tensor(out=ub[:, 30:30 + T], in0=tt[:], scalar=1.0, in1=bk2[:, 0:T],
                                                                            op0=ALU.add, op1=ALU.mult),
                    reads=[("t", ti)], writes=[("u", ui)], xreads=[b2])
                pool(lambda e, ub=ub, c=c: e.tensor_copy(out=ub[:, 0:30], in_=uh[:, c, :]), reads=[("uh", c)], writes=[("uhalo", ui)])
                pool(lambda e, ub=ub, c=c: e.tensor_copy(out=uh[:, c, :], in_=ub[:, T:T + 30]), reads=[("u", ui)], writes=[("uh", c)])
                return ui, ub

            def conv_chunk(c, ui, ub):
                dres, dgv = load_dg(c)
                b3, bk3 = newbank()
                mm_group(b3, bk3[:, 0:T], [(dgv[:, k, :], ub[:, k:k + T], [dres]) for k in range(CW)], [("u", ui), ("uhalo", ui)])
                act(cv[:, c, :], bk3[:, 0:T], AF.Identity, xreads=[b3], reads=["par"], writes=[("cv", c), ("mg", 2 * c), ("mg", 2 * c + 1)],
                    bias=col(par, P_CB + c))
                si, sq = sqt.next()
                act(sq[:], cv[:, c, :], AF.Square, reads=[("cv", c)], writes=[("sq", si)])
                if c == 0:
                    pool(lambda e: e.tensor_copy(out=acc1[:], in_=cv[:, 0, :]), reads=[("cv", 0)], writes=["acc1"])
                    pool(lambda e, sq=sq: e.tensor_copy(out=acc2[:], in_=sq[:]), reads=[("sq", si)], writes=["acc2"])
                else:
                    pool(lambda e, c=c: e.tensor_tensor(out=acc1[:], in0=acc1[:], in1=cv[:, c, :], op=ALU.add), reads=[("cv", c), "acc1"], writes=["acc1"])
                    pool(lambda e, sq=sq: e.tensor_tensor(out=acc2[:], in0=acc2[:], in1=sq[:], op=ALU.add), reads=[("sq", si), "acc2"], writes=["acc2"])

            def step_pair(c0):
                for c in (c0, c0 + 1):
                    ui, ub = gv_chunk(c)
                    pend.append((c, ui, ub))
                    if len(pend) > 1:
                        conv_chunk(*pend.pop(0))

            def rest():
                conv_chunk(*pend.pop(0))
                conv_rest(buf)

            return step_pair, rest

        def conv_rest(buf):
            wt = {}
            b1, bk1 = newbank()
            P.op("pe", lambda e: e.matmul(bk1[:, 0:T], lhsT=onesf[:], rhs=acc1[:], start=True, stop=True), reads=["onesf", "acc1"], writes=[b1], cost=0.6)
            b2, bk2 = newbank()
            P.op("pe", lambda e: e.matmul(bk2[:, 0:T], lhsT=onesf[:], rhs=acc2[:], start=True, stop=True), reads=["onesf", "acc2"], writes=[b2], cost=0.6)
            dve(lambda e: e.tensor_scalar(out=lnM[:], in0=bk1[:, 0:T], scalar1=1.0 / DC, scalar2=None, op0=ALU.mult), xreads=[b1], writes=["lnM"])
            dve(lambda e: e.tensor_tensor(out=lnN[:], in0=lnM[:], in1=lnM[:], op=ALU.mult), reads=["lnM"], writes=["lnN"])
            dve(lambda e: e.scalar_tensor_tensor(out=lnR[:], in0=bk2[:, 0:T], scalar=1.0 / DC, in1=lnN[:], op0=ALU.mult, op1=ALU.subtract),
                reads=["lnN"], writes=["lnR"], xreads=[b2])
            dve(lambda e: e.tensor_scalar(out=lnR[:], in0=lnR[:], scalar1=0.0, scalar2=LN_EPS, op0=ALU.max, op1=ALU.add), reads=["lnR"], writes=["lnR"])
            act(lnR[:], lnR[:], AF.Sqrt, reads=["lnR"], writes=["lnR"])
            dve(lambda e: e.reciprocal(out=lnR[:], in_=lnR[:]), reads=["lnR"], writes=["lnR"], k=8)
            dve(lambda e: e.scalar_tensor_tensor(out=lnN[:], in0=lnM[:], scalar=-1.0, in1=lnR[:], op0=ALU.mult, op1=ALU.mult),
                reads=["lnM", "lnR"], writes=["lnN"])
            for c in range(NCC):
                if c % 2 == 0:
                    wt["cg"] = load_win(O_CGATE + c * 128, "cgate")
                o = (c % 2) * 128
                wres, wv = wt["cg"]
                b3, bk3 = newbank()
                mm_group(b3, bk3[:, 0:T], [(wv[:, k, o:o + 128], hT[buf][:, k, :], [wres, ("hT", buf, k)]) for k in range(KC)], [])
                t2i, t2 = t2buf.next()
                act(t2[:], bk3[:, 0:T], AF.Tanh, xreads=[b3], writes=[("t2", t2i)], scale=0.5)
                dve(lambda e, t2=t2, bk3=bk3: e.scalar_tensor_tensor(out=t2[:], in0=t2[:], scalar=1.0, in1=bk3[:, 0:T], op0=ALU.add, op1=ALU.mult),
                    reads=[("t2", t2i)], writes=[("t2", t2i)], xreads=[b3])
                zi, z = zbuf.next()
                pool(lambda e, z=z, c=c: e.tensor_tensor(out=z[:], in0=cv[:, c, :], in1=lnR[:], op=ALU.mult), reads=[("cv", c), "lnR"], writes=[("z", zi)])
                pool(lambda e, z=z: e.tensor_tensor(out=z[:], in0=z[:], in1=lnN[:], op=ALU.add), reads=[("z", zi), "lnN"], writes=[("z", zi)])
                dve(lambda e, z=z, c=c: e.tensor_scalar(out=z[:], in0=z[:], scalar1=col(par, P_LG + c), scalar2=col(par, P_LB + c),
                                                        op0=ALU.mult, op1=ALU.add), reads=[("z", zi), "par"], writes=[("z", zi)])
                t1i, t1 = t1buf.next()
                act(t1[:], z[:], AF.Tanh, reads=[("z", zi)], writes=[("t1", t1i)], scale=0.5)
                dve(lambda e, z=z, t1=t1: e.scalar_tensor_tensor(out=z[:], in0=t1[:], scalar=1.0, in1=z[:], op0=ALU.add, op1=ALU.mult),
                    reads=[("t1", t1i), ("z", zi)], writes=[("z", zi)])
                dve(lambda e, z=z, t2=t2, c=c: e.tensor_tensor(out=ug[:, c, :], in0=z[:], in1=t2[:], op=ALU.mult),
                    reads=[("z", zi), ("t2", t2i)], writes=[("ug", c)])

        def merge_stage(buf):
            hr = hT_reads(buf)
            for mp in range(KC // 2):
                ms = (2 * mp, 2 * mp + 1)
                recs = {m: {} for m in ms}
                for key, col0, wbn in (("gc", O_GCONV, "gconv"), ("gr", O_GRNN, "grnn")):
                    wres, wv = load_win(col0 + ms[0] * 128, wbn)
                    for j, m in enumerate(ms):
                        b1, bk1 = newbank()
                        mm_group(b1, bk1[:, 0:T], [(wv[:, k, j * 128:(j + 1) * 128], hT[buf][:, k, :], [wres, ("hT", buf, k)]) for k in range(KC)], [])
                        ring = mt1 if key == "gc" else mt2
                        i1, m1 = ring.next()
                        act(m1[:], bk1[:, 0:T], AF.Tanh, xreads=[b1], writes=[(key, i1)], scale=0.5)
                        recs[m][key] = (i1, m1)
                for key, gkey, src, nk, wbn, rname in (("yr", "gr", hg, NRC, "wro", "hg"), ("yc", "gc", ug, NCC, "wco", "ug")):
                    wres, wv = load_tile(wbn, mp)
                    for j, m in enumerate(ms):
                        b3, bk3 = newbank()
                        mm_group(b3, bk3[:, 0:T], [(wv[:, k, j * 128:(j + 1) * 128], src[:, k, :], [wres, (rname, k)]) for k in range(nk)], [])
                        i1, m1 = recs[m][gkey]
                        dve(lambda e, m1=m1, bk3=bk3: e.scalar_tensor_tensor(out=m1[:], in0=m1[:], scalar=1.0, in1=bk3[:, 0:T], op0=ALU.add, op1=ALU.mult),
                            reads=[(gkey, i1)], writes=[(gkey, i1)], xreads=[b3])
                for m in ms:
                    i1, m1 = recs[m]["gc"]
                    i2, m2 = recs[m]["gr"]
                    dve(lambda e, m1=m1, m2=m2, m=m: e.tensor_tensor(out=mg[:, m, :], in0=m1[:], in1=m2[:], op=ALU.add),
                        reads=[("gc", i1), ("gr", i2)], writes=[("mg", m), ("cv", m // 2)])

        def out_stage(tile_idx):
            r0 = tile_idx * T
            xs = []
            for tb in range(2):
                xi, xb = xring.next()
                dma("pool", xb[:], xm[r0 + tb * 128:r0 + (tb + 1) * 128, :], [], [("x", xi)], ("x", xi))
                xs.append((xi, xb))
            for cg in range(8):
                wres, wv = load_tile("wo", cg)
                for tb in range(2):
                    xi, xb = xs[tb]
                    b1, bk1 = newbank()
                    mm_group(b1, bk1[:, 0:256], [(mg[:, k, tb * 128:(tb + 1) * 128], wv[:, k, :], [wres, ("mg", k)]) for k in range(KC)], [])
                    dve(lambda e, xb=xb, bk1=bk1, cg=cg: e.scalar_tensor_tensor(out=xb[:, cg * 256:(cg + 1) * 256], in0=bk1[:, 0:256], scalar=0.125,
                                                                               in1=xb[:, cg * 256:(cg + 1) * 256], op0=ALU.mult, op1=ALU.add),
                        reads=[("x", xi)], writes=[("x", xi)], xreads=[b1])
            for tb in range(2):
                xi, xb = xs[tb]
                act(junk[:], xb[:], AF.Square, reads=[("x", xi)], writes=["junk", ("ssq2", tb)], accum_out=col(ssq2, tb))
            dve(lambda e: e.tensor_scalar(out=rs2[:], in0=ssq2[:], scalar1=1.0 / D, scalar2=EPS, op0=ALU.mult, op1=ALU.add),
                reads=[("ssq2", 0), ("ssq2", 1)], writes=["rs2"], n=2, k=1)
            act(rs2[:], rs2[:], AF.Sqrt, reads=["rs2"], writes=["rs2"])
            dve(lambda e: e.reciprocal(out=rs2[:], in_=rs2[:]), reads=["rs2"], writes=["rs2"], n=2, k=8)
            outs = []
            for tb in range(2):
                xi, xb = xs[tb]
                dve(lambda e, xb=xb, tb=tb: e.scalar_tensor_tensor(out=xb[:], in0=xb[:], scalar=col(rs2, tb), in1=fg[:], op0=ALU.mult, op1=ALU.mult),
                    reads=[("x", xi), "rs2", "fg"], writes=[("x", xi)], n=D, k=2)
                outs.append(dma("pool", y[r0 + tb * 128:r0 + (tb + 1) * 128, :], xb[:], [("x", xi)], [], ("xo", xi)))
            return outs

        final = []
        diag_plan = {}
        if ntl == 1:
            diag_plan[0] = list(range(NCC))
        else:
            per = -(-NCC // (ntl - 1))
            per = max(per, 2)
            cc_ = 0
            for t_ in range(0, ntl - 1):
                diag_plan[t_] = list(range(cc_, min(NCC, cc_ + per)))
                cc_ += per
        for c in range(NCC):
            build_diag(c)
        n_groups = 4 * ntl
        late_per_group = -(-len(late_pieces) // max(1, n_groups - 6))
        pp = prep(xp, 0, 0)
        early_casts()
        prep_pe(pp, 0)
        cur = 0
        for t in range(ntl):
            nxt = 1 - cur
            if t + 1 < ntl:
                pp = prep(xp, t + 1, nxt)
            else:
                pp = prep(xm, 0, nxt)
            rnn_stage(cur, main=False, mid_hook=lambda pp=pp, nxt=nxt: prep_pe(pp, nxt, all_act=True), late_n=late_per_group)
            if t == ntl - 1:
                conv_halo(cur)
            cur = nxt
        emit_late(len(late_pieces))
        P.cutoff = len(P.ops)
        mask_state()
        for t in range(ntl):
            nxt = 1 - cur
            step_pair, conv_finish = conv_stage(cur)
            fillers = [lambda c0=c0: step_pair(c0) for c0 in range(0, NCC, 2)]
            rnn_stage(cur, main=True, fillers=fillers, nfill=FILL_N)
            if t + 1 < ntl:
                pp = prep(xm, t + 1, nxt)
            while fillers:
                fillers.pop(0)()
            conv_finish()
            merge_stage(cur)
            if t + 1 < ntl:
                prep_pe(pp, nxt)
            final += out_stage(t)
            cur = nxt
        P.final_dma = final
        P.build(st)
    return nc


def _pack_params(norm_g, conv_dw_w, conv_dw_b, conv_ln_g, conv_ln_b, rnn_conv_w, rnn_conv_b, b_rg_a, b_rg_x, rg_lambda):
    par = np.zeros((128, NPAR), np.float32)
    par[:, P_G:P_G + 16] = norm_g.reshape(16, 128).T
    par[:, P_CW:P_CW + 496] = conv_dw_w.reshape(CW, NCC, 128).transpose(2, 1, 0).reshape(128, NCC * CW)
    par[:, P_CB:P_CB + 16] = conv_dw_b.reshape(16, 128).T
    par[:, P_LG:P_LG + 16] = conv_ln_g.reshape(16, 128).T
    par[:, P_LB:P_LB + 16] = conv_ln_b.reshape(16, 128).T
    par[:, P_RW:P_RW + 80] = rnn_conv_w.reshape(RW, NRC, 128).transpose(2, 1, 0).reshape(128, NRC * RW)
    par[:, P_RB:P_RB + 20] = rnn_conv_b.reshape(20, 128).T
    par[:, P_BA:P_BA + 20] = b_rg_a.reshape(20, 128).T
    par[:, P_BX:P_BX + 20] = b_rg_x.reshape(20, 128).T
    par[:, P_LAM:P_LAM + 20] = rg_lambda.reshape(20, 128).T
    return par


def _pack_gates(w_a, w_x):
    out = np.zeros((4, 128, NBLK, 128), np.float32)
    for gate, w in enumerate((w_a, w_x)):
        for g in range(4):
            dense = np.zeros((640, 640), np.float32)
            for hh in range(4):
                dense[hh * 160:(hh + 1) * 160, hh * 160:(hh + 1) * 160] = w[g * 4 + hh]
            for oc in range(5):
                for ic in GATE_ICS[oc]:
                    out[g, :, GATE_BLK[(gate, oc, ic)], :] = dense[ic * 128:(ic + 1) * 128, oc * 128:(oc + 1) * 128]
    return out.reshape(4 * 128, NBLK * 128)


_NC_CACHE = {}


def kernel(x, norm_g, w_in, conv_dw_w, conv_dw_b, conv_ln_g, conv_ln_b, w_conv_out,
           rnn_conv_w, rnn_conv_b, w_rg_a, b_rg_a, w_rg_x, b_rg_x, rg_lambda,
           w_rnn_out, w_out, final_norm_g):
    f = lambda a: np.ascontiguousarray(np.asarray(a, dtype=np.float32))
    x = f(x)
    par = _pack_params(f(norm_g)[0], f(conv_dw_w)[0], f(conv_dw_b)[0], f(conv_ln_g)[0], f(conv_ln_b)[0],
                       f(rnn_conv_w)[0], f(rnn_conv_b)[0], f(b_rg_a)[0], f(b_rg_x)[0], f(rg_lambda)[0])
    gwp = _pack_gates(f(w_rg_a)[0], f(w_rg_x)[0])
    fgb = np.ascontiguousarray(np.broadcast_to(f(final_norm_g)[None, :], (128, D)))
    shared = {"params": par, "fg": fgb, "w_in": f(w_in)[0], "w_co": f(w_conv_out)[0], "w_ro": f(w_rnn_out)[0],
              "w_o": f(w_out)[0], "gw": gwp}
    in_maps = []
    for c in range(8):
        b, half = c // 2, c % 2
        m = dict(shared)
        m["xm"] = np.ascontiguousarray(x[b, half * TOK:(half + 1) * TOK, :])
        m["xp"] = np.ascontiguousarray(x[b, 0:TOK, :])
        m["flag"] = np.full((128, 1), float(half), np.float32)
        in_maps.append(m)
    if "nc" not in _NC_CACHE:
        _NC_CACHE["nc"] = build_program()
    nc = _NC_CACHE["nc"]
    res = run_bass_kernel_spmd(nc, in_maps, core_ids=list(range(8)))
    out = np.empty((4, 2 * TOK, D), np.float32)
    for c in range(8):
        b, half = c // 2, c % 2
        out[b, half * TOK:(half + 1) * TOK, :] = res.results[c]["y"]
    return out
```

```python
from contextlib import ExitStack
import numpy as np
import concourse.bass as bass
import concourse.mybir as mybir
from concourse.bass_utils import run_bass_kernel_spmd

F32 = mybir.dt.float32
BF16 = mybir.dt.bfloat16
AF = mybir.ActivationFunctionType
ALU = mybir.AluOpType

D = 2048
DC = 2048
DR = 2560
DIN = 15360
NCC = 16
NRC = 20
KC = 16
CW = 31
RW = 4
T = 256
FILL_N = 2
TOK = 4096
NT = TOK // T
EPS = 1e-6
LN_EPS = 1e-5

O_CVAL, O_CGLU, O_CGATE, O_RX, O_RGATE, O_GCONV, O_GRNN = 0, 2048, 4096, 6144, 8704, 11264, 13312

P_G, P_CW, P_CB, P_LG, P_LB, P_RW, P_RB, P_BA, P_BX, P_LAM = 0, 16, 512, 528, 544, 560, 640, 660, 680, 700
NPAR = 720
Q_HC, Q_CC, Q_HBA, Q_HBX, Q_HLG, Q_HLB = 0, 20, 40, 60, 80, 96
NDER = 112

GATE_ICS = {0: (0, 1), 1: (0, 1, 2), 2: (1, 2, 3), 3: (2, 3, 4), 4: (3, 4)}
GATE_BLK = {}
_b = 0
for _gate in range(2):
    for _oc in range(5):
        for _ic in GATE_ICS[_oc]:
            GATE_BLK[(_gate, _oc, _ic)] = _b
            _b += 1
NBLK = _b

ENGS = ("pe", "act", "dve", "pool", "sp")


class Op:
    __slots__ = ("eng", "fn", "reads", "writes", "xreads", "dma_key", "deps", "milestone", "cnt", "idx", "sem", "cost", "table", "alldeps", "pos")


class Prog:
    def __init__(self, nc):
        self.nc = nc
        self.ops = []
        self.last_writer = {}
        self.readers = {}
        self.final_dma = []
        self.cutoff = None

    def op(self, eng, fn, reads=(), writes=(), xreads=(), dma_key=None, cost=0.3, table=None):
        o = Op()
        o.cost, o.table = cost, table
        o.eng, o.fn, o.dma_key = eng, fn, dma_key
        o.reads, o.writes, o.xreads = tuple(reads), tuple(writes), tuple(xreads)
        o.milestone, o.cnt, o.sem = False, None, None
        o.idx = len(self.ops)
        deps = set()
        for r in o.reads:
            lw = self.last_writer.get(r)
            if lw is not None:
                deps.add(lw)
        for w in o.writes + o.xreads:
            lw = self.last_writer.get(w)
            if lw is not None:
                deps.add(lw)
            rd = self.readers.get(w)
            if rd:
                deps.update(rd)
        for r in o.reads:
            self.readers.setdefault(r, []).append(o.idx)
        for w in o.writes + o.xreads:
            self.last_writer[w] = o.idx
            self.readers[w] = []
        deps.discard(o.idx)
        o.deps = deps
        self.ops.append(o)
        return o

    def schedule(self, window=4000, cutoff=None):
        ops = self.ops
        n = len(ops)
        ndeps = [len(o.alldeps) for o in ops]
        users = [[] for _ in range(n)]
        for o in ops:
            for d in o.alldeps:
                users[d].append(o.idx)
        ready = [0.0] * n
        finish = [0.0] * n
        cand = {e: [] for e in ENGS}
        for o in ops:
            if ndeps[o.idx] == 0:
                cand[o.eng].append(o.idx)
        t_eng = {e: 0.0 for e in ENGS}
        self.t0 = [0.0] * n
        order = {e: [] for e in ENGS}
        act_table = None
        dma_free = 0.0
        done = [False] * n
        low = 0
        nsched = 0
        if cutoff is None:
            cutoff = n
        po = {e: [o.idx for o in ops if o.eng == e and o.idx >= cutoff] for e in ENGS}
        po_ptr = {e: 0 for e in ENGS}
        while nsched < n:
            while low < n and done[low]:
                low += 1
            lim = low + window
            best = None
            best_key = None
            for e in ENGS:
                cl = cand[e]
                if not cl:
                    continue
                te = t_eng[e]
                nxt_po = po[e][po_ptr[e]] if po_ptr[e] < len(po[e]) else -1
                for i in cl:
                    if i > lim:
                        continue
                    if i >= cutoff and i != nxt_po:
                        continue
                    st = ready[i] if ready[i] > te else te
                    if e == "act":
                        tb = ops[i].table
                        if tb is not None and tb != act_table:
                            st += 1.4
                    key = (st, i)
                    if best_key is None or key < best_key:
                        best_key = key
                        best = (e, i)
            e, i = best
            o = ops[i]
            st = best_key[0]
            if o.dma_key is not None:
                issue = 0.08 if e == "sp" else 0.7
                t_eng[e] = st + issue
                xfer = o.cost / 250000.0
                s0 = max(st + issue, dma_free)
                dma_free = s0 + xfer
                finish[i] = s0 + xfer + 2.0
            else:
                if e == "act" and o.table is not None:
                    act_table = o.table
                t_eng[e] = st + o.cost
                finish[i] = st + o.cost + (0.15 if e == "pe" else 0.05)
            done[i] = True
            nsched += 1
            self.t0[i] = st
            if i >= cutoff:
                po_ptr[e] += 1
            cand[e].remove(i)
            o.pos = len(order[e])
            order[e].append(i)
            for u in users[i]:
                ndeps[u] -= 1
                if finish[i] > ready[u]:
                    ready[u] = finish[i]
                if ndeps[u] == 0:
                    cand[ops[u].eng].append(u)
        self.sim_time = max(finish) if n else 0.0
        return order

    def build(self, stack, reorder=True):
        nc = self.nc
        ops = self.ops
        for o in ops:
            o.alldeps = set(o.deps)
        if reorder:
            order = self.schedule(cutoff=self.cutoff)
        else:
            order = {e: [o.idx for o in ops if o.eng == e] for e in ENGS}
            for e in ENGS:
                for p_, i in enumerate(order[e]):
                    ops[i].pos = p_
        for o in ops:
            best = {}
            for d in o.alldeps:
                p = ops[d]
                if p.dma_key is None and o.dma_key is None and p.eng == o.eng:
                    if p.eng == "pe":
                        continue
                    touched = set(o.reads) | set(o.writes) | set(o.xreads)
                    war = (set(p.reads) | set(p.xreads)) & set(o.writes)
                    if not (set(p.writes) & touched) and not war:
                        continue
                k = p.dma_key if p.dma_key is not None else p.eng
                if k not in best or ops[best[k]].pos < p.pos:
                    best[k] = d
            o.deps = sorted(best.values())
            for d in o.deps:
                ops[d].milestone = True
        for o in ops:
            if o.dma_key is not None:
                o.milestone = True
        eng_sem, eng_cnt = {}, {}
        for e in ENGS:
            eng_sem[e] = stack.enter_context(nc.semaphore("sem_" + e))
            eng_cnt[e] = 0
        dma_sem, dma_cnt = {}, {}
        for e in ENGS:
            for i in order[e]:
                o = ops[i]
                if o.dma_key is None and o.milestone:
                    eng_cnt[o.eng] += 1
                    o.sem, o.cnt = eng_sem[o.eng], eng_cnt[o.eng]
        for o in ops:
            if o.dma_key is not None:
                if o.dma_key not in dma_sem:
                    dma_sem[o.dma_key] = stack.enter_context(nc.semaphore("dsem_%d" % len(dma_sem)))
                    dma_cnt[o.dma_key] = 0
                dma_cnt[o.dma_key] += 16
                o.sem, o.cnt = dma_sem[o.dma_key], dma_cnt[o.dma_key]
        for o in ops:
            if o.dma_key is not None and o.dma_key[0] == "castfull":
                o.cnt = dma_cnt[o.dma_key]
        self.n_sems = len(dma_sem) + len(ENGS)
        per_eng = {e: [ops[i] for i in order[e]] for e in ENGS}
        block = stack.enter_context(nc.Block())
        final = [(o.sem, o.cnt) for o in self.final_dma]

        def emit(e, engobj):
            known = {}
            for o in per_eng[e]:
                need = {}
                for d in o.deps:
                    p = ops[d]
                    k = id(p.sem)
                    if k not in need or need[k][1] < p.cnt:
                        need[k] = (p.sem, p.cnt)
                for k, (sem, cnt) in need.items():
                    if known.get(k, 0) >= cnt:
                        continue
                    engobj.wait_ge(sem, cnt)
                    known[k] = cnt
                ins = o.fn(engobj)
                if o.milestone:
                    ins.then_inc(o.sem, 16 if o.dma_key is not None else 1)
            if e == "sp":
                for sem, cnt in final:
                    engobj.wait_ge(sem, cnt)

        @block.tensor
        def _(eng):
            emit("pe", eng)

        @block.scalar
        def _(eng):
            emit("act", eng)

        @block.vector
        def _(eng):
            emit("dve", eng)

        @block.gpsimd
        def _(eng):
            emit("pool", eng)

        @block.sync
        def _(eng):
            emit("sp", eng)


class Ring:
    def __init__(self, items):
        self.items = items
        self.i = 0

    def next(self):
        k = self.i % len(self.items)
        self.i += 1
        return k, self.items[k]


def build_program(tok=TOK):
    nc = bass.Bass("TRN2", target_bir_lowering=False)
    ntl = tok // T
    dt = nc.dram_tensor
    xm = dt("xm", [tok, D], F32, kind="ExternalInput").ap()
    xp = dt("xp", [tok, D], F32, kind="ExternalInput").ap()
    flag_d = dt("flag", [128, 1], F32, kind="ExternalInput").ap()
    par_d = dt("params", [128, NPAR], F32, kind="ExternalInput").ap()
    fg_d = dt("fg", [128, D], F32, kind="ExternalInput").ap()
    w_in = dt("w_in", [D, DIN], F32, kind="ExternalInput").ap()
    w_co = dt("w_co", [DC, D], F32, kind="ExternalInput").ap()
    w_ro = dt("w_ro", [DR, D], F32, kind="ExternalInput").ap()
    w_o = dt("w_o", [D, D], F32, kind="ExternalInput").ap()
    gw = dt("gw", [4 * 128, NBLK * 128], F32, kind="ExternalInput").ap()
    y = dt("y", [tok, D], F32, kind="ExternalOutput").ap()
    def _grp_tiles(base):
        tl = []
        for g in range(4):
            tl += [(base + g * 640, 256), (base + g * 640 + 256, 256), (base + g * 640 + 512, 128)]
        return tl
    FAM = {}
    for _nm, _base in (("rx", O_RX), ("rgate", O_RGATE)):
        FAM[_nm] = (dt(_nm + "_b", [12 * 128, KC * 256], BF16, kind="Internal").ap(), w_in, KC, _grp_tiles(_base))
    for _nm, _base in (("cglu", O_CGLU), ("cval", O_CVAL), ("cgate", O_CGATE), ("gconv", O_GCONV), ("grnn", O_GRNN)):
        FAM[_nm] = (dt(_nm + "_b", [8 * 128, KC * 256], BF16, kind="Internal").ap(), w_in, KC, [(_base + i * 256, 256) for i in range(8)])
    FAM["wro"] = (dt("wro_b", [8 * 128, NRC * 256], BF16, kind="Internal").ap(), w_ro, NRC, [(i * 256, 256) for i in range(8)])
    FAM["wco"] = (dt("wco_b", [8 * 128, NCC * 256], BF16, kind="Internal").ap(), w_co, NCC, [(i * 256, 256) for i in range(8)])
    FAM["wo"] = (dt("wo_b", [8 * 128, KC * 256], BF16, kind="Internal").ap(), w_o, KC, [(i * 256, 256) for i in range(8)])
    gw_b = dt("gw_b", [4 * 128, NBLK * 128], BF16, kind="Internal").ap()
    dg_b = dt("dg_b", [NCC * 128, CW * 128], BF16, kind="Internal").ap()

    with ExitStack() as st:
        def sb(name, shape, dtype):
            return st.enter_context(nc.sbuf_tensor(name, shape, dtype))

        P = Prog(nc)

        par = sb("par", [128, NPAR], F32)
        der = sb("der", [128, NDER], F32)
        tmp20 = sb("tmp20", [128, 20], F32)
        flag = sb("flag_sb", [128, 1], F32)
        fg = sb("fg_sb", [128, D], F32)
        identf = sb("identf", [128, 128], F32)
        ident = sb("ident", [128, 128], BF16)
        onesf = sb("onesf", [128, 128], F32)
        NWS = 3
        wslots = [sb("wslot%d" % i, [128, 20 * 256], BF16) for i in range(NWS)]
        wring = Ring(wslots)
        dgslots = [sb("dgslot%d" % i, [128, CW, 128], BF16) for i in range(2)]
        dgring = Ring(dgslots)
        hT = [sb("hT%d" % i, [128, KC, T], BF16) for i in range(2)]
        hs = Ring([sb("hs%d" % i, [128, D], BF16) for i in range(2)])
        xring = Ring([sb("xblk%d" % i, [128, D], F32) for i in range(3)])
        junk = sb("junk", [128, D], BF16)
        ssq = sb("ssq", [128, 2], F32)
        rs = sb("rs", [128, 2], F32)
        ssq2 = sb("ssq2", [128, 2], F32)
        rs2 = sb("rs2", [128, 2], F32)
        cv = sb("cv", [128, NCC, T], F32)
        ug = sb("ug", [128, NCC, T], BF16)
        hg = sb("hg", [128, NRC, T], BF16)
        mg = cv[:].rearrange("p c t -> p (c t)")[:, 0:KC * T // 2].bitcast(BF16).rearrange("p (k t) -> p k t", t=T)
        ubuf = Ring([sb("ubuf%d" % i, [128, 30 + T], BF16) for i in range(3)])
        uh = sb("uh", [128, NCC, 30], BF16)
        rxr = Ring([sb("rx%d" % i, [128, 3 + T], F32) for i in range(2)])
        rxh = sb("rxh", [128, NRC, 3], F32)
        state = sb("state", [128, NRC], F32)
        vbuf = [sb("v%d" % i, [128, T], F32) for i in range(10)]
        vbb = [sb("vb%d" % i, [128, T], BF16) for i in range(10)]
        NG = 6
        bufA = Ring([sb("bA%d" % i, [128, T], F32) for i in range(NG)])
        bufB = Ring([sb("bB%d" % i, [128, T], F32) for i in range(NG)])
        bufC = Ring([sb("bC%d" % i, [128, T], F32) for i in range(NG)])
        bufD = Ring([sb("bD%d" % i, [128, T], F32) for i in range(2)])
        bufE = Ring([sb("bE%d" % i, [128, T], F32) for i in range(5)])
        tbuf = Ring([sb("tb%d" % i, [128, T], F32) for i in range(2)])
        sqt = Ring([sb("sq%d" % i, [128, T], F32) for i in range(2)])
        acc1 = sb("acc1", [128, T], F32)
        acc2 = sb("acc2", [128, T], F32)
        zbuf = Ring([sb("z%d" % i, [128, T], F32) for i in range(2)])
        t1buf = Ring([sb("t1%d" % i, [128, T], F32) for i in range(1)])
        t2buf = Ring([sb("t2%d" % i, [128, T], F32) for i in range(2)])
        lnM = sb("lnM", [128, T], F32)
        lnR = sb("lnR", [128, T], F32)
        lnN = sb("lnN", [128, T], F32)
        mt1 = Ring([sb("mt1%d" % i, [128, T], F32) for i in range(2)])
        mt2 = Ring([sb("mt2%d" % i, [128, T], F32) for i in range(2)])
        banks = [st.enter_context(nc.psum_tensor("bank%d" % i, [128, 512], F32)) for i in range(8)]
        bring = Ring(banks[0:7])
        pvb = banks[7]

        def newbank():
            k, b = bring.next()
            return ("ps", k), b

        def nfree(ap):
            n = 1
            for d in ap.shape[1:]:
                n *= int(d)
            return n

        def dma(eng, out, in_, reads, writes, key):
            nb = nfree(out) * int(out.shape[0]) * (2 if out.dtype == BF16 else 4)
            if in_.dtype != out.dtype:
                nb *= 3
            return P.op(eng, lambda e: e.dma_start(out=out, in_=in_), reads=reads, writes=writes, dma_key=key, cost=float(nb))

        TABLE = {AF.Tanh: "exp", AF.Exp: "exp", AF.Sqrt: "sqrt", AF.Ln: "ln"}

        def act(out, in_, func, reads=(), writes=(), xreads=(), **kw):
            n = nfree(out)
            cost = 0.2 + n / 1200.0 + (0.09 if not isinstance(kw.get("scale", 1.0), float) else 0.0) + (0.09 if not isinstance(kw.get("bias", 0.0), float) else 0.0)
            return P.op("act", lambda e: e.activation(out=out, in_=in_, func=func, **kw), reads=reads, writes=writes, xreads=xreads,
                        cost=cost, table=TABLE.get(func))

        def dve(fn, reads=(), writes=(), xreads=(), n=T, k=2.0):
            if xreads and k == 2.0:
                k = 1.0
            return P.op("dve", fn, reads=reads, writes=writes, xreads=xreads, cost=0.08 + k * n / 960.0)

        def pool(fn, reads=(), writes=(), n=T):
            return P.op("pool", fn, reads=reads, writes=writes, cost=0.35 + n / 350.0)

        def col(t, c):
            return t[:, c:c + 1]

        dma("sp", par[:], par_d, [], ["par"], "ld_par")
        dma("sp", flag[:], flag_d, [], ["flag"], "ld_flag")
        dma("sp", fg[:], fg_d, [], ["fg"], "ld_fg")
        pool(lambda e: e.memset(identf[:], 0.0), writes=["identf"])
        pool(lambda e: e.affine_select(out=identf[:], in_=identf[:], pattern=[[-1, 128]], compare_op=ALU.not_equal,
                                       fill=1.0, base=0, channel_multiplier=1), reads=["identf"], writes=["identf"])
        pool(lambda e: e.memset(onesf[:], 1.0), writes=["onesf"])
        pool(lambda e: e.memset(state[:], 0.0), writes=[("state", c) for c in range(NRC)])
        pool(lambda e: e.memset(rxh[:], 0.0), writes=[("rxh", c) for c in range(NRC)])
        pool(lambda e: e.memset(uh[:], 0.0), writes=[("uh", c) for c in range(NCC)])
        dve(lambda e: e.tensor_copy(out=ident[:], in_=identf[:]), reads=["identf"], writes=["ident"])
        act(tmp20[:], par[:, P_LAM:P_LAM + 20], AF.Exp, reads=["par"], writes=["tmp20"], scale=-1.0)
        act(tmp20[:], tmp20[:], AF.Ln, reads=["tmp20"], writes=["tmp20"], bias=1.0)
        dve(lambda e: e.tensor_scalar(out=der[:, Q_HC:Q_HC + 20], in0=tmp20[:], scalar1=-4.0, scalar2=None, op0=ALU.mult),
            reads=["tmp20"], writes=["der_hc"])
        dve(lambda e: e.tensor_scalar(out=der[:, Q_CC:Q_CC + 20], in0=tmp20[:], scalar1=-8.0, scalar2=None, op0=ALU.mult),
            reads=["tmp20"], writes=["der_cc"])
        for (q, p_, n) in ((Q_HBA, P_BA, 20), (Q_HBX, P_BX, 20), (Q_HLG, P_LG, 16), (Q_HLB, P_LB, 16)):
            dve(lambda e, q=q, p_=p_, n=n: e.tensor_scalar(out=der[:, q:q + n], in0=par[:, p_:p_ + n], scalar1=0.5,
                                                           scalar2=None, op0=ALU.mult), reads=["par"], writes=[("der", q)])
        DER = ["der_hc", "der_cc", ("der", Q_HBA), ("der", Q_HBX), ("der", Q_HLG), ("der", Q_HLB)]

        pace = {"res": None}

        def cast_grp(name, i):
            return (i // 3) if name == "rx" else (i if name == "gw" else 0)

        def cast_piece(name, i, dst, src):
            grp = cast_grp(name, i)
            dma("pool", dst, src, ([pace["res"]] if pace["res"] is not None else []), [("wb", name, i)], ("castfull", name, grp))

        def fam_piece(name, j):
            scr, src, nk, tiles = FAM[name]
            c0, w = tiles[j]
            dst = scr[j * 128:(j + 1) * 128, 0:nk * w].rearrange("p (k n) -> p k n", n=w)
            cast_piece(name, j, dst, src[:, c0:c0 + w].rearrange("(k p) n -> p k n", p=128))

        def early_casts():
            for g in range(4):
                for j in range(3):
                    fam_piece("rx", 3 * g + j)
                cast_piece("gw", g, gw_b[g * 128:(g + 1) * 128, :].rearrange("p (a n) -> p a n", n=256),
                           gw[g * 128:(g + 1) * 128, :].rearrange("p (a n) -> p a n", n=256))
        late_pieces = []
        for i in range(8):
            late_pieces.append(lambda i=i: fam_piece("cglu", i))
            late_pieces.append(lambda i=i: fam_piece("cval", i))
        for i in range(12):
            late_pieces.append(lambda i=i: fam_piece("rgate", i))
        for i in range(8):
            late_pieces.append(lambda i=i: fam_piece("cgate", i))
        for i in range(8):
            late_pieces.append(lambda i=i: fam_piece("gconv", i))
            late_pieces.append(lambda i=i: fam_piece("grnn", i))
            late_pieces.append(lambda i=i: fam_piece("wro", i))
            late_pieces.append(lambda i=i: fam_piece("wco", i))
        for i in range(8):
            late_pieces.append(lambda i=i: fam_piece("wo", i))

        def emit_late(n):
            for _ in range(n):
                if late_pieces:
                    late_pieces.pop(0)()

        def build_diag(c):
            s_, slot = dgring.next()
            for k in range(CW):
                dve(lambda e, slot=slot, k=k, c=c: e.tensor_scalar(out=slot[:, k, :], in0=identf[:],
                                                                   scalar1=col(par, P_CW + c * CW + k), scalar2=0.5,
                                                                   op0=ALU.mult, op1=ALU.mult),
                    reads=["identf", "par"], writes=[("dgslot", s_)], n=128, k=1)
            dma("sp", dg_b[c * 128:(c + 1) * 128, :], slot[:].rearrange("p k n -> p (k n)"), [("dgslot", s_)], [("dgb", c)], ("dgout", s_))

        slot_gen = {}

        def chk(res):
            assert slot_gen[res[0:2]] == res[2], "stale weight tile %r" % (res,)
            return res[0:2]

        def load_tile(name, j):
            scr, src, nk, tiles = FAM[name]
            c0, w = tiles[j]
            s, slot = wring.next()
            slot_gen[("wslot", s)] = slot_gen.get(("wslot", s), 0) + 1
            view = slot[:, 0:nk * w].rearrange("p (k n) -> p k n", n=w)
            members = [("wb", name, i) for i in range(len(tiles)) if cast_grp(name, i) == cast_grp(name, j)]
            dma("sp", slot[:, 0:nk * w], scr[j * 128:(j + 1) * 128, 0:nk * w], members, [("wslot", s)], ("wslot", s))
            return ("wslot", s, slot_gen[("wslot", s)]), view

        def load_win(col0, wbname, ncols=256):
            tiles = FAM[wbname][3]
            j = [i for i, (c0, w) in enumerate(tiles) if c0 == col0]
            assert len(j) == 1 and tiles[j[0]][1] == ncols, (wbname, col0, ncols)
            return load_tile(wbname, j[0])

        def load_gate(g):
            s, slot = wring.next()
            slot_gen[("wslot", s)] = slot_gen.get(("wslot", s), 0) + 1
            view = slot[:, 0:NBLK * 128].rearrange("p (k n) -> p k n", n=128)
            dma("sp", slot[:, 0:NBLK * 128], gw_b[g * 128:(g + 1) * 128, :], [("wb", "gw", g)], [("wslot", s)], ("wslot", s))
            return ("wslot", s, slot_gen[("wslot", s)]), view

        def load_dg(c):
            s, slot = dgring.next()
            dma("sp", slot[:].rearrange("p k n -> p (k n)"), dg_b[c * 128:(c + 1) * 128, :], [("dgb", c)], [("dgslot", s)], ("dgslot", s))
            return ("dgslot", s), slot

        def mm_group(bank_res, out_ap, pairs, extra_reads):
            n = len(pairs)
            pairs = [(l, r, [chk(x) if (isinstance(x, tuple) and x[0] == "wslot") else x for x in rd]) for (l, r, rd) in pairs]
            for i, (l, r, rd) in enumerate(pairs):
                ncol = nfree(r)
                P.op("pe", lambda e, l=l, r=r, i=i: e.matmul(out_ap, lhsT=l, rhs=r, start=(i == 0), stop=(i == n - 1)),
                     reads=list(rd) + list(extra_reads), writes=[bank_res], cost=max(64, ncol) / 1900.0 * (4.0 if r.dtype == F32 else 1.0))

        def prep(src, tile_idx, buf):
            r0 = tile_idx * T
            blocks = []
            for tb in range(2):
                xi, xb = xring.next()
                dma("pool", xb[:], src[r0 + tb * 128:r0 + (tb + 1) * 128, :], [], [("x", xi)], ("x", xi))
                hi, hsb = hs.next()
                act(junk[:], xb[:], AF.Square, reads=[("x", xi)], writes=["junk", ("ssq", tb)], accum_out=col(ssq, tb))
                blocks.append((xi, xb, hi, hsb))
            dve(lambda e: e.tensor_scalar(out=rs[:], in0=ssq[:], scalar1=1.0 / D, scalar2=EPS, op0=ALU.mult, op1=ALU.add),
                reads=[("ssq", 0), ("ssq", 1)], writes=["rs"], n=2, k=1)
            act(rs[:], rs[:], AF.Sqrt, reads=["rs"], writes=["rs"])
            dve(lambda e: e.reciprocal(out=rs[:], in_=rs[:]), reads=["rs"], writes=["rs"], n=2, k=8)
            pe_parts = []
            for tb, (xi, xb, hi, hsb) in enumerate(blocks):
                act(hsb[:], xb[:], AF.Copy, reads=[("x", xi), "rs"], writes=[("hs", hi)], scale=col(rs, tb))
                pe_parts.append((tb, hi, hsb))
            return pe_parts

        def prep_pe(pe_parts, buf, all_act=False):
            for (tb, hi, hsb) in pe_parts:
                for q in range(4):
                    bres, bank = newbank()
                    bb = bank[:].bitcast(BF16)
                    for i in range(4):
                        fc = q * 4 + i
                        P.op("pe", lambda e, bb=bb, i=i, fc=fc, hsb=hsb: e.transpose(out=bb[:, i * 128:(i + 1) * 128],
                                                                                     in_=hsb[:, fc * 128:(fc + 1) * 128],
                                                                                     identity=ident[:]),
                             reads=[("hs", hi), "ident"], writes=[bres], cost=0.08)
                    for i in range(4):
                        fc = q * 4 + i
                        o_ap = hT[buf][:, fc, tb * 128:(tb + 1) * 128]
                        i_ap = bb[:, i * 128:(i + 1) * 128]
                        if q % 2 == 0 and not all_act:
                            dve(lambda e, o_ap=o_ap, i_ap=i_ap, fc=fc: e.tensor_scalar(out=o_ap, in0=i_ap, scalar1=col(par, P_G + fc),
                                                                                       scalar2=None, op0=ALU.mult),
                                reads=["par"], writes=[("hT", buf, fc)], xreads=[bres], n=128, k=1)
                        else:
                            act(o_ap, i_ap, AF.Copy, reads=["par"], writes=[("hT", buf, fc)], xreads=[bres], scale=col(par, P_G + fc))

        def hT_reads(buf):
            return [("hT", buf, fc) for fc in range(KC)]

        def rnn_stage(buf, main, mid_hook=None, late_n=0, fillers=None, nfill=2):
            hr = hT_reads(buf)
            wt = {}

            pend_cast = []

            def flush_cast():
                while pend_cast:
                    vi = pend_cast.pop(0)
                    pool(lambda e, vi=vi: e.tensor_copy(out=vbb[vi][:], in_=vbuf[vi][:]), reads=[("v", vi)], writes=[("vb", vi)])

            def rx_chunk(c):
                lc = c % 5
                if lc % 2 == 0:
                    wt["rx"] = load_win(O_RX + c * 128, "rx", 128 if lc == 4 else 256)
                wres, wv = wt["rx"]
                bres, bank = newbank()
                o = (lc % 2) * 128
                mm_group(bres, bank[:, 0:T], [(wv[:, k, o:o + 128], hT[buf][:, k, :], [wres, ("hT", buf, k)]) for k in range(KC)], [])
                ri, rx = rxr.next()
                pace["n"] = pace.get("n", 0) + 1
                act(rx[:, 3:3 + T], bank[:, 0:T], AF.Identity, xreads=[bres], writes=[("rx", ri), ("pace", pace["n"])])
                dve(lambda e, rx=rx, c=c: e.tensor_copy(out=rx[:, 0:3], in_=rxh[:, c, :]), reads=[("rxh", c)], writes=[("rxhalo", ri)], n=3, k=1)
                flush_cast()
                vi = c % 10
                v = vbuf[vi]
                dve(lambda e, rx=rx, c=c: e.tensor_scalar(out=pvb[:, 0:T], in0=rx[:, 3:3 + T], scalar1=col(par, P_RW + c * RW + 3),
                                                          scalar2=col(par, P_RB + c), op0=ALU.mult, op1=ALU.add),
                    reads=[("rx", ri), "par"], writes=["pv"], k=1)
                for k in range(3):
                    last = (k == 2)
                    dve(lambda e, rx=rx, v=v, c=c, k=k, last=last: e.scalar_tensor_tensor(out=(v[:] if last else pvb[:, 0:T]), in0=rx[:, k:k + T],
                                                                                          scalar=col(par, P_RW + c * RW + k),
                                                                                          in1=pvb[:, 0:T], op0=ALU.mult, op1=ALU.add),
                        reads=[("rx", ri), ("rxhalo", ri), "par", "pv"], writes=([("v", vi)] if last else ["pv"]), k=1)
                dve(lambda e, rx=rx, c=c: e.tensor_copy(out=rxh[:, c, :], in_=rx[:, T:T + 3]), reads=[("rx", ri)], writes=[("rxh", c)], n=3, k=1)
                pend_cast.append(vi)

            def gates_block(g):
                gres, gv = load_gate(g)
                items = []
                for oc in range(5):
                    c = g * 5 + oc
                    rec = {"c": c}
                    for gate in range(2):
                        bres, bank = newbank()
                        pairs = []
                        for ic in GATE_ICS[oc]:
                            vi = (g * 5 + ic) % 10
                            pairs.append((gv[:, GATE_BLK[(gate, oc, ic)], :], vbb[vi][:], [gres, ("vb", vi)]))
                        mm_group(bres, bank[:, 0:T], pairs, [])
                        rec[gate] = (bres, bank)
                    ai, A = bufA.next()
                    bi, B = bufB.next()
                    ci, C = bufC.next()
                    rec.update(A=A, ai=ai, B=B, bi=bi, C=C, ci=ci)
                    bres, bank = rec[0]
                    act(A[:], bank[:, 0:T], AF.Tanh, xreads=[bres], reads=DER, writes=[("A", ai)], scale=0.5, bias=col(der, Q_HBA + c))
                    bres, bank = rec[1]
                    act(C[:], bank[:, 0:T], AF.Tanh, xreads=[bres], reads=DER, writes=[("C", ci)], scale=0.5, bias=col(der, Q_HBX + c))
                    act(B[:], A[:], AF.Exp, reads=[("A", ai)] + DER, writes=[("B", bi)], scale=col(der, Q_HC + c), bias=col(der, Q_HC + c))
                    pool(lambda e, A=A, B=B: e.tensor_tensor(out=A[:], in0=B[:], in1=B[:], op=ALU.mult), reads=[("B", bi)], writes=[("A", ai)])
                    vi_ = c % 10
                    if main:
                        dve(lambda e, C=C, vv_=vbuf[vi_]: e.scalar_tensor_tensor(out=C[:], in0=C[:], scalar=1.0, in1=vv_[:], op0=ALU.add, op1=ALU.mult),
                            reads=[("C", ci), ("v", vi_)], writes=[("C", ci)])
                    else:
                        pool(lambda e, C=C: e.tensor_scalar(out=C[:], in0=C[:], scalar1=1.0, scalar2=1.0, op0=ALU.add, op1=ALU.mult),
                             reads=[("C", ci)], writes=[("C", ci)])
                        pool(lambda e, C=C, vv_=vbuf[vi_]: e.tensor_tensor(out=C[:], in0=C[:], in1=vv_[:], op=ALU.mult),
                             reads=[("C", ci), ("v", vi_)], writes=[("C", ci)])
                    items.append(rec)
                if main:
                    for oc in range(5):
                        rec = items[oc]
                        c = rec["c"]
                        lc = c % 5
                        if lc % 2 == 0:
                            wt["rg"] = load_win(O_RGATE + c * 128, "rgate", 128 if lc == 4 else 256)
                        wres, wv = wt["rg"]
                        bres, bank = newbank()
                        o = (lc % 2) * 128
                        mm_group(bres, bank[:, 0:T], [(wv[:, k, o:o + 128], hT[buf][:, k, :], [wres, ("hT", buf, k)]) for k in range(KC)], [])
                        ei, E = bufE.next()
                        act(E[:], bank[:, 0:T], AF.Tanh, xreads=[bres], writes=[("E", ei)], scale=0.5)
                        dve(lambda e, E=E, bank=bank: e.scalar_tensor_tensor(out=E[:], in0=E[:], scalar=1.0, in1=bank[:, 0:T],
                                                                             op0=ALU.add, op1=ALU.mult),
                            reads=[("E", ei)], writes=[("E", ei)], xreads=[bres])
                        rec.update(E=E, ei=ei)
                for rec in items:
                    A, ai = rec["A"], rec["ai"]
                    act(A[:], A[:], AF.Sqrt, reads=[("A", ai)], writes=[("A", ai)], scale=-1.0, bias=1.0)
                for rec in items:
                    c = rec["c"]
                    A, ai, B, bi, C, ci = rec["A"], rec["ai"], rec["B"], rec["bi"], rec["C"], rec["ci"]
                    vi = c % 10
                    v = vbuf[vi]
                    dve(lambda e, C=C, A=A: e.scalar_tensor_tensor(out=C[:], in0=A[:], scalar=0.0, in1=C[:], op0=ALU.max, op1=ALU.mult),
                        reads=[("C", ci), ("A", ai)], writes=[("C", ci)])
                    di, Dd = bufD.next()
                    dve(lambda e, Dd=Dd, B=B, C=C, c=c: e.tensor_tensor_scan(out=Dd[:], data0=B[:], data1=C[:], initial=col(state, c),
                                                                             op0=ALU.mult, op1=ALU.add),
                        reads=[("B", bi), ("C", ci), ("state", c)], writes=[("D", di)])
                    dve(lambda e, Dd=Dd, c=c: e.tensor_copy(out=col(state, c), in_=Dd[:, T - 1:T]), reads=[("D", di)], writes=[("state", c)], n=1, k=1)
                    if main:
                        E, ei = rec["E"], rec["ei"]
                        dve(lambda e, Dd=Dd, E=E, c=c: e.tensor_tensor(out=hg[:, c, :], in0=Dd[:], in1=E[:], op=ALU.mult),
                            reads=[("D", di), ("E", ei)], writes=[("hg", c)])

            for g in range(4):
                for c in range(g * 5, g * 5 + 5):
                    rx_chunk(c)
                flush_cast()
                if late_n:
                    pace["res"] = ("pace", pace["n"])
                    emit_late(late_n)
                if g == 2 and mid_hook is not None:
                    mid_hook()
                if fillers:
                    for _ in range(nfill):
                        if fillers:
                            fillers.pop(0)()
                if g >= 1:
                    gates_block(g - 1)
            gates_block(3)

        def conv_halo(buf):
            hr = hT_reads(buf)
            wt = {}
            for c in range(NCC):
                if c % 2 == 0:
                    wt["g"] = load_win(O_CGLU + c * 128, "cglu")
                    wt["v"] = load_win(O_CVAL + c * 128, "cval")
                o = (c % 2) * 128
                gres, gv = wt["g"]
                vres, vv = wt["v"]
                b1, bk1 = newbank()
                mm_group(b1, bk1[:, 0:32], [(gv[:, k, o:o + 128], hT[buf][:, k, T - 32:T], [gres, ("hT", buf, k)]) for k in range(KC)], [])
                b2, bk2 = newbank()
                mm_group(b2, bk2[:, 0:32], [(vv[:, k, o:o + 128], hT[buf][:, k, T - 32:T], [vres, ("hT", buf, k)]) for k in range(KC)], [])
                ti, tt = tbuf.next()
                act(tt[:, 0:32], bk1[:, 0:32], AF.Tanh, xreads=[b1], writes=[("t", ti)], scale=0.5)
                dve(lambda e, tt=tt, bk2=bk2, c=c: e.scalar_tensor_tensor(out=uh[:, c, :], in0=tt[:, 2:32], scalar=1.0, in1=bk2[:, 2:32],
                                                                          op0=ALU.add, op1=ALU.mult),
                    reads=[("t", ti)], writes=[("uh", c)], xreads=[b2])

        def mask_state():
            dve(lambda e: e.tensor_scalar(out=state[:], in0=state[:], scalar1=flag[:, 0:1], scalar2=None, op0=ALU.mult),
                reads=["flag"] + [("state", c) for c in range(NRC)], writes=[("state", c) for c in range(NRC)])
            dve(lambda e: e.tensor_scalar(out=rxh[:].rearrange("p c k -> p (c k)"), in0=rxh[:].rearrange("p c k -> p (c k)"),
                                          scalar1=flag[:, 0:1], scalar2=None, op0=ALU.mult),
                reads=["flag"] + [("rxh", c) for c in range(NRC)], writes=[("rxh", c) for c in range(NRC)])
            dve(lambda e: e.tensor_scalar(out=uh[:].rearrange("p c k -> p (c k)"), in0=uh[:].rearrange("p c k -> p (c k)"),
                                          scalar1=flag[:, 0:1], scalar2=None, op0=ALU.mult),
                reads=["flag"] + [("uh", c) for c in range(NCC)], writes=[("uh", c) for c in range(NCC)])

        def conv_stage(buf):
            hr = hT_reads(buf)
            wt = {}
            pend = []

            def gv_chunk(c):
                if c % 2 == 0:
                    wt["g"] = load_win(O_CGLU + c * 128, "cglu")
                    wt["v"] = load_win(O_CVAL + c * 128, "cval")
                o = (c % 2) * 128
                gres, gv_ = wt["g"]
                vres, vv = wt["v"]
                b1, bk1 = newbank()
                mm_group(b1, bk1[:, 0:T], [(gv_[:, k, o:o + 128], hT[buf][:, k, :], [gres, ("hT", buf, k)]) for k in range(KC)], [])
                b2, bk2 = newbank()
                mm_group(b2, bk2[:, 0:T], [(vv[:, k, o:o + 128], hT[buf][:, k, :], [vres, ("hT", buf, k)]) for k in range(KC)], [])
                ti, tt = tbuf.next()
                act(tt[:], bk1[:, 0:T], AF.Tanh, xreads=[b1], writes=[("t", ti)], scale=0.5)
                ui, ub = ubuf.next()
                dve(lambda e, tt=tt, bk2=bk2, ub=ub: e.scalar_tensor_tensor(out=ub[:, 30:30 + T], in0=tt[:], scalar=1.0, in1=bk2[:, 0:T],
                                                                            op0=ALU.add, op1=ALU.mult),
                    reads=[("t", ti)], writes=[("u", ui)], xreads=[b2])
                pool(lambda e, ub=ub, c=c: e.tensor_copy(out=ub[:, 0:30], in_=uh[:, c, :]), reads=[("uh", c)], writes=[("uhalo", ui)])
                pool(lambda e, ub=ub, c=c: e.tensor_copy(out=uh[:, c, :], in_=ub[:, T:T + 30]), reads=[("u", ui)], writes=[("uh", c)])
                return ui, ub

            def conv_chunk(c, ui, ub):
                dres, dgv = load_dg(c)
                b3, bk3 = newbank()
                mm_group(b3, bk3[:, 0:T], [(dgv[:, k, :], ub[:, k:k + T], [dres]) for k in range(CW)], [("u", ui), ("uhalo", ui)])
                act(cv[:, c, :], bk3[:, 0:T], AF.Identity, xreads=[b3], reads=["par"], writes=[("cv", c), ("mg", 2 * c), ("mg", 2 * c + 1)],
                    bias=col(par, P_CB + c))
                si, sq = sqt.next()
                act(sq[:], cv[:, c, :], AF.Square, reads=[("cv", c)], writes=[("sq", si)])
                if c == 0:
                    pool(lambda e: e.tensor_copy(out=acc1[:], in_=cv[:, 0, :]), reads=[("cv", 0)], writes=["acc1"])
                    pool(lambda e, sq=sq: e.tensor_copy(out=acc2[:], in_=sq[:]), reads=[("sq", si)], writes=["acc2"])
                else:
                    pool(lambda e, c=c: e.tensor_tensor(out=acc1[:], in0=acc1[:], in1=cv[:, c, :], op=ALU.add), reads=[("cv", c), "acc1"], writes=["acc1"])
                    pool(lambda e, sq=sq: e.tensor_tensor(out=acc2[:], in0=acc2[:], in1=sq[:], op=ALU.add), reads=[("sq", si), "acc2"], writes=["acc2"])

            def step_pair(c0):
                for c in (c0, c0 + 1):
                    ui, ub = gv_chunk(c)
                    pend.append((c, ui, ub))
                    if len(pend) > 1:
                        conv_chunk(*pend.pop(0))

            def rest():
                conv_chunk(*pend.pop(0))
                conv_rest(buf)

            return step_pair, rest

        def conv_rest(buf):
            wt = {}
            b1, bk1 = newbank()
            P.op("pe", lambda e: e.matmul(bk1[:, 0:T], lhsT=onesf[:], rhs=acc1[:], start=True, stop=True), reads=["onesf", "acc1"], writes=[b1], cost=0.6)
            b2, bk2 = newbank()
            P.op("pe", lambda e: e.matmul(bk2[:, 0:T], lhsT=onesf[:], rhs=acc2[:], start=True, stop=True), reads=["onesf", "acc2"], writes=[b2], cost=0.6)
            dve(lambda e: e.tensor_scalar(out=lnM[:], in0=bk1[:, 0:T], scalar1=1.0 / DC, scalar2=None, op0=ALU.mult), xreads=[b1], writes=["lnM"])
            dve(lambda e: e.tensor_tensor(out=lnN[:], in0=lnM[:], in1=lnM[:], op=ALU.mult), reads=["lnM"], writes=["lnN"])
            dve(lambda e: e.scalar_tensor_tensor(out=lnR[:], in0=bk2[:, 0:T], scalar=1.0 / DC, in1=lnN[:], op0=ALU.mult, op1=ALU.subtract),
                reads=["lnN"], writes=["lnR"], xreads=[b2])
            dve(lambda e: e.tensor_scalar(out=lnR[:], in0=lnR[:], scalar1=0.0, scalar2=LN_EPS, op0=ALU.max, op1=ALU.add), reads=["lnR"], writes=["lnR"])
            act(lnR[:], lnR[:], AF.Sqrt, reads=["lnR"], writes=["lnR"])
            dve(lambda e: e.reciprocal(out=lnR[:], in_=lnR[:]), reads=["lnR"], writes=["lnR"], k=8)
            dve(lambda e: e.scalar_tensor_tensor(out=lnN[:], in0=lnM[:], scalar=-1.0, in1=lnR[:], op0=ALU.mult, op1=ALU.mult),
                reads=["lnM", "lnR"], writes=["lnN"])
            for c in range(NCC):
                if c % 2 == 0:
                    wt["cg"] = load_win(O_CGATE + c * 128, "cgate")
                o = (c % 2) * 128
                wres, wv = wt["cg"]
                b3, bk3 = newbank()
                mm_group(b3, bk3[:, 0:T], [(wv[:, k, o:o + 128], hT[buf][:, k, :], [wres, ("hT", buf, k)]) for k in range(KC)], [])
                t2i, t2 = t2buf.next()
                act(t2[:], bk3[:, 0:T], AF.Tanh, xreads=[b3], writes=[("t2", t2i)], scale=0.5)
                dve(lambda e, t2=t2, bk3=bk3: e.scalar_tensor_tensor(out=t2[:], in0=t2[:], scalar=1.0, in1=bk3[:, 0:T], op0=ALU.add, op1=ALU.mult),
                    reads=[("t2", t2i)], writes=[("t2", t2i)], xreads=[b3])
                zi, z = zbuf.next()
                dve(lambda e, z=z, c=c: e.tensor_tensor(out=z[:], in0=cv[:, c, :], in1=lnR[:], op=ALU.mult), reads=[("cv", c), "lnR"], writes=[("z", zi)])
                dve(lambda e, z=z: e.tensor_tensor(out=z[:], in0=z[:], in1=lnN[:], op=ALU.add), reads=[("z", zi), "lnN"], writes=[("z", zi)])
                dve(lambda e, z=z, c=c: e.tensor_scalar(out=z[:], in0=z[:], scalar1=col(par, P_LG + c), scalar2=col(par, P_LB + c),
                                                        op0=ALU.mult, op1=ALU.add), reads=[("z", zi), "par"], writes=[("z", zi)])
                t1i, t1 = t1buf.next()
                act(t1[:], z[:], AF.Tanh, reads=[("z", zi)], writes=[("t1", t1i)], scale=0.5)
                dve(lambda e, z=z, t1=t1: e.scalar_tensor_tensor(out=z[:], in0=t1[:], scalar=1.0, in1=z[:], op0=ALU.add, op1=ALU.mult),
                    reads=[("t1", t1i), ("z", zi)], writes=[("z", zi)])
                dve(lambda e, z=z, t2=t2, c=c: e.tensor_tensor(out=ug[:, c, :], in0=z[:], in1=t2[:], op=ALU.mult),
                    reads=[("z", zi), ("t2", t2i)], writes=[("ug", c)])

        def merge_stage(buf):
            hr = hT_reads(buf)
            for mp in range(KC // 2):
                ms = (2 * mp, 2 * mp + 1)
                recs = {m: {} for m in ms}
                for key, col0, wbn in (("gc", O_GCONV, "gconv"), ("gr", O_GRNN, "grnn")):
                    wres, wv = load_win(col0 + ms[0] * 128, wbn)
                    for j, m in enumerate(ms):
                        b1, bk1 = newbank()
                        mm_group(b1, bk1[:, 0:T], [(wv[:, k, j * 128:(j + 1) * 128], hT[buf][:, k, :], [wres, ("hT", buf, k)]) for k in range(KC)], [])
                        ring = mt1 if key == "gc" else mt2
                        i1, m1 = ring.next()
                        act(m1[:], bk1[:, 0:T], AF.Tanh, xreads=[b1], writes=[(key, i1)], scale=0.5)
                        recs[m][key] = (i1, m1)
                for key, gkey, src, nk, wbn, rname in (("yr", "gr", hg, NRC, "wro", "hg"), ("yc", "gc", ug, NCC, "wco", "ug")):
                    wres, wv = load_tile(wbn, mp)
                    for j, m in enumerate(ms):
                        b3, bk3 = newbank()
                        mm_group(b3, bk3[:, 0:T], [(wv[:, k, j * 128:(j + 1) * 128], src[:, k, :], [wres, (rname, k)]) for k in range(nk)], [])
                        i1, m1 = recs[m][gkey]
                        dve(lambda e, m1=m1, bk3=bk3: e.scalar_tensor_tensor(out=m1[:], in0=m1[:], scalar=1.0, in1=bk3[:, 0:T], op0=ALU.add, op1=ALU.mult),
                            reads=[(gkey, i1)], writes=[(gkey, i1)], xreads=[b3])
                for m in ms:
                    i1, m1 = recs[m]["gc"]
                    i2, m2 = recs[m]["gr"]
                    dve(lambda e, m1=m1, m2=m2, m=m: e.tensor_tensor(out=mg[:, m, :], in0=m1[:], in1=m2[:], op=ALU.add),
                        reads=[("gc", i1), ("gr", i2)], writes=[("mg", m), ("cv", m // 2)])

        def out_stage(tile_idx):
            r0 = tile_idx * T
            xs = []
            for tb in range(2):
                xi, xb = xring.next()
                dma("pool", xb[:], xm[r0 + tb * 128:r0 + (tb + 1) * 128, :], [], [("x", xi)], ("x", xi))
                xs.append((xi, xb))
            for cg in range(8):
                wres, wv = load_tile("wo", cg)
                for tb in range(2):
                    xi, xb = xs[tb]
                    b1, bk1 = newbank()
                    mm_group(b1, bk1[:, 0:256], [(mg[:, k, tb * 128:(tb + 1) * 128], wv[:, k, :], [wres, ("mg", k)]) for k in range(KC)], [])
                    dve(lambda e, xb=xb, bk1=bk1, cg=cg: e.scalar_tensor_tensor(out=xb[:, cg * 256:(cg + 1) * 256], in0=bk1[:, 0:256], scalar=0.125,
                                                                               in1=xb[:, cg * 256:(cg + 1) * 256], op0=ALU.mult, op1=ALU.add),
                        reads=[("x", xi)], writes=[("x", xi)], xreads=[b1])
            for tb in range(2):
                xi, xb = xs[tb]
                act(junk[:], xb[:], AF.Square, reads=[("x", xi)], writes=["junk", ("ssq2", tb)], accum_out=col(ssq2, tb))
            dve(lambda e: e.tensor_scalar(out=rs2[:], in0=ssq2[:], scalar1=1.0 / D, scalar2=EPS, op0=ALU.mult, op1=ALU.add),
                reads=[("ssq2", 0), ("ssq2", 1)], writes=["rs2"], n=2, k=1)
            act(rs2[:], rs2[:], AF.Sqrt, reads=["rs2"], writes=["rs2"])
            dve(lambda e: e.reciprocal(out=rs2[:], in_=rs2[:]), reads=["rs2"], writes=["rs2"], n=2, k=8)
            outs = []
            for tb in range(2):
                xi, xb = xs[tb]
                dve(lambda e, xb=xb, tb=tb: e.scalar_tensor_tensor(out=xb[:], in0=xb[:], scalar=col(rs2, tb), in1=fg[:], op0=ALU.mult, op1=ALU.mult),
                    reads=[("x", xi), "rs2", "fg"], writes=[("x", xi)], n=D, k=2)
                outs.append(dma("pool", y[r0 + tb * 128:r0 + (tb + 1) * 128, :], xb[:], [("x", xi)], [], ("xo", xi)))
            return outs

        final = []
        diag_plan = {}
        if ntl == 1:
            diag_plan[0] = list(range(NCC))
        else:
            per = -(-NCC // (ntl - 1))
            per = max(per, 2)
            cc_ = 0
            for t_ in range(0, ntl - 1):
                diag_plan[t_] = list(range(cc_, min(NCC, cc_ + per)))
                cc_ += per
        for c in range(NCC):
            build_diag(c)
        n_groups = 4 * ntl
        late_per_group = -(-len(late_pieces) // max(1, n_groups - 6))
        pp = prep(xp, 0, 0)
        early_casts()
        prep_pe(pp, 0)
        cur = 0
        for t in range(ntl):
            nxt = 1 - cur
            if t + 1 < ntl:
                pp = prep(xp, t + 1, nxt)
            else:
                pp = prep(xm, 0, nxt)
            rnn_stage(cur, main=False, mid_hook=lambda pp=pp, nxt=nxt: prep_pe(pp, nxt, all_act=True), late_n=late_per_group)
            if t == ntl - 1:
                conv_halo(cur)
            cur = nxt
        emit_late(len(late_pieces))
        P.cutoff = len(P.ops)
        mask_state()
        for t in range(ntl):
            nxt = 1 - cur
            step_pair, conv_finish = conv_stage(cur)
            fillers = [lambda c0=c0: step_pair(c0) for c0 in range(0, NCC, 2)]
            rnn_stage(cur, main=True, fillers=fillers, nfill=FILL_N)
            if t + 1 < ntl:
                pp = prep(xm, t + 1, nxt)
            while fillers:
                fillers.pop(0)()
            conv_finish()
            merge_stage(cur)
            if t + 1 < ntl:
                prep_pe(pp, nxt)
            final += out_stage(t)
            cur = nxt
        P.final_dma = final
        P.build(st)
    return nc


def _pack_params(norm_g, conv_dw_w, conv_dw_b, conv_ln_g, conv_ln_b, rnn_conv_w, rnn_conv_b, b_rg_a, b_rg_x, rg_lambda):
    par = np.zeros((128, NPAR), np.float32)
    par[:, P_G:P_G + 16] = norm_g.reshape(16, 128).T
    par[:, P_CW:P_CW + 496] = conv_dw_w.reshape(CW, NCC, 128).transpose(2, 1, 0).reshape(128, NCC * CW)
    par[:, P_CB:P_CB + 16] = conv_dw_b.reshape(16, 128).T
    par[:, P_LG:P_LG + 16] = conv_ln_g.reshape(16, 128).T
    par[:, P_LB:P_LB + 16] = conv_ln_b.reshape(16, 128).T
    par[:, P_RW:P_RW + 80] = rnn_conv_w.reshape(RW, NRC, 128).transpose(2, 1, 0).reshape(128, NRC * RW)
    par[:, P_RB:P_RB + 20] = rnn_conv_b.reshape(20, 128).T
    par[:, P_BA:P_BA + 20] = b_rg_a.reshape(20, 128).T
    par[:, P_BX:P_BX + 20] = b_rg_x.reshape(20, 128).T
    par[:, P_LAM:P_LAM + 20] = rg_lambda.reshape(20, 128).T
    return par


def _pack_gates(w_a, w_x):
    out = np.zeros((4, 128, NBLK, 128), np.float32)
    for gate, w in enumerate((w_a, w_x)):
        for g in range(4):
            dense = np.zeros((640, 640), np.float32)
            for hh in range(4):
                dense[hh * 160:(hh + 1) * 160, hh * 160:(hh + 1) * 160] = w[g * 4 + hh]
            for oc in range(5):
                for ic in GATE_ICS[oc]:
                    out[g, :, GATE_BLK[(gate, oc, ic)], :] = dense[ic * 128:(ic + 1) * 128, oc * 128:(oc + 1) * 128]
    return out.reshape(4 * 128, NBLK * 128)


_NC_CACHE = {}


def kernel(x, norm_g, w_in, conv_dw_w, conv_dw_b, conv_ln_g, conv_ln_b, w_conv_out,
           rnn_conv_w, rnn_conv_b, w_rg_a, b_rg_a, w_rg_x, b_rg_x, rg_lambda,
           w_rnn_out, w_out, final_norm_g):
    f = lambda a: np.ascontiguousarray(np.asarray(a, dtype=np.float32))
    x = f(x)
    par = _pack_params(f(norm_g)[0], f(conv_dw_w)[0], f(conv_dw_b)[0], f(conv_ln_g)[0], f(conv_ln_b)[0],
                       f(rnn_conv_w)[0], f(rnn_conv_b)[0], f(b_rg_a)[0], f(b_rg_x)[0], f(rg_lambda)[0])
    gwp = _pack_gates(f(w_rg_a)[0], f(w_rg_x)[0])
    fgb = np.ascontiguousarray(np.broadcast_to(f(final_norm_g)[None, :], (128, D)))
    shared = {"params": par, "fg": fgb, "w_in": f(w_in)[0], "w_co": f(w_conv_out)[0], "w_ro": f(w_rnn_out)[0],
              "w_o": f(w_out)[0], "gw": gwp}
    in_maps = []
    for c in range(8):
        b, half = c // 2, c % 2
        m = dict(shared)
        m["xm"] = np.ascontiguousarray(x[b, half * TOK:(half + 1) * TOK, :])
        m["xp"] = np.ascontiguousarray(x[b, 0:TOK, :])
        m["flag"] = np.full((128, 1), float(half), np.float32)
        in_maps.append(m)
    if "nc" not in _NC_CACHE:
        _NC_CACHE["nc"] = build_program()
    nc = _NC_CACHE["nc"]
    res = run_bass_kernel_spmd(nc, in_maps, core_ids=list(range(8)))
    out = np.empty((4, 2 * TOK, D), np.float32)
    for c in range(8):
        b, half = c // 2, c % 2
        out[b, half * TOK:(half + 1) * TOK, :] = res.results[c]["y"]
    return out
```
